# Optimizing a Trainium2 kernel written in Bass

```python
import functools
import jax, jax.numpy as jnp
from jax import lax
import numpy as np

D_MODEL = 4096
BATCH = 2
SEQ = 8192
DEPTH = 1
DEC_BATCH = 32
DEC_SEQ = 16
PAST_LEN = 1024

CHUNK = 64
Q_BLOCK = 128
EPS = 1e-6
ML_HEADS = 4
ML_WIDTH = D_MODEL // 2
ML_DV = ML_WIDTH // ML_HEADS
ML_DQK = ML_DV // 2
ML_QK = ML_HEADS * ML_DQK
MLA_HEADS = 16
MLA_NOPE = 128
MLA_ROPE = 64
MLA_DV = 128
MLA_WIDTH = MLA_HEADS * MLA_DV
Q_LORA = 1024
KV_LORA = 512
ROPE_THETA = 10000.0
MLA_SCALE = (MLA_NOPE + MLA_ROPE) ** -0.5
D_FF = 4 * D_MODEL
N_MOD = 6
IN_SIZES = (ML_QK, ML_QK, ML_WIDTH, ML_WIDTH, ML_HEADS, ML_HEADS, Q_LORA, KV_LORA, MLA_ROPE)
D_IN = sum(IN_SIZES)

kernel_name = 'hymba_mlstm_mla_streaming_step'


def rmsnorm(x, g):
    xf = x.astype(jnp.float32)
    y = xf * lax.rsqrt(jnp.mean(xf * xf, axis=-1, keepdims=True) + EPS)
    return (y * g.astype(jnp.float32)).astype(x.dtype)


def rope(x, pos):
    half = x.shape[-1] // 2
    inv = ROPE_THETA ** (-jnp.arange(half, dtype=jnp.float32) / half)
    ang = pos.astype(jnp.float32)[:, None] * inv[None, :]
    ang = ang.reshape((ang.shape[0],) + (1,) * (x.ndim - 3) + (half,))
    cos, sin = jnp.cos(ang), jnp.sin(ang)
    xf = x.astype(jnp.float32)
    x1, x2 = xf[..., :half], xf[..., half:]
    return jnp.concatenate([x1 * cos - x2 * sin, x2 * cos + x1 * sin], axis=-1).astype(x.dtype)


def modulation(c, w_ada, b_ada):
    mod = jnp.einsum('bd,de->be', jax.nn.silu(c), w_ada) + b_ada
    return jnp.split(mod[:, None, :], N_MOD, axis=-1)


def mixer_inputs(u, pos, w_in, b_ig, b_fg, g_cq, w_uq, g_ckv):
    bsz, s = u.shape[0], u.shape[1]
    z = jnp.einsum('bsd,de->bse', u, w_in)
    points, acc = [], 0
    for n in IN_SIZES[:-1]:
        acc += n
        points.append(acc)
    q, k, v, o, ig, fg, cq, ckv, kr = jnp.split(z, points, axis=-1)
    q = q.reshape(bsz, s, ML_HEADS, ML_DQK)
    k = k.reshape(bsz, s, ML_HEADS, ML_DQK) * (ML_DQK ** -0.5)
    v = v.reshape(bsz, s, ML_HEADS, ML_DV)
    ig = ig.astype(jnp.float32) + b_ig.astype(jnp.float32)
    lf = jax.nn.log_sigmoid(fg.astype(jnp.float32) + b_fg.astype(jnp.float32))
    qm = jnp.einsum('bsc,che->bshe', rmsnorm(cq, g_cq), w_uq)
    q_nope = qm[..., :MLA_NOPE]
    q_rope = rope(qm[..., MLA_NOPE:], pos)
    c_kv = rmsnorm(ckv, g_ckv)
    k_rope = rope(kr, pos)
    return q, k, v, o, ig, lf, q_nope, q_rope, c_kv, k_rope


def mlstm_chunk(state, q, k, v, ig, lf):
    C, n, m = state
    L = q.shape[2]
    bcum = jnp.cumsum(lf, axis=-1)
    a = bcum + m[..., None].astype(jnp.float32)
    dmat = bcum[..., :, None] - bcum[..., None, :] + ig[..., None, :]
    dmat = jnp.where(jnp.tril(jnp.ones((L, L), dtype=bool)), dmat, -jnp.inf)
    m_t = jnp.maximum(a, jnp.max(dmat, axis=-1))
    w_inter = jnp.exp(a - m_t)
    w_intra = jnp.exp(dmat - m_t[..., None])
    sc = jnp.einsum('bhtd,bhsd->bhts', q, k) * w_intra
    num = jnp.einsum('bhts,bhsv->bhtv', sc, v) + w_inter[..., None] * jnp.einsum('bhtd,bhdv->bhtv', q, C)
    nq = jnp.sum(sc, axis=-1) + w_inter * jnp.einsum('bhtd,bhd->bht', q, n)
    h = num / jnp.maximum(jnp.abs(nq), jnp.exp(-m_t))[..., None]
    m_new = m_t[..., -1]
    g_state = jnp.exp(bcum[..., -1] + m.astype(jnp.float32) - m_new)
    g_tok = jnp.exp(bcum[..., -1:] - bcum + ig - m_new[..., None])
    kg = k * g_tok[..., None]
    C_new = g_state[..., None, None] * C + jnp.einsum('bhsd,bhsv->bhdv', kg, v)
    n_new = g_state[..., None] * n + jnp.sum(kg, axis=2)
    return (C_new, n_new, m_new), h


def mlstm_prompt(q, k, v, ig, lf):
    bsz, s = q.shape[0], q.shape[1]
    nc = s // CHUNK

    def to_chunks(t):
        t = t.reshape((bsz, nc, CHUNK) + t.shape[2:])
        return jnp.swapaxes(jnp.moveaxis(t, 1, 0), 2, 3)

    init = (jnp.zeros((bsz, ML_HEADS, ML_DQK, ML_DV), jnp.float32),
            jnp.zeros((bsz, ML_HEADS, ML_DQK), jnp.float32),
            jnp.zeros((bsz, ML_HEADS), jnp.float32))
    xs = (to_chunks(q), to_chunks(k), to_chunks(v), to_chunks(ig), to_chunks(lf))
    state, h = lax.scan(lambda st, xc: mlstm_chunk(st, *xc), init, xs)
    h = jnp.moveaxis(jnp.swapaxes(h, 2, 3), 0, 1).reshape(bsz, s, ML_HEADS, ML_DV)
    return h, state


def mlstm_sample(q, k, v, ig, lf, C0, n0, m0):
    tr = lambda t: jnp.swapaxes(t, 1, 2)
    state, h = mlstm_chunk((C0, n0, m0), tr(q), tr(k), tr(v), tr(ig), tr(lf))
    return jnp.swapaxes(h, 1, 2), state


def mla_attend(q_nope, q_rope, c_kv, k_rope, w_uk, w_uv, mask):
    q_lat = jnp.einsum('bthd,chd->bthc', q_nope, w_uk)
    sc = jnp.einsum('bthc,bsc->bhts', q_lat, c_kv) + jnp.einsum('bthr,bsr->bhts', q_rope, k_rope)
    sc = sc.astype(jnp.float32) * MLA_SCALE
    if mask is not None:
        sc = jnp.where(mask, sc, -jnp.inf)
    p = jax.nn.softmax(sc, axis=-1).astype(c_kv.dtype)
    lat = jnp.einsum('bhts,bsc->bthc', p, c_kv)
    return jnp.einsum('bthc,chd->bthd', lat, w_uv)


def mla_prompt(q_nope, q_rope, c_kv, k_rope, w_uk, w_uv):
    bsz, s = q_nope.shape[0], q_nope.shape[1]
    nb = s // Q_BLOCK
    blocks = lambda t: jnp.moveaxis(t.reshape((bsz, nb, Q_BLOCK) + t.shape[2:]), 1, 0)
    k_chunk = jnp.arange(s) // CHUNK

    def one_block(args):
        i, qn, qr = args
        q_chunk = (i * Q_BLOCK + jnp.arange(Q_BLOCK)) // CHUNK
        mask = k_chunk[None, :] <= q_chunk[:, None]
        return mla_attend(qn, qr, c_kv, k_rope, w_uk, w_uv, mask)

    o = lax.map(one_block, (jnp.arange(nb), blocks(q_nope), blocks(q_rope)))
    return jnp.moveaxis(o, 0, 1).reshape(bsz, s, MLA_HEADS, MLA_DV)


def mla_sample(q_nope, q_rope, c_kv, k_rope, w_uk, w_uv, ckv_past, krope_past):
    c_all = jnp.concatenate([ckv_past.astype(c_kv.dtype), c_kv], axis=1)
    k_all = jnp.concatenate([krope_past.astype(k_rope.dtype), k_rope], axis=1)
    return mla_attend(q_nope, q_rope, c_all, k_all, w_uk, w_uv, None)


def trunk_layer(x, c, pos, run_mlstm, run_mla, w_ada, b_ada, g_pre1, g_post1, w_in, b_ig, b_fg,
                g_mlnorm, g_cq, w_uq, g_ckv, w_uk, w_uv, w_out, g_pre2, g_post2, w_ff1, w_ff2):
    bsz, s = x.shape[0], x.shape[1]
    sh1, sc1, gt1, sh2, sc2, gt2 = modulation(c, w_ada, b_ada)
    u = rmsnorm(x, g_pre1) * (1.0 + sc1) + sh1
    q, k, v, o, ig, lf, q_nope, q_rope, c_kv, k_rope = mixer_inputs(u, pos, w_in, b_ig, b_fg, g_cq, w_uq, g_ckv)
    h_ml, ml_state = run_mlstm(q, k, v, ig, lf)
    h_ml = rmsnorm(h_ml.astype(u.dtype), g_mlnorm.reshape(ML_HEADS, ML_DV))
    h_ml = h_ml.reshape(bsz, s, ML_WIDTH) * jax.nn.sigmoid(o)
    o_mla = run_mla(q_nope, q_rope, c_kv, k_rope, w_uk, w_uv).reshape(bsz, s, MLA_WIDTH)
    y = jnp.einsum('bse,ed->bsd', jnp.concatenate([h_ml, o_mla.astype(h_ml.dtype)], axis=-1), w_out)
    x = x + gt1 * rmsnorm(y, g_post1)
    u = rmsnorm(x, g_pre2) * (1.0 + sc2) + sh2
    f = jnp.einsum('bsf,fd->bsd', jnp.square(jax.nn.relu(jnp.einsum('bsd,df->bsf', u, w_ff1))), w_ff2)
    x = x + gt2 * rmsnorm(f, g_post2)
    return x, ml_state, c_kv, k_rope


def setup_inputs(seed: int = 0) -> dict:
    key = jax.random.key(seed)
    ks = jax.random.split(key, 32)
    f32 = jnp.float32
    L = DEPTH

    def nrm(k, shape, scale):
        return jax.random.normal(k, shape, f32) * scale

    def gain(k, n):
        return 1.0 + 0.02 * jax.random.normal(k, (L, n), f32)

    return {
        'x_prompt': nrm(ks[0], (BATCH, SEQ, D_MODEL), 1.0),
        'x_sample': nrm(ks[1], (DEC_BATCH, DEC_SEQ, D_MODEL), 1.0),
        'cache_mla_ckv': nrm(ks[2], (L, DEC_BATCH, PAST_LEN, KV_LORA), 1.0),
        'cache_mla_krope': nrm(ks[3], (L, DEC_BATCH, PAST_LEN, MLA_ROPE), 1.0),
        'state_mlstm_C': nrm(ks[4], (L, DEC_BATCH, ML_HEADS, ML_DQK, ML_DV), 0.05),
        'state_mlstm_n': jnp.abs(nrm(ks[5], (L, DEC_BATCH, ML_HEADS, ML_DQK), 0.1)),
        'state_mlstm_m': nrm(ks[6], (L, DEC_BATCH, ML_HEADS), 1.0),
        'c_prompt': nrm(ks[7], (BATCH, D_MODEL), 1.0),
        'c_sample': nrm(ks[8], (DEC_BATCH, D_MODEL), 1.0),
        'w_ada': nrm(ks[9], (L, D_MODEL, N_MOD * D_MODEL), D_MODEL ** -0.5),
        'b_ada': nrm(ks[10], (L, N_MOD * D_MODEL), 0.02),
        'g_pre1': gain(ks[11], D_MODEL),
        'g_post1': gain(ks[12], D_MODEL),
        'w_in': nrm(ks[13], (L, D_MODEL, D_IN), D_MODEL ** -0.5),
        'b_ig': nrm(ks[14], (L, ML_HEADS), 0.1),
        'b_fg': jnp.linspace(3.0, 6.0, ML_HEADS, dtype=f32)[None, :] + nrm(ks[15], (L, ML_HEADS), 0.1),
        'g_mlnorm': gain(ks[16], ML_WIDTH),
        'g_cq': gain(ks[17], Q_LORA),
        'w_uq': nrm(ks[18], (L, Q_LORA, MLA_HEADS, MLA_NOPE + MLA_ROPE), Q_LORA ** -0.5),
        'g_ckv': gain(ks[19], KV_LORA),
        'w_uk': nrm(ks[20], (L, KV_LORA, MLA_HEADS, MLA_NOPE), KV_LORA ** -0.5),
        'w_uv': nrm(ks[21], (L, KV_LORA, MLA_HEADS, MLA_DV), KV_LORA ** -0.5),
        'w_out': nrm(ks[22], (L, D_MODEL, D_MODEL), D_MODEL ** -0.5),
        'g_pre2': gain(ks[23], D_MODEL),
        'g_post2': gain(ks[24], D_MODEL),
        'w_ff1': nrm(ks[25], (L, D_MODEL, D_FF), D_MODEL ** -0.5),
        'w_ff2': nrm(ks[26], (L, D_FF, D_MODEL), D_FF ** -0.5),
    }


def reference(x_prompt, x_sample, cache_mla_ckv, cache_mla_krope, state_mlstm_C, state_mlstm_n,
              state_mlstm_m, c_prompt, c_sample, w_ada, b_ada, g_pre1, g_post1, w_in, b_ig, b_fg,
              g_mlnorm, g_cq, w_uq, g_ckv, w_uk, w_uv, w_out, g_pre2, g_post2, w_ff1, w_ff2):
    xp, xs = x_prompt, x_sample
    pos_p = jnp.arange(xp.shape[1])
    pos_s = PAST_LEN + jnp.arange(xs.shape[1])
    p_ckv, p_kr, p_C, p_n, p_m = [], [], [], [], []
    s_ckv, s_kr, s_C, s_n, s_m = [], [], [], [], []
    for l in range(DEPTH):
        lw = (w_ada[l], b_ada[l], g_pre1[l], g_post1[l], w_in[l], b_ig[l], b_fg[l], g_mlnorm[l],
              g_cq[l], w_uq[l], g_ckv[l], w_uk[l], w_uv[l], w_out[l], g_pre2[l], g_post2[l],
              w_ff1[l], w_ff2[l])
        xp, (cp, np_, mp), ckv_p, kr_p = trunk_layer(xp, c_prompt, pos_p, mlstm_prompt, mla_prompt, *lw)
        run_ml_s = functools.partial(mlstm_sample, C0=state_mlstm_C[l], n0=state_mlstm_n[l], m0=state_mlstm_m[l])
        run_mla_s = functools.partial(mla_sample, ckv_past=cache_mla_ckv[l], krope_past=cache_mla_krope[l])
        xs, (cs, ns, ms), ckv_s, kr_s = trunk_layer(xs, c_sample, pos_s, run_ml_s, run_mla_s, *lw)
        p_ckv.append(ckv_p); p_kr.append(kr_p); p_C.append(cp); p_n.append(np_); p_m.append(mp)
        s_ckv.append(ckv_s); s_kr.append(kr_s); s_C.append(cs); s_n.append(ns); s_m.append(ms)
    return (xp, xs, jnp.stack(p_ckv), jnp.stack(p_kr), jnp.stack(p_C), jnp.stack(p_n), jnp.stack(p_m),
            jnp.stack(s_ckv), jnp.stack(s_kr), jnp.stack(s_C), jnp.stack(s_n), jnp.stack(s_m))
```

```python
import contextlib
import numpy as np
import concourse.bass as bass
import concourse.mybir as mybir
from concourse.bass_utils import run_bass_kernel_spmd

F32 = mybir.dt.float32
BF16 = mybir.dt.bfloat16
ALU = mybir.AluOpType
AF = mybir.ActivationFunctionType
AX = mybir.AxisListType
EPS = 1e-6
MLA_SCALE = 192.0 ** -0.5


class Res:
    __slots__ = ("name", "w", "r")

    def __init__(self, name=None):
        self.name = name
        self.w = None
        self.r = []


class Buf:
    __slots__ = ("t", "r")

    def __init__(self, t):
        self.t = t
        self.r = Res()


class View:
    __slots__ = ("t", "r")

    def __init__(self, ap, res):
        self.t = ap
        self.r = res


class Ring:
    def __init__(self, bufs):
        self.b = bufs
        self.i = 0

    def next(self):
        b = self.b[self.i % len(self.b)]
        self.i += 1
        return b


class _Eng:
    W = 24000

    def __init__(self, K, name, handle):
        self.K, self.name, self.h = K, name, handle
        self.count = 0
        self.sems = []
        self.seen = {}

    def sem_for(self, idx):
        w = (idx - 1) // self.W
        while len(self.sems) <= w:
            self.sems.append(self.K.new_sem(f"{self.name}{len(self.sems)}"))
        return self.sems[w], (idx - 1) % self.W + 1

    def key(self, idx):
        return (self.name, (idx - 1) // self.W)


class _Dma(_Eng):
    NS = 8

    def __init__(self, K, name, handle):
        super().__init__(K, name, handle)
        self.slots = [K.new_sem(f"{name}s{i}") for i in range(self.NS)]

    def sem_for(self, idx):
        i = idx - 1
        return self.slots[i % self.NS], 16 * (i // self.NS + 1)

    def key(self, idx):
        return (self.name, (idx - 1) % self.NS)


class Kern:
    def __init__(self, nc, stack):
        self.nc = nc
        self.gstack = stack
        self.stacks = [stack]
        self.e = {
            "pe": _Eng(self, "pe", nc.tensor),
            "act": _Eng(self, "act", nc.scalar),
            "dve": _Eng(self, "dve", nc.vector),
            "pool": _Eng(self, "pool", nc.gpsimd),
            "sq": _Dma(self, "sq", nc.sync),
            "gq": _Dma(self, "gq", nc.gpsimd),
        }
        self._alt = 0
        self.nsb = 0

    def new_sem(self, name):
        return self.gstack.enter_context(self.nc.semaphore(name))

    def sb(self, shape, dt, name=None):
        self.nsb += 1
        t = self.stacks[-1].enter_context(self.nc.sbuf_tensor(f"t{self.nsb}_{name or 'x'}", list(shape), dt))
        return Buf(t)

    def ps(self, shape, dt, name):
        return Buf(self.gstack.enter_context(self.nc.psum_tensor(name, list(shape), dt)))

    def alt(self):
        self._alt ^= 1
        return "act" if self._alt else "dve"

    @contextlib.contextmanager
    def scope(self):
        st = contextlib.ExitStack()
        self.stacks.append(st)
        try:
            yield
            self.barrier()
        finally:
            self.stacks.pop()
            st.close()

    def _wait(self, E, F, n):
        k = F.key(n)
        if E.seen.get(k, 0) >= n:
            return
        sem, val = F.sem_for(n)
        E.h.wait_ge(sem, val)
        E.seen[k] = n

    def op(self, en, fn, reads=(), writes=(), inc=True):
        E = self.e[en]
        idx = E.count + 1
        isd = isinstance(E, _Dma)
        for r in reads:
            if r.w is not None:
                F, n = r.w
                if not (F is E and n >= idx):
                    self._wait(E, F, n)
        for r in writes:
            if r.w is not None:
                F, n = r.w
                if not (F is E and n >= idx):
                    self._wait(E, F, n)
            for (F, n) in r.r:
                if not (F is E and n >= idx):
                    self._wait(E, F, n)
        if isd and idx > E.NS:
            self._wait(E, E, idx - E.NS)
        ins = fn()
        if inc or isd:
            E.count = idx
            sem, _ = E.sem_for(idx)
            ins.then_inc(sem, 16 if isd else 1)
        for r in writes:
            r.w = (E, idx)
            r.r = []
        for r in reads:
            r.r.append((E, idx))
        return ins

    def barrier(self, with_gq=False):
        for en in ("pe", "act", "dve", "sq"):
            E = self.e[en]
            for fn_ in ("pe", "act", "dve", "pool"):
                F = self.e[fn_]
                if F.count and F is not E:
                    self._wait(E, F, F.count)
            for qn in (("sq", "gq") if with_gq else ("sq",)):
                Q = self.e[qn]
                for i in range(max(0, Q.count - Q.NS), Q.count):
                    self._wait(E, Q, i + 1)

    def finish(self):
        self.barrier(with_gq=True)
        self.nc.kcounts = {k: e.count for k, e in self.e.items()}


C_Q, C_K, C_V, C_O, C_IG, C_FG, C_CQ, C_CKV, C_KR = 0, 1024, 2048, 4096, 6144, 6148, 6152, 7176, 7688


def build_program(SEQ=8192, PAST=1024, debug=False, stop_after=99, NHB=64):
    nc = bass.Bass("TRN2", target_bir_lowering=False)
    NBLK, NTA, OWN = SEQ // 128, SEQ // 512, SEQ // 4
    NOB, NTO, NQ = OWN // 128, OWN // 512, OWN + 64
    NKS = PAST + 16
    NPT = PAST // 512

    def din(name, shape, dt=F32):
        return nc.dram_tensor(name, list(shape), dt, kind="ExternalInput").ap()

    def dout(name, shape, dt=F32):
        return nc.dram_tensor(name, list(shape), dt, kind="ExternalOutput").ap()

    def dscr(name, shape, dt):
        return nc.dram_tensor(name, list(shape), dt, kind="ExternalOutput" if debug else "Internal").ap()

    x_all = din("x_all", [SEQ, 4096]); x_own = din("x_own", [OWN, 4096]); x_smp = din("x_smp", [64, 4096])
    c_T = din("c_T", [128, 32, 5])
    ckv_past = din("ckv_past", [4, PAST, 512]); kr_past = din("kr_past", [4, PAST, 64])
    C0_in = din("C0", [16, 256, 512]); n0_in = din("n0", [16, 256]); m0_in = din("m0T", [4, 4])
    w_ada = din("w_ada", [4096, 24576]); b_ada_fm = din("b_ada_fm", [128, 192])
    gpre1_d = din("g_pre1_fm", [128, 32]); gpost1_d = din("g_post1_fm", [128, 32])
    gpre2_d = din("g_pre2_fm", [128, 32]); gpost2_d = din("g_post2_fm", [128, 32])
    w_in = din("w_in", [4096, 7752]); b_ig_d = din("b_ig", [4, 1]); b_fg_d = din("b_fg", [4, 1])
    g_ml_d = din("g_mlnorm", [1, 2048]); g_cq_d = din("g_cq_fm", [128, 8]); w_uq = din("w_uq", [1024, 3072])
    g_ckv_d = din("g_ckv", [1, 512]); w_uk = din("w_uk", [512, 2048]); w_uv = din("w_uv", [512, 2048])
    w_out = din("w_out", [4096, 4096]); w_ff1 = din("w_ff1", [4096, 16384]); w_ff2 = din("w_ff2", [16384, 4096])
    ident_d = din("ident", [128, 128]); cs_all = din("cs_all", [SEQ, 64]); cs_own = din("cs_own", [NQ, 64])
    bias_mla_d = din("bias_mla", [128, 512]); mask_ml_d = din("mask_ml", [128, 512])
    sel_d = din("sel", [NBLK, NOB]); tril16_d = din("tril16", [16, 16])
    y_own = dout("y_own", [OWN, 4096]); y_smp = dout("y_smp", [64, 4096])
    ckv_o = dout("ckv_o", [SEQ, 512]); kr_o = dout("kr_o", [SEQ, 64])
    C_o = dout("C_o", [4, 256, 512]); n_o = dout("n_o", [4, 256]); m_o = dout("m_o", [4, 1])
    s_ckv = dout("s_ckv", [64, 512]); s_kr = dout("s_kr", [64, 64])
    s_C = dout("s_C", [16, 256, 512]); s_n = dout("s_n", [16, 256]); s_m = dout("s_m", [4, 4])
    NWB = 19 + 19 + 16 + 128
    WB_A = dscr("WB_A", [54, 128, 8192], BF16)
    WB_F1 = dscr("WB_F1", [64, 128, 8192], BF16)
    WB_F2 = dscr("WB_F2", [64, 128, 8192], BF16)

    class _WB:
        def __getitem__(self, i):
            if i < 54:
                return WB_A[i]
            j = i - 54
            return WB_F1[j // 2] if j % 2 == 0 else WB_F2[j // 2]
    WBLK = _WB()
    KT_ml = dscr("KT_ml", [1024, SEQ], BF16); K_tm = dscr("K_tm", [SEQ, 1024], BF16); V_ml = dscr("V_ml", [SEQ, 2048], BF16)
    GATES = dscr("GATES", [8, SEQ], F32); ROWS = dscr("ROWS", [3, 4, SEQ], F32)
    KT_mla = dscr("KT_mla", [2048, SEQ], BF16); KRT = dscr("KRT", [64, SEQ], BF16); V_mla = dscr("V_mla", [SEQ, 2048], BF16)
    QT_ml = dscr("QT_ml", [1024, NQ], BF16); OG = dscr("OG", [NQ, 2048], BF16)
    QN_T = dscr("QN_T", [2048, NQ], BF16); QR_T = dscr("QR_T", [1024, NQ], BF16)
    A_s = dscr("A_s", [NQ, 4096], BF16); X1 = dscr("X1", [NQ, 4096], F32)
    KT_ml_s = dscr("KT_ml_s", [1024, 64], BF16); K_tm_s = dscr("K_tm_s", [64, 1024], BF16); V_ml_s = dscr("V_ml_s", [64, 2048], BF16)
    GATES_s = dscr("GATES_s", [8, 64], F32); ROWS_s = dscr("ROWS_s", [3, 4, 64], F32)
    KT_mla_s = dscr("KT_mla_s", [2048, 64], BF16); KRT_s = dscr("KRT_s", [64, 64], BF16); V_mla_s = dscr("V_mla_s", [64, 2048], BF16)
    KT_mla_p = dscr("KT_mla_p", [4, 2048, PAST], BF16); KRT_p = dscr("KRT_p", [4, 64, PAST], BF16)
    V_mla_p = dscr("V_mla_p", [4, PAST, 2048], BF16)
    junk_ckv = dscr("junk_ckv", [PAST, 512], F32); junk_kr = dscr("junk_kr", [PAST, 64], F32)

    with contextlib.ExitStack() as gst:
        K = Kern(nc, gst)
        op = K.op

        def dma(q, out, in_, reads=(), writes=()):
            h = nc.sync if q == "sq" else nc.gpsimd
            return op(q, lambda: h.dma_start(out=out, in_=in_, allow_slow_non_contiguous=True), reads, writes)

        def copy(en, out, in_, reads, writes):
            if en == "act":
                return op("act", lambda: nc.scalar.copy(out=out, in_=in_), reads, writes)
            h = nc.vector if en == "dve" else nc.gpsimd
            return op(en, lambda: h.tensor_copy(out=out, in_=in_), reads, writes)

        def mm(out, lhsT, rhs, start, stop, reads, writes, inc=None):
            return op("pe", lambda: nc.tensor.matmul(out, lhsT=lhsT, rhs=rhs, start=start, stop=stop),
                      reads, writes, inc=(stop if inc is None else inc))

        def tr(out, in_, idn, reads, writes, inc=True):
            return op("pe", lambda: nc.tensor.transpose(out, in_, idn), reads, writes, inc=inc)

        psA = Ring([K.ps([128, 512], F32, f"psA{i}") for i in range(3)])
        psT = Ring([K.ps([128, 512], F32, f"psT{i}") for i in range(2)])
        psO = Ring([K.ps([128, 512], F32, f"psO{i}") for i in range(2)])
        psX = K.ps([128, 512], F32, "psX")

        def bfv(ps):
            return ps.t[:].bitcast(BF16)

        ident = K.sb([128, 128], F32, "ident"); identb = K.sb([128, 128], BF16, "identb")
        ones_bf = K.sb([128, 128], BF16, "ones_bf")
        mods = K.sb([128, 192, 5], F32, "mods")
        A1 = K.sb([128, 32, 5], F32, "A1"); G1 = K.sb([128, 32, 5], F32, "G1")
        A2 = K.sb([128, 32, 5], F32, "A2"); G2 = K.sb([128, 32, 5], F32, "G2")
        gfm = K.sb([128, 4, 32], F32, "gfm")
        gcq = K.sb([128, 8], F32, "gcq")
        bada = K.sb([128, 192], F32, "bada")
        bigfg = K.sb([4, 2], F32, "bigfg")
        Mown = K.sb([128, 4, NOB], F32, "Mown"); bown = K.sb([128, 4, NOB], F32, "bown")
        negMown = K.sb([128, 4, NOB], F32, "negMown"); ebown = K.sb([128, 4, NOB], F32, "ebown")
        dma("sq", ident.t[:], ident_d, writes=[ident.r])
        dma("gq", identb.t[:], ident_d, writes=[identb.r])
        op("pool", lambda: nc.gpsimd.memset(ones_bf.t[:], 1.0), writes=[ones_bf.r])
        eps_t = K.sb([128, 1], F32, "eps_t")
        op("pool", lambda: nc.gpsimd.memset(eps_t.t[:], EPS), writes=[eps_t.r])
        for i, g in enumerate((gpre1_d, gpost1_d, gpre2_d, gpost2_d)):
            dma("sq", gfm.t[:, i, :], g, writes=[gfm.r])
        dma("sq", gcq.t[:], g_cq_d, writes=[gcq.r])
        dma("sq", bada.t[:], b_ada_fm, writes=[bada.r])
        dma("sq", bigfg.t[:, 0:1], b_ig_d, writes=[bigfg.r])
        dma("sq", bigfg.t[:, 1:2], b_fg_d, writes=[bigfg.r])

        wres = {}
        wshape = {}
        wnext = [0]

        def wconv(name, src3, KC, ncols):
            i = wnext[0]
            wnext[0] += 1
            assert i < NWB and KC * ncols <= 8192
            r = Res(name)
            dst = WBLK[i][:, 0:KC * ncols].rearrange("p (k c) -> p k c", c=ncols)
            dma("gq", dst, src3, writes=[r])
            wres[name] = (i, r)
            wshape[name] = (KC, ncols)

        def w3(src, c0, ncols):
            return src.rearrange("(k p) c -> p k c", p=128)[:, :, c0:c0 + ncols]

        def conv_stage1():
            for i in range(4):
                wconv(f"k{i}", w3(w_in, C_K + 256 * i, 256), 32, 256)
            for i in range(8):
                wconv(f"v{i}", w3(w_in, C_V + 256 * i, 256), 32, 256)
            for i in range(2):
                wconv(f"ckv{i}", w3(w_in, C_CKV + 256 * i, 256), 32, 256)
            wconv("ig", w3(w_in, C_IG, 4), 32, 4)
            wconv("fg", w3(w_in, C_FG, 4), 32, 4)
            wconv("kr", w3(w_in, C_KR, 64), 32, 64)
            wconv("wuk", w3(w_uk, 0, 2048), 4, 2048)
            wconv("wuv", w3(w_uv, 0, 2048), 4, 2048)

        def conv_rest():
            for i in range(4):
                wconv(f"q{i}", w3(w_in, C_Q + 256 * i, 256), 32, 256)
            for i in range(8):
                wconv(f"o{i}", w3(w_in, C_O + 256 * i, 256), 32, 256)
            for i in range(4):
                wconv(f"cq{i}", w3(w_in, C_CQ + 256 * i, 256), 32, 256)
            for i in range(3):
                wconv(f"uq{i}", w3(w_uq, 1024 * i, 1024), 8, 1024)
            for i in range(16):
                wconv(f"wo{i}", w3(w_out, 256 * i, 256), 32, 256)
            for i in range(64):
                wconv(f"f1_{i}", w3(w_ff1, 256 * i, 256), 32, 256)
                wconv(f"f2_{i}", w_ff2[256 * i:256 * (i + 1), :].rearrange("(k p) c -> p k c", p=128), 2, 4096)

        class WLoader:
            def __init__(self, nbuf):
                self.ring = Ring([K.sb([128, 8192], BF16) for _ in range(nbuf)])

            def load(self, name, buf=None):
                i, r = wres[name]
                KC, ncols = wshape[name]
                b = buf or self.ring.next()
                dma("sq", b.t[:, 0:KC * ncols], WBLK[i][:, 0:KC * ncols], reads=[r], writes=[b.r])
                return b, b.t[:, 0:KC * ncols].rearrange("p (k c) -> p k c", c=ncols)

        def rstd_from_ss(out, ss, inv_n, reads, writes):
            P = out.shape[0]
            op("act", lambda: nc.scalar.activation(out=out, in_=ss, func=AF.Sqrt, bias=eps_t.t[0:P, :], scale=inv_n),
               list(reads) + [eps_t.r], writes)
            op("dve", lambda: nc.vector.reciprocal(out=out, in_=out), list(writes), writes)

        sample_groups = [(16 * s, 16, 1 + s) for s in range(4)]

        def prep(src, T, Amod, Bmod, groups, uT, rows_ring, junk, ss, rstd):
            nsub = (T + 127) // 128
            for s in range(nsub):
                rows = min(128, T - s * 128)
                xt = rows_ring.next()
                dma("sq", xt.t[:rows, :], src[s * 128:s * 128 + rows, :], writes=[xt.r])
                op("act", lambda: nc.scalar.activation(out=junk.t[:rows, :], in_=xt.t[:rows, :], func=AF.Square,
                                                       accum_out=ss.t[:rows, :]), [xt.r], [junk.r, ss.r])
                rstd_from_ss(rstd.t[:rows, :], ss.t[:rows, :], 1.0 / 4096, [ss.r], [rstd.r])
                op("act", lambda: nc.scalar.activation(out=xt.t[:rows, :], in_=xt.t[:rows, :], func=AF.Copy,
                                                       scale=rstd.t[:rows, :]), [rstd.r, xt.r], [xt.r])
                for g in range(8):
                    pt = psT.next()
                    for i in range(4):
                        k = g * 4 + i
                        tr(pt.t[:, i * 128:i * 128 + rows], xt.t[:rows, k * 128:(k + 1) * 128], ident.t[:rows, :rows],
                           [xt.r, ident.r], [pt.r], inc=(i == 3))
                    for i in range(4):
                        k = g * 4 + i
                        for (c0, n, r) in groups:
                            lo, hi = max(c0, s * 128), min(c0 + n, s * 128 + rows)
                            if lo >= hi:
                                continue
                            o_ = uT.t[:, k, lo:hi]
                            i_ = pt.t[:, i * 128 + lo - s * 128:i * 128 + hi - s * 128]
                            a_, b_ = Amod.t[:, k, r:r + 1], Bmod[:, k, r:r + 1]
                            if K.alt() == "act":
                                op("act", lambda: nc.scalar.activation(out=o_, in_=i_, func=AF.Identity, bias=b_, scale=a_),
                                   [pt.r, Amod.r, mods.r], [uT.r])
                            else:
                                op("dve", lambda: nc.vector.tensor_scalar(out=o_, in0=i_, scalar1=a_, scalar2=b_,
                                                                          op0=ALU.mult, op1=ALU.add),
                                   [pt.r, Amod.r, mods.r], [uT.r])

        def prep_plain(src, T, aT, rowsb_ring):
            nsub = (T + 127) // 128
            for s in range(nsub):
                rows = min(128, T - s * 128)
                xt = rowsb_ring.next()
                dma("sq", xt.t[:rows, :], src[s * 128:s * 128 + rows, :], writes=[xt.r])
                for g in range(4):
                    pt = psT.next()
                    pv = bfv(pt)
                    for i in range(8):
                        k = g * 8 + i
                        tr(pv[:, i * 128:i * 128 + rows], xt.t[:rows, k * 128:(k + 1) * 128], identb.t[:rows, :rows],
                           [xt.r, identb.r], [pt.r], inc=(i == 7))
                    o_ = aT.t[:, g * 8:(g + 1) * 8, s * 128:s * 128 + rows]
                    i_ = pv.rearrange("p (a b) -> p a b", b=128)[:, :, 0:rows]
                    copy(K.alt(), o_, i_, [pt.r], [aT.r])

        def post(yT, T, Gmod, groups, res_src, dst, rows_ring, sq_ring, rbc):
            for k in range(32):
                sqb = sq_ring.next()
                op("act", lambda: nc.scalar.activation(out=sqb.t[:, 0:T], in_=yT.t[:, k, 0:T], func=AF.Square),
                   [yT.r], [sqb.r])
                mm(psX.t[:, 0:T], ones_bf.t[:], sqb.t[:, 0:T], k == 0, k == 31, [ones_bf.r, sqb.r], [psX.r], inc=True)
            rstd_from_ss(rbc.t[:, 0:T], psX.t[:, 0:T], 1.0 / 4096, [psX.r], [rbc.r])
            for k in range(32):
                for (c0, n, r) in groups:
                    y_ = yT.t[:, k, c0:c0 + n]
                    op("dve", lambda: nc.vector.scalar_tensor_tensor(out=y_, in0=y_, scalar=Gmod.t[:, k, r:r + 1],
                                                          in1=rbc.t[:, c0:c0 + n], op0=ALU.mult, op1=ALU.mult),
                       [yT.r, Gmod.r, rbc.r], [yT.r])
            nsub = (T + 127) // 128
            for s in range(nsub):
                rows = min(128, T - s * 128)
                xr = rows_ring.next()
                dma("sq", xr.t[:rows, :], res_src[s * 128:s * 128 + rows, :], writes=[xr.r])
                for g in range(8):
                    pt = psT.next()
                    for i in range(4):
                        tr(pt.t[:rows, i * 128:(i + 1) * 128], yT.t[:, g * 4 + i, s * 128:s * 128 + rows], ident.t[:],
                           [yT.r, ident.r], [pt.r], inc=(i == 3))
                    x_ = xr.t[:rows, g * 512:(g + 1) * 512]
                    op("dve", lambda: nc.vector.tensor_tensor(out=x_, in0=pt.t[:rows, 0:512], in1=x_, op=ALU.add),
                       [pt.r, xr.r], [xr.r])
                dma("sq", dst[s * 128:s * 128 + rows, :], xr.t[:rows, :], reads=[xr.r])

        conv_stage1()
        with K.scope():
            cT = K.sb([128, 32, 5], F32); cs = K.sb([128, 32, 5], BF16)
            dma("sq", cT.t[:], c_T, writes=[cT.r])
            op("act", lambda: nc.scalar.activation(out=cs.t[:], in_=cT.t[:], func=AF.Silu), [cT.r], [cs.r])
            wr = Ring([K.sb([128, 32, 256], BF16) for _ in range(3)])
            t5r = Ring([K.sb([8, 256], F32) for _ in range(2)])
            wav = w_ada.rearrange("(k p) c -> p k c", p=128)
            for cb in range(96):
                wb = wr.next()
                dma("gq", wb.t[:], wav[:, :, cb * 256:(cb + 1) * 256], writes=[wb.r])
                ps = psA.next()
                for k in range(32):
                    mm(ps.t[0:5, 0:256], cs.t[:, k, :], wb.t[:, k, :], k == 0, k == 31, [cs.r, wb.r], [ps.r])
                t5 = t5r.next()
                copy("act", t5.t[0:5, :], ps.t[0:5, 0:256], [ps.r], [t5.r])
                pt = psT.next()
                for i in range(2):
                    tr(pt.t[:, i * 5:(i + 1) * 5], t5.t[0:5, i * 128:(i + 1) * 128], ident.t[0:5, 0:5],
                       [t5.r, ident.r], [pt.r], inc=(i == 1))
                op("dve", lambda: nc.vector.tensor_tensor(
                    out=mods.t[:, 2 * cb:2 * cb + 2, :], in0=pt.t[:, 0:10].rearrange("p (a b) -> p a b", b=5),
                    in1=bada.t[:, 2 * cb:2 * cb + 2, None].to_broadcast([128, 2, 5]), op=ALU.add),
                   [pt.r, bada.r], [mods.r])
            conv_rest()

            def bc5(i):
                return gfm.t[:, i, :, None].to_broadcast([128, 32, 5])
            op("dve", lambda: nc.vector.scalar_tensor_tensor(out=A1.t[:], in0=mods.t[:, 32:64, :], scalar=1.0, in1=bc5(0),
                                                             op0=ALU.add, op1=ALU.mult), [mods.r, gfm.r], [A1.r])
            op("dve", lambda: nc.vector.tensor_tensor(out=G1.t[:], in0=mods.t[:, 64:96, :], in1=bc5(1), op=ALU.mult),
               [mods.r, gfm.r], [G1.r])
            op("dve", lambda: nc.vector.scalar_tensor_tensor(out=A2.t[:], in0=mods.t[:, 128:160, :], scalar=1.0, in1=bc5(2),
                                                             op0=ALU.add, op1=ALU.mult), [mods.r, gfm.r], [A2.r])
            op("dve", lambda: nc.vector.tensor_tensor(out=G2.t[:], in0=mods.t[:, 160:192, :], in1=bc5(3), op=ALU.mult),
               [mods.r, gfm.r], [G2.r])
        B1 = mods.t[:, 0:32, :]
        B2 = mods.t[:, 96:128, :]
        if stop_after <= 0:
            K.finish()
            return nc

        def mla_up(ckv32, kro, T, wuk, wuv, dKT, dKRT, dV, c0, L):
            nsub = (T + 127) // 128
            rws = [min(128, T - s * 128) for s in range(nsub)]
            op("dve", lambda: nc.vector.tensor_copy(out=L["ckvb"].t[:, 0:nsub, :], in_=ckv32.t[:, 0:nsub, :]),
               [ckv32.r], [L["ckvb"].r])
            op("dve", lambda: nc.vector.tensor_copy(out=L["krb"].t[:, 0:nsub, :], in_=kro.t[:, 0:nsub, :]),
               [kro.r], [L["krb"].r])
            for c in range(4):
                pt = psT.next()
                pv = bfv(pt)
                for s in range(nsub):
                    tr(pv[:, s * 128:s * 128 + rws[s]], L["ckvb"].t[:rws[s], s, c * 128:(c + 1) * 128],
                       identb.t[:rws[s], :rws[s]], [L["ckvb"].r, identb.r], [pt.r], inc=(s == nsub - 1))
                copy(K.alt(), L["ckvT"].t[:, c, 0:T], pv[:, 0:T], [pt.r], [L["ckvT"].r])
            pt = psT.next()
            pv = bfv(pt)
            for s in range(nsub):
                tr(pv[0:64, s * 128:s * 128 + rws[s]], L["krb"].t[:rws[s], s, :], identb.t[:rws[s], :rws[s]],
                   [L["krb"].r, identb.r], [pt.r], inc=(s == nsub - 1))
            copy(K.alt(), L["krT"].t[:, 0:T], pv[0:64, 0:T], [pt.r], [L["krT"].r])
            dma("sq", dKRT[:, c0:c0 + T], L["krT"].t[:, 0:T], reads=[L["krT"].r])
            for h in range(16):
                ps = psA.next()
                for c in range(4):
                    mm(ps.t[:, 0:T], wuk[1][:, c, h * 128:(h + 1) * 128], L["ckvT"].t[:, c, 0:T], c == 0, c == 3,
                       [wuk[0].r, L["ckvT"].r], [ps.r])
                st = L["st512"].next()
                copy(K.alt(), st.t[:, 0:T], ps.t[:, 0:T], [ps.r], [st.r])
                dma("sq", dKT[h * 128:(h + 1) * 128, c0:c0 + T], st.t[:, 0:T], reads=[st.r])
            vst = L["vst"]
            for s in range(nsub):
                for cb in range(4):
                    ps = psA.next()
                    for c in range(4):
                        mm(ps.t[:rws[s], 0:512], L["ckvT"].t[:, c, s * 128:s * 128 + rws[s]], wuv[1][:, c, cb * 512:(cb + 1) * 512],
                           c == 0, c == 3, [wuv[0].r, L["ckvT"].r], [ps.r])
                    copy(K.alt(), vst.t[:rws[s], s, cb * 512:(cb + 1) * 512], ps.t[:rws[s], 0:512], [ps.r], [vst.r])
            if T % 128 == 0:
                dma("sq", dV[c0:c0 + T, :].rearrange("(s p) f -> p s f", p=128), vst.t[:, 0:nsub, :], reads=[vst.r])
            else:
                dma("sq", dV[c0:c0 + T, :], vst.t[:T, 0, :], reads=[vst.r])

        def tm_out(dst, c0, T, buf, width):
            if T % 128 == 0:
                dma("sq", dst[c0:c0 + T, :].rearrange("(s p) f -> p s f", p=128), buf.t[:, 0:T // 128, 0:width], reads=[buf.r])
            else:
                dma("sq", dst[c0:c0 + T, :], buf.t[:T, 0, 0:width], reads=[buf.r])

        def tm_in(buf, src, c0, T, width):
            if T % 128 == 0:
                dma("sq", buf.t[:, 0:T // 128, 0:width], src[c0:c0 + T, :].rearrange("(s p) f -> p s f", p=128), writes=[buf.r])
            else:
                dma("sq", buf.t[:T, 0, 0:width], src[c0:c0 + T, :], writes=[buf.r])

        def rope_tm(x, cst, nsub, rows, L, out):
            x1, x2 = x.t[:rows, 0:nsub, 0:32], x.t[:rows, 0:nsub, 32:64]
            co, si = cst.t[:rows, 0:nsub, 0:32], cst.t[:rows, 0:nsub, 32:64]
            t = [L["rt"].t[:rows, i, 0:nsub, :] for i in range(4)]
            rr = [x.r, cst.r]
            op("dve", lambda: nc.vector.tensor_tensor(out=t[0], in0=x1, in1=co, op=ALU.mult), rr, [L["rt"].r])
            op("dve", lambda: nc.vector.tensor_tensor(out=t[1], in0=x2, in1=si, op=ALU.mult), rr, [L["rt"].r])
            op("dve", lambda: nc.vector.tensor_tensor(out=t[2], in0=x2, in1=co, op=ALU.mult), rr, [L["rt"].r])
            op("dve", lambda: nc.vector.tensor_tensor(out=t[3], in0=x1, in1=si, op=ALU.mult), rr, [L["rt"].r])
            op("dve", lambda: nc.vector.tensor_tensor(out=out.t[:rows, 0:nsub, 0:32], in0=t[0], in1=t[1], op=ALU.subtract),
               [L["rt"].r], [out.r])
            op("dve", lambda: nc.vector.tensor_tensor(out=out.t[:rows, 0:nsub, 32:64], in0=t[2], in1=t[3], op=ALU.add),
               [L["rt"].r], [out.r])

        def kv_side(uT, T, c0, D, WL, wuk, wuv, L, cs_src):
            nsub = (T + 127) // 128
            rws = [min(128, T - s * 128) for s in range(nsub)]
            ktm = L["ktm"]
            for cb in range(4):
                wb, wv = WL.load(f"k{cb}")
                for cc in range(2):
                    ch = cb * 2 + cc
                    ps = psA.next()
                    for k in range(32):
                        mm(ps.t[:, 0:T], wv[:, k, cc * 128:(cc + 1) * 128], uT.t[:, k, 0:T], k == 0, k == 31, [wb.r, uT.r], [ps.r])
                    st = L["st512"].next()
                    if K.alt() == "act":
                        op("act", lambda: nc.scalar.activation(out=st.t[:, 0:T], in_=ps.t[:, 0:T], func=AF.Copy, scale=0.0625),
                           [ps.r], [st.r])
                    else:
                        op("dve", lambda: nc.vector.tensor_scalar(out=st.t[:, 0:T], in0=ps.t[:, 0:T], scalar1=0.0625,
                                                                  scalar2=None, op0=ALU.mult), [ps.r], [st.r])
                    dma("sq", D["KT_ml"][ch * 128:(ch + 1) * 128, c0:c0 + T], st.t[:, 0:T], reads=[st.r])
                    pt = psT.next()
                    pv = bfv(pt)
                    for s in range(nsub):
                        tr(pv[:rws[s], s * 128:(s + 1) * 128], st.t[:, s * 128:s * 128 + rws[s]], identb.t[:],
                           [st.r, identb.r], [pt.r], inc=(s == nsub - 1))
                    rmax = rws[0]
                    copy(K.alt(), ktm.t[:rmax, 0:nsub, ch * 128:(ch + 1) * 128],
                         pv.rearrange("p (a b) -> p a b", b=128)[:rmax, 0:nsub, :], [pt.r], [ktm.r])
            tm_out(D["K_tm"], c0, T, ktm, 1024)
            vst = L["vst"]
            for cb in range(8):
                wb, wv = WL.load(f"v{cb}")
                for s in range(nsub):
                    ps = psA.next()
                    for k in range(32):
                        mm(ps.t[:rws[s], 0:256], uT.t[:, k, s * 128:s * 128 + rws[s]], wv[:, k, :], k == 0, k == 31, [wb.r, uT.r], [ps.r])
                    copy(K.alt(), vst.t[:rws[s], s, cb * 256:(cb + 1) * 256], ps.t[:rws[s], 0:256], [ps.r], [vst.r])
            tm_out(D["V_ml"], c0, T, vst, 2048)
            for gi, nm in enumerate(("ig", "fg")):
                wb, wv = WL.load(nm)
                ps = psA.next()
                for k in range(32):
                    mm(ps.t[0:4, 0:T], wv[:, k, :], uT.t[:, k, 0:T], k == 0, k == 31, [wb.r, uT.r], [ps.r])
                gs = L["gst"].next()
                op("act", lambda: nc.scalar.activation(out=gs.t[0:4, 0:T], in_=ps.t[0:4, 0:T], func=AF.Identity,
                                                       bias=bigfg.t[:, gi:gi + 1], scale=1.0), [ps.r, bigfg.r], [gs.r])
                dma("sq", D["GATES"][4 * gi:4 * gi + 4, c0:c0 + T], gs.t[0:4, 0:T], reads=[gs.r])
            ckv32 = L["ckv32"]
            for cb in range(2):
                wb, wv = WL.load(f"ckv{cb}")
                for s in range(nsub):
                    ps = psA.next()
                    for k in range(32):
                        mm(ps.t[:rws[s], 0:256], uT.t[:, k, s * 128:s * 128 + rws[s]], wv[:, k, :], k == 0, k == 31, [wb.r, uT.r], [ps.r])
                    copy(K.alt(), ckv32.t[:rws[s], s, cb * 256:(cb + 1) * 256], ps.t[:rws[s], 0:256], [ps.r], [ckv32.r])
            wb, wv = WL.load("kr")
            kr32 = L["kr32"]
            for s in range(nsub):
                ps = psA.next()
                for k in range(32):
                    mm(ps.t[:rws[s], 0:64], uT.t[:, k, s * 128:s * 128 + rws[s]], wv[:, k, :], k == 0, k == 31, [wb.r, uT.r], [ps.r])
                copy(K.alt(), kr32.t[:rws[s], s, :], ps.t[:rws[s], 0:64], [ps.r], [kr32.r])
            ss4, rs4 = L["ss4"], L["rs4"]
            for s in range(nsub):
                op("act", lambda: nc.scalar.activation(out=L["junk512"].t[:rws[s], :], in_=ckv32.t[:rws[s], s, :], func=AF.Square,
                                                       accum_out=ss4.t[:rws[s], s:s + 1]), [ckv32.r], [L["junk512"].r, ss4.r])
            rstd_from_ss(rs4.t[:rws[0], 0:nsub], ss4.t[:rws[0], 0:nsub], 1.0 / 512, [ss4.r], [rs4.r])
            for s in range(nsub):
                c_ = ckv32.t[:rws[s], s, :]
                op("dve", lambda: nc.vector.scalar_tensor_tensor(out=c_, in0=c_, scalar=rs4.t[:rws[s], s:s + 1],
                                                                 in1=L["gckv"].t[:rws[s], :], op0=ALU.mult, op1=ALU.mult),
                   [ckv32.r, rs4.r, L["gckv"].r], [ckv32.r])
            tm_out(D["ckv_o"], c0, T, ckv32, 512)
            cst = L["cst"]
            tm_in(cst, cs_src, 0, T, 64)
            rope_tm(kr32, cst, nsub, rws[0], L, L["kro"])
            tm_out(D["kr_o"], c0, T, L["kro"], 64)
            mla_up(ckv32, L["kro"], T, wuk, wuv, D["KT_mla"], D["KRT"], D["V_mla"], c0, L)

        def kv_locals():
            L = {}
            L["ktm"] = K.sb([128, 4, 1024], BF16)
            L["vst"] = K.sb([128, 4, 2048], BF16)
            L["st512"] = Ring([K.sb([128, 512], BF16) for _ in range(3)])
            L["gst"] = Ring([K.sb([4, 512], F32) for _ in range(2)])
            L["ckv32"] = K.sb([128, 4, 512], F32)
            L["kr32"] = K.sb([128, 4, 64], F32)
            L["kro"] = K.sb([128, 4, 64], F32)
            L["cst"] = K.sb([128, 4, 64], F32)
            L["rt"] = K.sb([128, 4, 4, 32], F32)
            L["ss4"] = K.sb([128, 4], F32)
            L["rs4"] = K.sb([128, 4], F32)
            L["junk512"] = K.sb([128, 512], BF16)
            L["gckv"] = K.sb([128, 512], F32)
            L["ckvb"] = K.sb([128, 4, 512], BF16)
            L["krb"] = K.sb([128, 4, 64], BF16)
            L["ckvT"] = K.sb([128, 4, 512], BF16)
            L["krT"] = K.sb([64, 512], BF16)
            dma("sq", L["gckv"].t[:], g_ckv_d.partition_broadcast(128), writes=[L["gckv"].r])
            return L

        Dp = dict(KT_ml=KT_ml, K_tm=K_tm, V_ml=V_ml, GATES=GATES, ckv_o=ckv_o, kr_o=kr_o, KT_mla=KT_mla, KRT=KRT, V_mla=V_mla)
        Ds = dict(KT_ml=KT_ml_s, K_tm=K_tm_s, V_ml=V_ml_s, GATES=GATES_s, ckv_o=s_ckv, kr_o=s_kr, KT_mla=KT_mla_s, KRT=KRT_s,
                  V_mla=V_mla_s)

        with K.scope():
            uT = K.sb([128, 32, 512], BF16)
            rows_ring = Ring([K.sb([128, 4096], F32) for _ in range(1)])
            ss = K.sb([128, 1], F32); rstd = K.sb([128, 1], F32)
            WL = WLoader(3)
            L = kv_locals()
            junk = View(L["vst"].t[:, 0:2, :].rearrange("p a b -> p (a b)"), L["vst"].r)
            wukb = K.sb([128, 8192], BF16); wuvb = K.sb([128, 8192], BF16)
            wuk = WL.load("wuk", wukb); wuv = WL.load("wuv", wuvb)
            for ti in range(NTA):
                prep(x_all[ti * 512:(ti + 1) * 512, :], 512, A1, B1, [(0, 512, 0)], uT, rows_ring, junk, ss, rstd)
                kv_side(uT, 512, ti * 512, Dp, WL, wuk, wuv, L, cs_all[ti * 512:(ti + 1) * 512, :])
            prep(x_smp, 64, A1, B1, sample_groups, uT, rows_ring, junk, ss, rstd)
            kv_side(uT, 64, 0, Ds, WL, wuk, wuv, L, cs_own[OWN:OWN + 64, :])
            for sq_ in range(4):
                for ti in range(NPT):
                    tm_in(L["ckv32"], ckv_past[sq_], ti * 512, 512, 512)
                    tm_in(L["kro"], kr_past[sq_], ti * 512, 512, 64)
                    mla_up(L["ckv32"], L["kro"], 512, wuk, wuv, KT_mla_p[sq_], KRT_p[sq_], V_mla_p[sq_], ti * 512, L)
        if stop_after <= 1:
            K.finish()
            return nc

        def vmemset(buf, ap, val):
            op("dve", lambda: nc.vector.memset(ap, val), [], [buf.r])

        with K.scope():
            SG = min(2048, SEQ)
            ones4 = K.sb([4, SG], F32); vmemset(ones4, ones4.t[:], 1.0)
            one1 = K.sb([4, 1], F32); vmemset(one1, one1.t[:], 1.0)
            bprev = K.sb([4, 1], F32); Mprev = K.sb([4, 1], F32)
            igt = K.sb([4, SG], F32); fgt = K.sb([4, SG], F32); bt = K.sb([4, SG], F32)
            ct = K.sb([4, SG], F32); Mt = K.sb([4, SG], F32)
            m0T = K.sb([4, 4], F32); smt = K.sb([4, 4], F32); mf = K.sb([4, 1], F32)
            dma("sq", m0T.t[:], m0_in, writes=[m0T.r])

            def scan_seg(ig_src, fg_src, n, c_dst, b_dst, M_dst):
                dma("sq", igt.t[:, 0:n], ig_src, writes=[igt.r])
                dma("sq", fgt.t[:, 0:n], fg_src, writes=[fgt.r])
                f_ = fgt.t[:, 0:n]
                op("act", lambda: nc.scalar.activation(out=f_, in_=f_, func=AF.Exp, scale=-1.0), [fgt.r], [fgt.r])
                op("act", lambda: nc.scalar.activation(out=f_, in_=f_, func=AF.Ln, bias=one1.t[:, 0:1], scale=1.0),
                   [fgt.r, one1.r], [fgt.r])
                op("dve", lambda: nc.vector.tensor_tensor_scan(out=bt.t[:, 0:n], data0=ones4.t[:, 0:n], data1=f_,
                                                               initial=bprev.t[:, 0:1], op0=ALU.mult, op1=ALU.subtract),
                   [ones4.r, fgt.r, bprev.r], [bt.r])
                op("dve", lambda: nc.vector.tensor_tensor(out=ct.t[:, 0:n], in0=igt.t[:, 0:n], in1=bt.t[:, 0:n], op=ALU.subtract),
                   [igt.r, bt.r], [ct.r])
                op("dve", lambda: nc.vector.tensor_tensor_scan(out=Mt.t[:, 0:n], data0=ones4.t[:, 0:n], data1=ct.t[:, 0:n],
                                                               initial=Mprev.t[:, 0:1], op0=ALU.mult, op1=ALU.max),
                   [ones4.r, ct.r, Mprev.r], [Mt.r])
                op("dve", lambda: nc.vector.tensor_copy(out=bprev.t[:], in_=bt.t[:, n - 1:n]), [bt.r], [bprev.r])
                op("dve", lambda: nc.vector.tensor_copy(out=Mprev.t[:], in_=Mt.t[:, n - 1:n]), [Mt.r], [Mprev.r])
                dma("sq", c_dst, ct.t[:, 0:n], reads=[ct.r])
                dma("sq", b_dst, bt.t[:, 0:n], reads=[bt.r])
                dma("sq", M_dst, Mt.t[:, 0:n], reads=[Mt.r])

            vmemset(bprev, bprev.t[:], 0.0); vmemset(Mprev, Mprev.t[:], 0.0)
            for g0 in range(0, SEQ, SG):
                scan_seg(GATES[0:4, g0:g0 + SG], GATES[4:8, g0:g0 + SG], SG,
                         ROWS[0, :, g0:g0 + SG], ROWS[1, :, g0:g0 + SG], ROWS[2, :, g0:g0 + SG])
            op("dve", lambda: nc.vector.tensor_tensor(out=mf.t[:], in0=bprev.t[:], in1=Mprev.t[:], op=ALU.add),
               [bprev.r, Mprev.r], [mf.r])
            dma("sq", m_o, mf.t[:], reads=[mf.r])
            for s_ in range(4):
                vmemset(bprev, bprev.t[:], 0.0)
                op("dve", lambda: nc.vector.tensor_copy(out=Mprev.t[:], in_=m0T.t[:, s_:s_ + 1]), [m0T.r], [Mprev.r])
                scan_seg(GATES_s[0:4, 16 * s_:16 * s_ + 16], GATES_s[4:8, 16 * s_:16 * s_ + 16], 16,
                         ROWS_s[0, :, 16 * s_:16 * s_ + 16], ROWS_s[1, :, 16 * s_:16 * s_ + 16], ROWS_s[2, :, 16 * s_:16 * s_ + 16])
                op("dve", lambda: nc.vector.tensor_tensor(out=smt.t[:, s_:s_ + 1], in0=bprev.t[:], in1=Mprev.t[:], op=ALU.add),
                   [bprev.r, Mprev.r], [smt.r])
            dma("sq", s_m, smt.t[:], reads=[smt.r])
        if stop_after <= 2:
            K.finish()
            return nc

        with K.scope():
            selsb = K.sb([NBLK, NOB], F32)
            dma("sq", selsb.t[:], sel_d, writes=[selsb.r])
            rowr = Ring([K.sb([NBLK, 128], F32) for _ in range(2)])
            for h in range(4):
                for ridx, dst in ((2, Mown), (1, bown)):
                    rb = rowr.next()
                    dma("sq", rb.t[:], ROWS[ridx, h, :].rearrange("(b t) -> b t", t=128), writes=[rb.r])
                    mm(psX.t[:, 0:NOB], rb.t[:, :], selsb.t[:, :], True, True, [rb.r, selsb.r], [psX.r])
                    copy("dve", dst.t[:, h, :], psX.t[:, 0:NOB], [psX.r], [dst.r])
            op("dve", lambda: nc.vector.tensor_scalar(out=negMown.t[:], in0=Mown.t[:], scalar1=-1.0, scalar2=None, op0=ALU.mult),
               [Mown.r], [negMown.r])
            op("dve", lambda: nc.vector.tensor_tensor(out=ebown.t[:], in0=bown.t[:], in1=Mown.t[:], op=ALU.add),
               [bown.r, Mown.r], [ebown.r])
            op("act", lambda: nc.scalar.activation(out=ebown.t[:], in_=ebown.t[:], func=AF.Exp, scale=-1.0), [ebown.r], [ebown.r])
            wcol = K.sb([128, 4, NBLK], F32); wcolb = K.sb([128, 4, NBLK], BF16)
            mlb = K.sb([128, 4], F32)
            for h in range(4):
                rb = rowr.next()
                dma("sq", rb.t[:], ROWS[0, h, :].rearrange("(b t) -> b t", t=128), writes=[rb.r])
                dma("sq", mlb.t[:, h:h + 1], ROWS[2, h, SEQ - 1:SEQ].partition_broadcast(128), writes=[mlb.r])
                pt = psT.next()
                tr(pt.t[:, 0:NBLK], rb.t[:, :], ident.t[:NBLK, :NBLK], [rb.r, ident.r], [pt.r])
                op("dve", lambda: nc.vector.tensor_scalar(out=wcol.t[:, h, :], in0=pt.t[:, 0:NBLK], scalar1=mlb.t[:, h:h + 1],
                                                          scalar2=None, op0=ALU.subtract), [pt.r, mlb.r], [wcol.r])
            op("act", lambda: nc.scalar.activation(out=wcol.t[:], in_=wcol.t[:], func=AF.Exp), [wcol.r], [wcol.r])
            copy("dve", wcolb.t[:], wcol.t[:], [wcol.r], [wcolb.r])
            kbr = Ring([K.sb([128, 256], BF16) for _ in range(3)])
            vbr = Ring([K.sb([128, 512], BF16) for _ in range(3)])
            kgr = Ring([K.sb([128, 256], BF16) for _ in range(3)])
            cstr = Ring([K.sb([128, 512], F32) for _ in range(2)])
            nst = K.sb([1, 256], F32)
            for h in range(4):
                po = [psO.next(), psO.next()]
                for blk in range(NBLK):
                    kb, vb, kg = kbr.next(), vbr.next(), kgr.next()
                    dma("sq", kb.t[:], K_tm[blk * 128:(blk + 1) * 128, h * 256:(h + 1) * 256], writes=[kb.r])
                    dma("sq", vb.t[:], V_ml[blk * 128:(blk + 1) * 128, h * 512:(h + 1) * 512], writes=[vb.r])
                    op("dve", lambda: nc.vector.tensor_scalar(out=kg.t[:], in0=kb.t[:], scalar1=wcol.t[:, h, blk:blk + 1],
                                                              scalar2=None, op0=ALU.mult), [kb.r, wcol.r], [kg.r])
                    for dc in range(2):
                        op("pe", lambda: nc.tensor.matmul(po[dc].t[:, 0:512], lhsT=kg.t[:, dc * 128:(dc + 1) * 128], rhs=vb.t[:],
                                                          start=(blk == 0), stop=(blk == NBLK - 1)),
                           [kg.r, vb.r], [po[dc].r], inc=True)
                    op("pe", lambda: nc.tensor.matmul(psX.t[0:1, 0:256], lhsT=wcolb.t[:, h, blk:blk + 1], rhs=kb.t[:],
                                                      start=(blk == 0), stop=(blk == NBLK - 1)),
                       [wcolb.r, kb.r], [psX.r], inc=True)
                for dc in range(2):
                    cs_ = cstr.next()
                    copy(K.alt(), cs_.t[:], po[dc].t[:, 0:512], [po[dc].r], [cs_.r])
                    dma("sq", C_o[h, dc * 128:(dc + 1) * 128, :], cs_.t[:], reads=[cs_.r])
                copy("dve", nst.t[:], psX.t[0:1, 0:256], [psX.r], [nst.r])
                dma("sq", n_o[h:h + 1, :], nst.t[:], reads=[nst.r])
        if stop_after <= 3:
            K.finish()
            return nc

        def q_side(uT, T, c0, WL, L):
            nsub = (T + 127) // 128
            rws = [min(128, T - s * 128) for s in range(nsub)]
            for cb in range(4):
                wb, wv = WL.load(f"q{cb}")
                for cc in range(2):
                    ch = cb * 2 + cc
                    ps = psA.next()
                    for k in range(32):
                        mm(ps.t[:, 0:T], wv[:, k, cc * 128:(cc + 1) * 128], uT.t[:, k, 0:T], k == 0, k == 31, [wb.r, uT.r], [ps.r])
                    st = L["st512"].next()
                    copy(K.alt(), st.t[:, 0:T], ps.t[:, 0:T], [ps.r], [st.r])
                    dma("sq", QT_ml[ch * 128:(ch + 1) * 128, c0:c0 + T], st.t[:, 0:T], reads=[st.r])
            og = L["og"]
            for cb in range(8):
                wb, wv = WL.load(f"o{cb}")
                for s in range(nsub):
                    ps = psA.next()
                    for k in range(32):
                        mm(ps.t[:rws[s], 0:256], uT.t[:, k, s * 128:s * 128 + rws[s]], wv[:, k, :], k == 0, k == 31, [wb.r, uT.r], [ps.r])
                    op("act", lambda: nc.scalar.activation(out=og.t[:rws[s], s, cb * 256:(cb + 1) * 256], in_=ps.t[:rws[s], 0:256],
                                                           func=AF.Sigmoid), [ps.r], [og.r])
            tm_out(OG, c0, T, og, 2048)
            cq32, cqnT, rbc = L["cq32"], L["cqnT"], L["rbc"]
            for cb in range(4):
                wb, wv = WL.load(f"cq{cb}")
                for cc in range(2):
                    ch = cb * 2 + cc
                    ps = psA.next()
                    for k in range(32):
                        mm(ps.t[:, 0:T], wv[:, k, cc * 128:(cc + 1) * 128], uT.t[:, k, 0:T], k == 0, k == 31, [wb.r, uT.r], [ps.r])
                    copy("dve", cq32.t[:, ch, 0:T], ps.t[:, 0:T], [ps.r], [cq32.r])
                    sqb = L["sqb"].next()
                    op("act", lambda: nc.scalar.activation(out=sqb.t[:, 0:T], in_=cq32.t[:, ch, 0:T], func=AF.Square), [cq32.r], [sqb.r])
                    mm(psX.t[:, 0:T], ones_bf.t[:], sqb.t[:, 0:T], ch == 0, ch == 7, [ones_bf.r, sqb.r], [psX.r], inc=True)
            rstd_from_ss(rbc.t[:, 0:T], psX.t[:, 0:T], 1.0 / 1024, [psX.r], [rbc.r])
            for ch in range(8):
                op("dve", lambda: nc.vector.scalar_tensor_tensor(out=cqnT.t[:, ch, 0:T], in0=cq32.t[:, ch, 0:T], scalar=gcq.t[:, ch:ch + 1],
                                                                 in1=rbc.t[:, 0:T], op0=ALU.mult, op1=ALU.mult),
                   [cq32.r, gcq.r, rbc.r], [cqnT.r])
            qm32, qmb, rt2, cso, qnst, qrst = L["qm32"], L["qmb"], L["rt2"], L["cso"], L["qnst"], L["qrst"]
            for s in range(nsub):
                rows = rws[s]
                for cb in range(3):
                    wb, wv = WL.load(f"uq{cb}")
                    for half in range(2):
                        ps = psA.next()
                        for k in range(8):
                            mm(ps.t[:rows, 0:512], cqnT.t[:, k, s * 128:s * 128 + rows], wv[:, k, half * 512:(half + 1) * 512],
                               k == 0, k == 7, [wb.r, cqnT.r], [ps.r])
                        copy(K.alt(), qm32.t[:rows, cb * 1024 + half * 512:cb * 1024 + (half + 1) * 512], ps.t[:rows, 0:512], [ps.r], [qm32.r])
                dma("sq", cso.t[:rows, :], cs_own[c0 + s * 128:c0 + s * 128 + rows, :], writes=[cso.r])
                qv = qm32.t[:rows, :].rearrange("p (h e) -> p h e", e=192)
                qb = qmb.t[:rows, :].rearrange("p (h e) -> p h e", e=192)
                x1, x2 = qv[:, :, 128:160], qv[:, :, 160:192]
                co = cso.t[:rows, None, 0:32].to_broadcast([rows, 16, 32])
                si = cso.t[:rows, None, 32:64].to_broadcast([rows, 16, 32])
                t = [rt2.t[:rows, i, :, :] for i in range(4)]
                rr = [qm32.r, cso.r]
                op("dve", lambda: nc.vector.tensor_tensor(out=t[0], in0=x1, in1=co, op=ALU.mult), rr, [rt2.r])
                op("dve", lambda: nc.vector.tensor_tensor(out=t[1], in0=x2, in1=si, op=ALU.mult), rr, [rt2.r])
                op("dve", lambda: nc.vector.tensor_tensor(out=t[2], in0=x2, in1=co, op=ALU.mult), rr, [rt2.r])
                op("dve", lambda: nc.vector.tensor_tensor(out=t[3], in0=x1, in1=si, op=ALU.mult), rr, [rt2.r])
                op("act", lambda: nc.scalar.copy(out=qb[:, :, 0:128], in_=qv[:, :, 0:128]), [qm32.r], [qmb.r])
                op("dve", lambda: nc.vector.tensor_tensor(out=qb[:, :, 128:160], in0=t[0], in1=t[1], op=ALU.subtract), [rt2.r], [qmb.r])
                op("dve", lambda: nc.vector.tensor_tensor(out=qb[:, :, 160:192], in0=t[2], in1=t[3], op=ALU.add), [rt2.r], [qmb.r])
                for g in range(2):
                    pt = psT.next()
                    pv = bfv(pt)
                    for i in range(8):
                        tr(pv[:, i * 128:i * 128 + rows], qb[:, g * 8 + i, 0:128], identb.t[:rows, :rows], [qmb.r, identb.r], [pt.r], inc=(i == 7))
                    copy(K.alt(), qnst.t[:, g * 8:(g + 1) * 8, 0:rows], pv.rearrange("p (a b) -> p a b", b=128)[:, :, 0:rows], [pt.r], [qnst.r])
                for g in range(2):
                    pt = psT.next()
                    pv = bfv(pt)
                    for i in range(8):
                        tr(pv[0:64, i * 128:i * 128 + rows], qb[:, g * 8 + i, 128:192], identb.t[:rows, :rows], [qmb.r, identb.r], [pt.r], inc=(i == 7))
                    copy(K.alt(), qrst.t[:, g * 8:(g + 1) * 8, 0:rows], pv.rearrange("p (a b) -> p a b", b=128)[0:64, :, 0:rows], [pt.r], [qrst.r])
                n0_ = c0 + s * 128
                dma("sq", QN_T.rearrange("(h d) n -> d h n", d=128)[:, :, n0_:n0_ + rows], qnst.t[:, :, 0:rows], reads=[qnst.r])
                dma("sq", QR_T.rearrange("(h d) n -> d h n", d=64)[:, :, n0_:n0_ + rows], qrst.t[:, :, 0:rows], reads=[qrst.r])

        with K.scope():
            uT = K.sb([128, 32, 512], BF16)
            rows_ring = Ring([K.sb([128, 4096], F32) for _ in range(1)])
            ss = K.sb([128, 1], F32); rstd = K.sb([128, 1], F32)
            WL = WLoader(3)
            L = dict(st512=Ring([K.sb([128, 512], BF16) for _ in range(3)]), og=K.sb([128, 4, 2048], BF16),
                     cq32=K.sb([128, 8, 512], F32), cqnT=K.sb([128, 8, 512], BF16), rbc=K.sb([128, 512], F32),
                     sqb=Ring([K.sb([128, 512], BF16) for _ in range(3)]), qm32=K.sb([128, 3072], F32), qmb=K.sb([128, 3072], BF16),
                     rt2=K.sb([128, 4, 16, 32], F32), cso=K.sb([128, 64], F32), qnst=K.sb([128, 16, 128], BF16),
                     qrst=K.sb([64, 16, 128], BF16))
            junk = View(L["og"].t[:, 0:2, :].rearrange("p a b -> p (a b)"), L["og"].r)
            for ti in range(NTO):
                prep(x_own[ti * 512:(ti + 1) * 512, :], 512, A1, B1, [(0, 512, 0)], uT, rows_ring, junk, ss, rstd)
                q_side(uT, 512, ti * 512, WL, L)
            prep(x_smp, 64, A1, B1, sample_groups, uT, rows_ring, junk, ss, rstd)
            q_side(uT, 64, OWN, WL, L)
        if stop_after <= 4:
            K.finish()
            return nc

        NPG = (max(NBLK, (NKS + 127) // 128) + 7) // 8

        def pv_blocks(nk):
            return [(k0, min(128, nk - k0)) for k0 in range(0, nk, 128)]

        def transposes_and_pv(nq, Pb, nk, PT, PTres, po, ncols, vget):
            blks = pv_blocks(nk)
            nb = len(blks)
            for g0 in range(0, nb, 8):
                grp = blks[g0:g0 + 8]
                pt = psT.next()
                pv = bfv(pt)
                pr = PTres[g0 // 8]
                for i, (k0, w) in enumerate(grp):
                    tr(pv[:w, i * 128:i * 128 + nq], Pb.t[:nq, k0:k0 + w], identb.t[:nq, :nq], [Pb.r, identb.r], [pt.r],
                       inc=(i == len(grp) - 1))
                full = [i for i, (k0, w) in enumerate(grp) if w == 128]
                if full:
                    nf = len(full)
                    copy(K.alt(), PT.t[:, g0:g0 + nf, 0:nq], pv.rearrange("p (a b) -> p a b", b=128)[:, 0:nf, 0:nq], [pt.r], [pr])
                for i, (k0, w) in enumerate(grp):
                    if w < 128:
                        copy(K.alt(), PT.t[:w, g0 + i, 0:nq], pv[:w, i * 128:i * 128 + nq], [pt.r], [pr])
                for i, (k0, w) in enumerate(grp):
                    bi = g0 + i
                    v_ap, v_res = vget(bi, w)
                    op("pe", lambda: nc.tensor.matmul(po.t[:nq, 0:ncols], lhsT=PT.t[:w, bi, 0:nq], rhs=v_ap,
                                                      start=(bi == 0), stop=(bi == nb - 1)),
                       [pr] + v_res, [po.r], inc=(bi == nb - 1))

        def attn_tile(nq, qn, qr, qres, KTs, KRTs, Vs, tiles, Sb, Pb, PT, PTres, sm, out_ap, out_res):
            nk = sum(w for (_, w, _) in tiles)
            for (k0, w, bias) in tiles:
                ps = psA.next()
                mm(ps.t[:nq, 0:w], qn, KTs.t[:, k0:k0 + w], True, False, qres + [KTs.r], [ps.r])
                mm(ps.t[:nq, 0:w], qr, KRTs.t[0:64, k0:k0 + w], False, True, qres + [KRTs.r], [ps.r])
                if bias is None:
                    copy(K.alt(), Sb.t[:nq, k0:k0 + w], ps.t[:nq, 0:w], [ps.r], [Sb.r])
                else:
                    op("dve", lambda: nc.vector.tensor_tensor(out=Sb.t[:nq, k0:k0 + w], in0=ps.t[:nq, 0:w], in1=bias.t[:nq, 0:w],
                                                              op=ALU.add), [ps.r, bias.r], [Sb.r])
            mx, negm, rsum, rinv = sm
            op("dve", lambda: nc.vector.reduce_max(out=mx.t[:nq, :], in_=Sb.t[:nq, 0:nk], axis=AX.X), [Sb.r], [mx.r])
            op("dve", lambda: nc.vector.tensor_scalar(out=negm.t[:nq, :], in0=mx.t[:nq, :], scalar1=-MLA_SCALE, scalar2=None,
                                                      op0=ALU.mult), [mx.r], [negm.r])
            op("act", lambda: nc.scalar.activation(out=Pb.t[:nq, 0:nk], in_=Sb.t[:nq, 0:nk], func=AF.Exp, bias=negm.t[:nq, :],
                                                   scale=MLA_SCALE, accum_out=rsum.t[:nq, :]), [Sb.r, negm.r], [Pb.r, rsum.r])
            op("dve", lambda: nc.vector.reciprocal(out=rinv.t[:nq, :], in_=rsum.t[:nq, :]), [rsum.r], [rinv.r])
            po = psO.next()
            transposes_and_pv(nq, Pb, nk, PT, PTres, po, 128, lambda bi, w: (Vs.t[:w, bi, :], [Vs.r]))
            op("act", lambda: nc.scalar.activation(out=out_ap, in_=po.t[:nq, 0:128], func=AF.Copy, scale=rinv.t[:nq, :]),
               [po.r, rinv.r], [out_res])

        KMAX = max(SEQ, NKS)
        with K.scope():
            KRTs = K.sb([64, KMAX], BF16); KTs = K.sb([128, KMAX], BF16)
            Vs = K.sb([128, (KMAX + 127) // 128, 128], BF16)
            QN = K.sb([128, OWN], BF16); QR = K.sb([64, OWN], BF16)
            biasm = K.sb([128, 512], F32)
            Sbr = Ring([K.sb([128, KMAX], F32) for _ in range(2)])
            Pbr = Ring([K.sb([128, KMAX], BF16) for _ in range(2)])
            PT = K.sb([128, NPG * 8, 128], BF16)
            PTres = [Res() for _ in range(NPG)]
            Aacc = K.sb([128, NOB, 128], BF16)
            Asmp = K.sb([16, 16, 128], BF16)
            smr = Ring([tuple(K.sb([128, 1], F32) for _ in range(4)) for _ in range(2)])
            dma("sq", biasm.t[:], bias_mla_d, writes=[biasm.r])
            dma("sq", KRTs.t[:, 0:SEQ], KRT, writes=[KRTs.r])
            for h in range(16):
                dma("sq", KTs.t[:, 0:SEQ], KT_mla[h * 128:(h + 1) * 128, :], writes=[KTs.r])
                dma("sq", Vs.t[:, 0:NBLK, :], V_mla.rearrange("(b p) f -> p b f", p=128)[:, :, h * 128:(h + 1) * 128], writes=[Vs.r])
                dma("sq", QN.t[:], QN_T[h * 128:(h + 1) * 128, 0:OWN], writes=[QN.r])
                dma("sq", QR.t[:], QR_T[h * 64:(h + 1) * 64, 0:OWN], writes=[QR.r])
                for m in range(NOB):
                    tiles = [(kt * 512, 512, None) for kt in range(m)] + [(m * 512, 512, biasm)]
                    attn_tile(128, QN.t[:, m * 128:(m + 1) * 128], QR.t[:, m * 128:(m + 1) * 128], [QN.r, QR.r], KTs, KRTs, Vs, tiles,
                              Sbr.next(), Pbr.next(), PT, PTres, smr.next(), Aacc.t[:, m, :], Aacc.r)
                dma("sq", A_s[0:OWN, :].rearrange("(m p) f -> p m f", p=128)[:, :, 2048 + h * 128:2048 + (h + 1) * 128], Aacc.t[:],
                    reads=[Aacc.r])
            for sq_ in range(4):
                dma("sq", KRTs.t[:, 0:PAST], KRT_p[sq_], writes=[KRTs.r])
                dma("sq", KRTs.t[:, PAST:NKS], KRT_s[:, 16 * sq_:16 * sq_ + 16], writes=[KRTs.r])
                for h in range(16):
                    dma("sq", KTs.t[:, 0:PAST], KT_mla_p[sq_][h * 128:(h + 1) * 128, :], writes=[KTs.r])
                    dma("sq", KTs.t[:, PAST:NKS], KT_mla_s[h * 128:(h + 1) * 128, 16 * sq_:16 * sq_ + 16], writes=[KTs.r])
                    dma("sq", Vs.t[:, 0:PAST // 128, :], V_mla_p[sq_].rearrange("(b p) f -> p b f", p=128)[:, :, h * 128:(h + 1) * 128],
                        writes=[Vs.r])
                    dma("sq", Vs.t[0:16, PAST // 128, :], V_mla_s[16 * sq_:16 * sq_ + 16, h * 128:(h + 1) * 128], writes=[Vs.r])
                    dma("sq", QN.t[:, 0:16], QN_T[h * 128:(h + 1) * 128, OWN + 16 * sq_:OWN + 16 * sq_ + 16], writes=[QN.r])
                    dma("sq", QR.t[:, 0:16], QR_T[h * 64:(h + 1) * 64, OWN + 16 * sq_:OWN + 16 * sq_ + 16], writes=[QR.r])
                    tiles = [(i * 512, 512, None) for i in range(NPT)] + [(PAST, 16, None)]
                    attn_tile(16, QN.t[:, 0:16], QR.t[:, 0:16], [QN.r, QR.r], KTs, KRTs, Vs, tiles,
                              Sbr.next(), Pbr.next(), PT, PTres, smr.next(), Asmp.t[:, h, :], Asmp.r)
                dma("sq", A_s[OWN + 16 * sq_:OWN + 16 * sq_ + 16, 2048:4096], Asmp.t[:].rearrange("p h d -> p (h d)"), reads=[Asmp.r])
        if stop_after <= 5:
            K.finish()
            return nc

        with K.scope():
            maskm = K.sb([128, 512], F32); tril16 = K.sb([16, 16], F32)
            gml = K.sb([128, 2048], F32)
            KT2 = K.sb([128, 2, SEQ], BF16); QT2 = K.sb([128, 2, OWN], BF16)
            cbc = K.sb([128, SEQ], F32); Eb = K.sb([128, SEQ], F32)
            Pbr = Ring([K.sb([128, SEQ], BF16) for _ in range(1)])
            PT = K.sb([128, NPG * 8, 128], BF16)
            PTres = [Res() for _ in range(NPG)]
            vring = Ring([K.sb([128, 4, 512], BF16) for _ in range(2)])
            hbr = Ring([K.sb([128, 512], F32) for _ in range(2)])
            ogr = Ring([K.sb([128, 512], BF16) for _ in range(2)])
            astr = Ring([K.sb([128, 512], BF16) for _ in range(2)])
            junk5 = K.sb([128, 512], BF16)
            smr = Ring([tuple(K.sb([128, 1], F32) for _ in range(6)) for _ in range(2)])
            dma("sq", maskm.t[:], mask_ml_d, writes=[maskm.r])
            dma("sq", tril16.t[:], tril16_d, writes=[tril16.r])
            dma("sq", gml.t[:], g_ml_d.partition_broadcast(128), writes=[gml.r])

            def ml_tile2(nq, qT, qres, tiles, negM, eb, colres, vsrc, h, init, og_src, a_dst):
                nk = sum(w for (_, w, _) in tiles)
                Pb = Pbr.next()
                nqv, an, den, rden, ssq, rs = smr.next()
                lastk0 = tiles[-1][0]
                if lastk0 > 0:
                    op("act", lambda: nc.scalar.activation(out=Eb.t[:nq, 0:lastk0], in_=cbc.t[:nq, 0:lastk0], func=AF.Exp, bias=negM,
                                                           scale=1.0), [cbc.r] + colres, [Eb.r])
                op("dve", lambda: nc.vector.tensor_scalar(out=Eb.t[:nq, lastk0:nk], in0=cbc.t[:nq, lastk0:nk], scalar1=negM, scalar2=0.0,
                                                          op0=ALU.add, op1=ALU.min), [cbc.r] + colres, [Eb.r])
                op("act", lambda: nc.scalar.activation(out=Eb.t[:nq, lastk0:nk], in_=Eb.t[:nq, lastk0:nk], func=AF.Exp), [Eb.r], [Eb.r])
                for (k0, w, mask) in tiles:
                    ps = psA.next()
                    for c in range(2):
                        mm(ps.t[:nq, 0:w], qT[:, c, :], KT2.t[:, c, k0:k0 + w], c == 0, c == 1, qres + [KT2.r], [ps.r])
                    op("dve", lambda: nc.vector.tensor_tensor(out=Pb.t[:nq, k0:k0 + w], in0=ps.t[:nq, 0:w], in1=Eb.t[:nq, k0:k0 + w],
                                                              op=ALU.mult), [ps.r, Eb.r], [Pb.r])
                    if mask is not None:
                        op("dve", lambda: nc.vector.tensor_tensor(out=Pb.t[:nq, k0:k0 + w], in0=Pb.t[:nq, k0:k0 + w],
                                                                  in1=mask.t[:nq, 0:w], op=ALU.mult), [Pb.r, mask.r], [Pb.r])
                op("dve", lambda: nc.vector.reduce_sum(out=nqv.t[:nq, :], in_=Pb.t[:nq, 0:nk], axis=AX.X), [Pb.r], [nqv.r])
                po = psO.next()
                nb = (nk + 127) // 128
                for ti_, (k0, w, _) in enumerate(tiles):
                    vb = vring.next()
                    if w % 128 == 0:
                        dma("sq", vb.t[:, 0:w // 128, :], vsrc(k0, w).rearrange("(b p) f -> p b f", p=128), writes=[vb.r])
                    else:
                        dma("sq", vb.t[:w, 0, :], vsrc(k0, w), writes=[vb.r])
                    tb = pv_blocks(w)
                    pt = psT.next()
                    pv = bfv(pt)
                    pr = PTres[(k0 // 512) % NPG]
                    for i, (b0, bw) in enumerate(tb):
                        tr(pv[:bw, i * 128:i * 128 + nq], Pb.t[:nq, k0 + b0:k0 + b0 + bw], identb.t[:nq, :nq], [Pb.r, identb.r], [pt.r],
                           inc=(i == len(tb) - 1))
                    g0 = (k0 // 128)
                    if all(bw == 128 for (_, bw) in tb):
                        copy(K.alt(), PT.t[:, g0:g0 + len(tb), 0:nq], pv.rearrange("p (a b) -> p a b", b=128)[:, 0:len(tb), 0:nq], [pt.r], [pr])
                    else:
                        for i, (b0, bw) in enumerate(tb):
                            copy(K.alt(), PT.t[:bw, g0 + i, 0:nq], pv[:bw, i * 128:i * 128 + nq], [pt.r], [pr])
                    for i, (b0, bw) in enumerate(tb):
                        bi = g0 + i
                        op("pe", lambda: nc.tensor.matmul(po.t[:nq, 0:512], lhsT=PT.t[:bw, bi, 0:nq], rhs=vb.t[:bw, i, :],
                                                          start=(bi == 0), stop=(bi == nb - 1)),
                           [pr, vb.r], [po.r], inc=(i == len(tb) - 1))
                if init is not None:
                    op("dve", lambda: nc.vector.tensor_tensor(out=nqv.t[:nq, :], in0=nqv.t[:nq, :], in1=init[1].t[:nq, :], op=ALU.add),
                       [nqv.r, init[1].r], [nqv.r])
                op("dve", lambda: nc.vector.tensor_scalar(out=an.t[:nq, :], in0=nqv.t[:nq, :], scalar1=-1.0, scalar2=None, op0=ALU.mult),
                   [nqv.r], [an.r])
                op("dve", lambda: nc.vector.tensor_tensor(out=an.t[:nq, :], in0=an.t[:nq, :], in1=nqv.t[:nq, :], op=ALU.max),
                   [nqv.r, an.r], [an.r])
                op("dve", lambda: nc.vector.tensor_tensor(out=den.t[:nq, :], in0=an.t[:nq, :], in1=eb, op=ALU.max), [an.r] + colres, [den.r])
                op("dve", lambda: nc.vector.reciprocal(out=rden.t[:nq, :], in_=den.t[:nq, :]), [den.r], [rden.r])
                hb = hbr.next()
                if init is None:
                    op("act", lambda: nc.scalar.activation(out=hb.t[:nq, :], in_=po.t[:nq, 0:512], func=AF.Copy, scale=rden.t[:nq, :]),
                       [po.r, rden.r], [hb.r])
                else:
                    op("dve", lambda: nc.vector.tensor_tensor(out=hb.t[:nq, :], in0=po.t[:nq, 0:512], in1=init[0].t[:nq, :], op=ALU.add),
                       [po.r, init[0].r], [hb.r])
                    op("act", lambda: nc.scalar.activation(out=hb.t[:nq, :], in_=hb.t[:nq, :], func=AF.Copy, scale=rden.t[:nq, :]),
                       [hb.r, rden.r], [hb.r])
                op("act", lambda: nc.scalar.activation(out=junk5.t[:nq, :], in_=hb.t[:nq, :], func=AF.Square, accum_out=ssq.t[:nq, :]),
                   [hb.r], [junk5.r, ssq.r])
                rstd_from_ss(rs.t[:nq, :], ssq.t[:nq, :], 1.0 / 512, [ssq.r], [rs.r])
                og = ogr.next()
                dma("sq", og.t[:nq, :], og_src, writes=[og.r])
                op("dve", lambda: nc.vector.scalar_tensor_tensor(out=hb.t[:nq, :], in0=hb.t[:nq, :], scalar=rs.t[:nq, :],
                                                                 in1=gml.t[:nq, h * 512:(h + 1) * 512], op0=ALU.mult, op1=ALU.mult),
                   [hb.r, rs.r, gml.r], [hb.r])
                ast = astr.next()
                op("dve", lambda: nc.vector.tensor_tensor(out=ast.t[:nq, :], in0=hb.t[:nq, :], in1=og.t[:nq, :], op=ALU.mult),
                   [hb.r, og.r], [ast.r])
                dma("sq", a_dst, ast.t[:nq, :], reads=[ast.r])

            for h in range(4):
                dma("sq", KT2.t[:], KT_ml[h * 256:(h + 1) * 256, :].rearrange("(c p) n -> p c n", p=128), writes=[KT2.r])
                dma("sq", QT2.t[:], QT_ml[h * 256:(h + 1) * 256, 0:OWN].rearrange("(c p) n -> p c n", p=128), writes=[QT2.r])
                dma("sq", cbc.t[:], ROWS[0, h, :].partition_broadcast(128), writes=[cbc.r])
                for m in range(NOB):
                    tiles = [(kt * 512, 512, None) for kt in range(m)] + [(m * 512, 512, maskm)]
                    ml_tile2(128, QT2.t[:, :, m * 128:(m + 1) * 128], [QT2.r], tiles, negMown.t[:, h, m:m + 1], ebown.t[:, h, m:m + 1],
                             [negMown.r, ebown.r], lambda k0, w: V_ml[k0:k0 + w, h * 512:(h + 1) * 512], h, None,
                             OG[m * 128:(m + 1) * 128, h * 512:(h + 1) * 512], A_s[m * 128:(m + 1) * 128, h * 512:(h + 1) * 512])
            colr = Ring([tuple(K.sb([128, 1], F32) for _ in range(8)) for _ in range(2)])
            C0f = Ring([K.sb([128, 2, 512], F32) for _ in range(1)]); C0b = K.sb([128, 2, 512], BF16)
            n0f = K.sb([128, 2], F32); n0b = K.sb([128, 2], BF16); n0row = K.sb([1, 256], F32)
            inter = Ring([K.sb([16, 512], F32) for _ in range(2)])
            kb16 = Ring([K.sb([16, 256], BF16) for _ in range(2)]); vb16 = Ring([K.sb([16, 512], BF16) for _ in range(2)])
            kg16 = Ring([K.sb([16, 256], BF16) for _ in range(2)]); wtb = K.sb([16, 1], BF16)
            cstr = Ring([K.sb([128, 512], F32) for _ in range(2)]); nst = K.sb([1, 256], F32)
            for sq_ in range(4):
                t0_ = 16 * sq_
                for h in range(4):
                    sh = sq_ * 4 + h
                    Mc, bc_, nM, ebc, m0b, wint, cc_, mlb16 = colr.next()
                    colres = [Mc.r]
                    dma("sq", KT2.t[:, :, 0:16], KT_ml_s[h * 256:(h + 1) * 256, t0_:t0_ + 16].rearrange("(c p) n -> p c n", p=128), writes=[KT2.r])
                    dma("sq", QT2.t[:, :, 0:16], QT_ml[h * 256:(h + 1) * 256, OWN + t0_:OWN + t0_ + 16].rearrange("(c p) n -> p c n", p=128),
                        writes=[QT2.r])
                    dma("sq", cbc.t[0:16, 0:16], ROWS_s[0, h, t0_:t0_ + 16].partition_broadcast(16), writes=[cbc.r])
                    dma("sq", Mc.t[0:16, :], ROWS_s[2, h, t0_:t0_ + 16].rearrange("(t o) -> t o", o=1), writes=[Mc.r])
                    dma("sq", bc_.t[0:16, :], ROWS_s[1, h, t0_:t0_ + 16].rearrange("(t o) -> t o", o=1), writes=[Mc.r])
                    dma("sq", cc_.t[0:16, :], ROWS_s[0, h, t0_:t0_ + 16].rearrange("(t o) -> t o", o=1), writes=[Mc.r])
                    dma("sq", m0b.t[:, :], m0_in[h, sq_:sq_ + 1].partition_broadcast(128), writes=[Mc.r])
                    dma("sq", mlb16.t[:, :], ROWS_s[2, h, t0_ + 15:t0_ + 16].partition_broadcast(128), writes=[Mc.r])
                    op("dve", lambda: nc.vector.tensor_scalar(out=nM.t[0:16, :], in0=Mc.t[0:16, :], scalar1=-1.0, scalar2=None, op0=ALU.mult),
                       colres, colres)
                    op("dve", lambda: nc.vector.tensor_tensor(out=ebc.t[0:16, :], in0=bc_.t[0:16, :], in1=Mc.t[0:16, :], op=ALU.add), colres, colres)
                    op("act", lambda: nc.scalar.activation(out=ebc.t[0:16, :], in_=ebc.t[0:16, :], func=AF.Exp, scale=-1.0), colres, colres)
                    op("dve", lambda: nc.vector.tensor_tensor(out=wint.t[0:16, :], in0=m0b.t[0:16, :], in1=Mc.t[0:16, :], op=ALU.subtract),
                       colres, colres)
                    op("act", lambda: nc.scalar.activation(out=wint.t[0:16, :], in_=wint.t[0:16, :], func=AF.Exp), colres, colres)
                    c0f = C0f.next()
                    dma("sq", c0f.t[:], C0_in[sh].rearrange("(c p) f -> p c f", p=128), writes=[c0f.r])
                    copy("dve", C0b.t[:], c0f.t[:], [c0f.r], [C0b.r])
                    for c_ in range(2):
                        dma("sq", n0f.t[:, c_:c_ + 1], n0_in[sh, c_ * 128:(c_ + 1) * 128].rearrange("(p o) -> p o", o=1), writes=[n0f.r])
                    copy("dve", n0b.t[:], n0f.t[:], [n0f.r], [n0b.r])
                    dma("sq", n0row.t[:], n0_in[sh:sh + 1, :], writes=[n0row.r])
                    ps = psA.next()
                    for c in range(2):
                        mm(ps.t[0:16, 0:512], QT2.t[:, c, 0:16], C0b.t[:, c, :], c == 0, c == 1, [QT2.r, C0b.r], [ps.r])
                    it = inter.next()
                    op("act", lambda: nc.scalar.activation(out=it.t[:, :], in_=ps.t[0:16, 0:512], func=AF.Copy, scale=wint.t[0:16, :]),
                       [ps.r] + colres, [it.r])
                    ps2 = psA.next()
                    for c in range(2):
                        mm(ps2.t[0:16, 0:1], QT2.t[:, c, 0:16], n0b.t[:, c:c + 1], c == 0, c == 1, [QT2.r, n0b.r], [ps2.r])
                    qn0w = Buf(None); qn0w.t = bc_.t
                    op("dve", lambda: nc.vector.tensor_tensor(out=bc_.t[0:16, :], in0=ps2.t[0:16, 0:1], in1=wint.t[0:16, :], op=ALU.mult),
                       [ps2.r] + colres, colres)
                    qn0w.r = Mc.r
                    ml_tile2(16, QT2.t[:, :, 0:16], [QT2.r], [(0, 16, tril16)], nM.t[0:16, :], ebc.t[0:16, :], colres,
                             lambda k0, w: V_ml_s[t0_:t0_ + 16, h * 512:(h + 1) * 512], h, (it, qn0w),
                             OG[OWN + t0_:OWN + t0_ + 16, h * 512:(h + 1) * 512], A_s[OWN + t0_:OWN + t0_ + 16, h * 512:(h + 1) * 512])
                    kb, vb, kg = kb16.next(), vb16.next(), kg16.next()
                    dma("sq", kb.t[:], K_tm_s[t0_:t0_ + 16, h * 256:(h + 1) * 256], writes=[kb.r])
                    dma("sq", vb.t[:], V_ml_s[t0_:t0_ + 16, h * 512:(h + 1) * 512], writes=[vb.r])
                    op("dve", lambda: nc.vector.tensor_tensor(out=cc_.t[0:16, :], in0=cc_.t[0:16, :], in1=mlb16.t[0:16, :], op=ALU.subtract),
                       colres, colres)
                    op("act", lambda: nc.scalar.activation(out=cc_.t[0:16, :], in_=cc_.t[0:16, :], func=AF.Exp), colres, colres)
                    op("dve", lambda: nc.vector.tensor_tensor(out=m0b.t[:, :], in0=m0b.t[:, :], in1=mlb16.t[:, :], op=ALU.subtract), colres, colres)
                    op("act", lambda: nc.scalar.activation(out=m0b.t[:, :], in_=m0b.t[:, :], func=AF.Exp), colres, colres)
                    copy("dve", wtb.t[:], cc_.t[0:16, :], colres, [wtb.r])
                    op("dve", lambda: nc.vector.tensor_scalar(out=kg.t[:], in0=kb.t[:], scalar1=cc_.t[0:16, :], scalar2=None, op0=ALU.mult),
                       [kb.r] + colres, [kg.r])
                    for dc in range(2):
                        ps = psA.next()
                        mm(ps.t[:, 0:512], kg.t[:, dc * 128:(dc + 1) * 128], vb.t[:], True, True, [kg.r, vb.r], [ps.r])
                        cs_ = cstr.next()
                        op("dve", lambda: nc.vector.scalar_tensor_tensor(out=cs_.t[:], in0=c0f.t[:, dc, :], scalar=m0b.t[:, :], in1=ps.t[:, 0:512],
                                                                         op0=ALU.mult, op1=ALU.add), [c0f.r, ps.r] + colres, [cs_.r])
                        dma("sq", s_C[sh, dc * 128:(dc + 1) * 128, :], cs_.t[:], reads=[cs_.r])
                    ps = psA.next()
                    mm(ps.t[0:1, 0:256], wtb.t[:, :], kb.t[:], True, True, [wtb.r, kb.r], [ps.r])
                    op("dve", lambda: nc.vector.scalar_tensor_tensor(out=nst.t[:], in0=n0row.t[:], scalar=m0b.t[0:1, :], in1=ps.t[0:1, 0:256],
                                                                     op0=ALU.mult, op1=ALU.add), [n0row.r, ps.r] + colres, [nst.r])
                    dma("sq", s_n[sh:sh + 1, :], nst.t[:], reads=[nst.r])
        if stop_after <= 6:
            K.finish()
            return nc

        tiles_own = [(ti * 512, 512, [(0, 512, 0)]) for ti in range(NTO)] + [(OWN, 64, sample_groups)]

        with K.scope():
            aT = K.sb([128, 32, 512], BF16)
            yT = K.sb([128, 32, 512], F32)
            rowsb = Ring([K.sb([128, 4096], BF16) for _ in range(1)])
            rows_ring = Ring([K.sb([128, 4096], F32) for _ in range(1)])
            sq_ring = Ring([K.sb([128, 512], BF16) for _ in range(3)])
            rbc = K.sb([128, 512], F32)
            WL = WLoader(3)
            for (c0, T, groups) in tiles_own:
                prep_plain(A_s[c0:c0 + T, :], T, aT, rowsb)
                for cb in range(16):
                    wb, wv = WL.load(f"wo{cb}")
                    for cc in range(2):
                        ch = cb * 2 + cc
                        ps = psA.next()
                        for k in range(32):
                            mm(ps.t[:, 0:T], wv[:, k, cc * 128:(cc + 1) * 128], aT.t[:, k, 0:T], k == 0, k == 31, [wb.r, aT.r], [ps.r])
                        copy(K.alt(), yT.t[:, ch, 0:T], ps.t[:, 0:T], [ps.r], [yT.r])
                src = x_own[c0:c0 + T, :] if c0 < OWN else x_smp
                post(yT, T, G1, groups, src, X1[c0:c0 + T, :], rows_ring, sq_ring, rbc)
        if stop_after <= 7:
            K.finish()
            return nc

        with K.scope():
            uT = K.sb([128, 32, 512], BF16)
            facc = K.sb([128, 32, 512], F32)
            rows_ring = Ring([K.sb([128, 4096], F32) for _ in range(1)])
            sq_ring = Ring([K.sb([128, 512], BF16) for _ in range(2)])
            rbc = K.sb([128, 512], F32)
            ss = K.sb([128, 1], F32); rstd = K.sb([128, 1], F32)
            r32 = Ring([K.sb([128, 512], F32) for _ in range(2)])
            hTr = Ring([K.sb([128, 2, 512], BF16) for _ in range(2)])
            WL = WLoader(4)
            jk = View(facc.t[:, 0:4, :].rearrange("p a b -> p (a b)").bitcast(BF16), facc.r)
            for (c0, T, groups) in tiles_own:
                prep(X1[c0:c0 + T, :], T, A2, B2, groups, uT, rows_ring, jk, ss, rstd)
                for hb in range(NHB):
                    w1b, w1 = WL.load(f"f1_{hb}")
                    w2b, w2 = WL.load(f"f2_{hb}")
                    hT = hTr.next()
                    for cc in range(2):
                        ps = psA.next()
                        for k in range(32):
                            mm(ps.t[:, 0:T], w1[:, k, cc * 128:(cc + 1) * 128], uT.t[:, k, 0:T], k == 0, k == 31, [w1b.r, uT.r], [ps.r])
                        r_ = r32.next()
                        op("act", lambda: nc.scalar.activation(out=r_.t[:, 0:T], in_=ps.t[:, 0:T], func=AF.Relu), [ps.r], [r_.r])
                        op("dve", lambda: nc.vector.tensor_tensor(out=hT.t[:, cc, 0:T], in0=r_.t[:, 0:T], in1=r_.t[:, 0:T], op=ALU.mult),
                           [r_.r], [hT.r])
                    for oc in range(32):
                        ps = psA.next()
                        for k in range(2):
                            mm(ps.t[:, 0:T], w2[:, k, oc * 128:(oc + 1) * 128], hT.t[:, k, 0:T], k == 0, k == 1, [w2b.r, hT.r], [ps.r])
                        f_ = facc.t[:, oc, 0:T]
                        if hb == 0:
                            copy(K.alt(), f_, ps.t[:, 0:T], [ps.r], [facc.r])
                        else:
                            op("dve", lambda: nc.vector.tensor_tensor(out=f_, in0=ps.t[:, 0:T], in1=f_, op=ALU.add), [ps.r, facc.r], [facc.r])
                dst = y_own[c0:c0 + T, :] if c0 < OWN else y_smp
                post(facc, T, G2, groups, X1[c0:c0 + T, :], dst, rows_ring, sq_ring, rbc)
        K.finish()
    return nc


def _fm(v, n):
    return np.ascontiguousarray(np.asarray(v, np.float32).reshape(n, 128).T)


def host_inputs(inp, SEQ, PAST):
    f = lambda a: np.ascontiguousarray(np.asarray(a, np.float32))
    NBLK, OWN = SEQ // 128, SEQ // 4
    NOB, NQ = OWN // 128, OWN // 4 * 0 + OWN + 64
    inv = (10000.0 ** (-np.arange(32, dtype=np.float32) / 32)).astype(np.float32)

    def cs_table(pos):
        ang = pos.astype(np.float32)[:, None] * inv[None, :]
        return np.concatenate([np.cos(ang), np.sin(ang)], axis=1).astype(np.float32)

    shared = dict(
        w_ada=f(inp["w_ada"][0]), b_ada_fm=_fm(inp["b_ada"][0], 192),
        g_pre1_fm=_fm(inp["g_pre1"][0], 32), g_post1_fm=_fm(inp["g_post1"][0], 32),
        g_pre2_fm=_fm(inp["g_pre2"][0], 32), g_post2_fm=_fm(inp["g_post2"][0], 32),
        w_in=f(inp["w_in"][0]), b_ig=f(inp["b_ig"][0]).reshape(4, 1), b_fg=f(inp["b_fg"][0]).reshape(4, 1),
        g_mlnorm=f(inp["g_mlnorm"][0]).reshape(1, 2048), g_cq_fm=_fm(inp["g_cq"][0], 8),
        w_uq=f(inp["w_uq"][0]).reshape(1024, 3072), g_ckv=f(inp["g_ckv"][0]).reshape(1, 512),
        w_uk=f(inp["w_uk"][0]).reshape(512, 2048), w_uv=f(inp["w_uv"][0]).reshape(512, 2048),
        w_out=f(inp["w_out"][0]), w_ff1=f(inp["w_ff1"][0]), w_ff2=f(inp["w_ff2"][0]),
        ident=np.eye(128, dtype=np.float32), cs_all=cs_table(np.arange(SEQ)),
        tril16=np.tril(np.ones((16, 16), np.float32)),
    )
    xp, xs = f(inp["x_prompt"]), f(inp["x_sample"])
    maps = []
    kk = np.arange(512)[None, :]
    for c in range(8):
        b, j = c // 4, c % 4
        qq = (j * 128 + np.arange(128))[:, None]
        c5 = np.concatenate([f(inp["c_prompt"])[b:b + 1], f(inp["c_sample"])[4 * c:4 * c + 4]], axis=0)
        own_pos = (np.arange(NOB)[:, None] * 4 + j) * 128 + np.arange(128)[None, :]
        sel = np.zeros((NBLK, NOB), np.float32)
        sel[np.arange(NOB) * 4 + j, np.arange(NOB)] = 1.0
        m = dict(shared)
        m.update(
            x_all=xp[b], x_own=np.ascontiguousarray(xp[b].reshape(NBLK, 128, 4096)[j::4].reshape(OWN, 4096)),
            x_smp=np.ascontiguousarray(xs[4 * c:4 * c + 4].reshape(64, 4096)),
            c_T=np.ascontiguousarray(c5.T.reshape(32, 128, 5).transpose(1, 0, 2)),
            ckv_past=f(inp["cache_mla_ckv"][0, 4 * c:4 * c + 4]), kr_past=f(inp["cache_mla_krope"][0, 4 * c:4 * c + 4]),
            C0=f(inp["state_mlstm_C"][0, 4 * c:4 * c + 4]).reshape(16, 256, 512),
            n0=f(inp["state_mlstm_n"][0, 4 * c:4 * c + 4]).reshape(16, 256),
            m0T=np.ascontiguousarray(f(inp["state_mlstm_m"][0, 4 * c:4 * c + 4]).reshape(4, 4).T),
            cs_own=np.concatenate([cs_table(own_pos.reshape(-1)), cs_table(PAST + np.tile(np.arange(16), 4))], axis=0),
            bias_mla=np.where(kk // 64 <= qq // 64, 0.0, -1e9).astype(np.float32),
            mask_ml=(kk <= qq).astype(np.float32),
            sel=sel,
        )
        maps.append(m)
    return maps


def assemble(results, SEQ, PAST, BATCH=2):
    NBLK, OWN = SEQ // 128, SEQ // 4
    NOB = OWN // 128
    y_p = np.zeros((BATCH, SEQ, 4096), np.float32)
    y_s = np.zeros((32, 16, 4096), np.float32)
    p_ckv = np.zeros((1, BATCH, SEQ, 512), np.float32); p_kr = np.zeros((1, BATCH, SEQ, 64), np.float32)
    p_C = np.zeros((1, BATCH, 4, 256, 512), np.float32); p_n = np.zeros((1, BATCH, 4, 256), np.float32)
    p_m = np.zeros((1, BATCH, 4), np.float32)
    s_ckv = np.zeros((1, 32, 16, 512), np.float32); s_kr = np.zeros((1, 32, 16, 64), np.float32)
    s_C = np.zeros((1, 32, 4, 256, 512), np.float32); s_n = np.zeros((1, 32, 4, 256), np.float32)
    s_m = np.zeros((1, 32, 4), np.float32)
    for c, r in enumerate(results):
        b, j = c // 4, c % 4
        y_p[b].reshape(NBLK, 128, 4096)[j::4] = np.asarray(r["y_own"], np.float32).reshape(NOB, 128, 4096)
        y_s[4 * c:4 * c + 4] = np.asarray(r["y_smp"], np.float32).reshape(4, 16, 4096)
        if j == 0:
            p_ckv[0, b] = r["ckv_o"]; p_kr[0, b] = r["kr_o"]; p_C[0, b] = r["C_o"]; p_n[0, b] = r["n_o"]
            p_m[0, b] = np.asarray(r["m_o"]).reshape(4)
        s_ckv[0, 4 * c:4 * c + 4] = np.asarray(r["s_ckv"]).reshape(4, 16, 512)
        s_kr[0, 4 * c:4 * c + 4] = np.asarray(r["s_kr"]).reshape(4, 16, 64)
        s_C[0, 4 * c:4 * c + 4] = np.asarray(r["s_C"]).reshape(4, 4, 256, 512)
        s_n[0, 4 * c:4 * c + 4] = np.asarray(r["s_n"]).reshape(4, 4, 256)
        s_m[0, 4 * c:4 * c + 4] = np.asarray(r["s_m"]).reshape(4, 4).T
    return (y_p, y_s, p_ckv, p_kr, p_C, p_n, p_m, s_ckv, s_kr, s_C, s_n, s_m)


def kernel(**inputs):
    inp = {k: np.asarray(v) for k, v in inputs.items()}
    SEQ = inp["x_prompt"].shape[1]
    PAST = inp["cache_mla_ckv"].shape[2]
    nc = build_program(SEQ=SEQ, PAST=PAST)
    maps = host_inputs(inp, SEQ, PAST)
    res = run_bass_kernel_spmd(nc, maps, core_ids=list(range(8)))
    return assemble(res.results, SEQ, PAST, BATCH=inp["x_prompt"].shape[0])
```

```python
import contextlib
import numpy as np
import concourse.bass as bass
import concourse.mybir as mybir
from concourse.bass_utils import run_bass_kernel_spmd

F32 = mybir.dt.float32
BF16 = mybir.dt.bfloat16
ALU = mybir.AluOpType
AF = mybir.ActivationFunctionType
AX = mybir.AxisListType
EPS = 1e-6
MLA_SCALE = 192.0 ** -0.5


class Res:
    __slots__ = ("name", "w", "r")

    def __init__(self, name=None):
        self.name = name
        self.w = None
        self.r = []


class Buf:
    __slots__ = ("t", "r")

    def __init__(self, t):
        self.t = t
        self.r = Res()


class View:
    __slots__ = ("t", "r")

    def __init__(self, ap, res):
        self.t = ap
        self.r = res


class Ring:
    def __init__(self, bufs):
        self.b = bufs
        self.i = 0

    def next(self):
        b = self.b[self.i % len(self.b)]
        self.i += 1
        return b


class _Eng:
    W = 24000

    def __init__(self, K, name, handle):
        self.K, self.name, self.h = K, name, handle
        self.count = 0
        self.sems = []
        self.seen = {}

    def sem_for(self, idx):
        w = (idx - 1) // self.W
        while len(self.sems) <= w:
            self.sems.append(self.K.new_sem(f"{self.name}{len(self.sems)}"))
        return self.sems[w], (idx - 1) % self.W + 1

    def key(self, idx):
        return (self.name, (idx - 1) // self.W)


class _Dma(_Eng):
    NS = 8

    def __init__(self, K, name, handle):
        super().__init__(K, name, handle)
        self.slots = [K.new_sem(f"{name}s{i}") for i in range(self.NS)]

    def sem_for(self, idx):
        i = idx - 1
        return self.slots[i % self.NS], 16 * (i // self.NS + 1)

    def key(self, idx):
        return (self.name, (idx - 1) % self.NS)


class Kern:
    def __init__(self, nc, stack):
        self.nc = nc
        self.gstack = stack
        self.stacks = [stack]
        self.e = {
            "pe": _Eng(self, "pe", nc.tensor),
            "act": _Eng(self, "act", nc.scalar),
            "dve": _Eng(self, "dve", nc.vector),
            "pool": _Eng(self, "pool", nc.gpsimd),
            "sq": _Dma(self, "sq", nc.sync),
            "gq": _Dma(self, "gq", nc.gpsimd),
        }
        self._alt = 0
        self.nsb = 0

    def new_sem(self, name):
        return self.gstack.enter_context(self.nc.semaphore(name))

    def sb(self, shape, dt, name=None):
        self.nsb += 1
        t = self.stacks[-1].enter_context(self.nc.sbuf_tensor(f"t{self.nsb}_{name or 'x'}", list(shape), dt))
        return Buf(t)

    def ps(self, shape, dt, name):
        return Buf(self.gstack.enter_context(self.nc.psum_tensor(name, list(shape), dt)))

    def alt(self):
        self._alt ^= 1
        return "act" if self._alt else "dve"

    @contextlib.contextmanager
    def scope(self):
        st = contextlib.ExitStack()
        self.stacks.append(st)
        try:
            yield
            self.barrier()
        finally:
            self.stacks.pop()
            st.close()

    def _wait(self, E, F, n):
        if E is F and E.name == "pe":
            return
        k = F.key(n)
        if E.seen.get(k, 0) >= n:
            return
        sem, val = F.sem_for(n)
        E.h.wait_ge(sem, val)
        E.seen[k] = n

    def op(self, en, fn, reads=(), writes=(), inc=True):
        E = self.e[en]
        idx = E.count + 1
        isd = isinstance(E, _Dma)
        for r in reads:
            if r.w is not None:
                F, n = r.w
                if not (F is E and n >= idx):
                    self._wait(E, F, n)
        for r in writes:
            if r.w is not None:
                F, n = r.w
                if not (F is E and n >= idx):
                    self._wait(E, F, n)
            for (F, n) in r.r:
                if not (F is E and n >= idx):
                    self._wait(E, F, n)
        if isd and idx > E.NS:
            self._wait(E, E, idx - E.NS)
        ins = fn()
        if inc or isd:
            E.count = idx
            sem, _ = E.sem_for(idx)
            ins.then_inc(sem, 16 if isd else 1)
        for r in writes:
            r.w = (E, idx)
            r.r = []
        for r in reads:
            r.r.append((E, idx))
        return ins

    def barrier(self, with_gq=False):
        for en in ("pe", "act", "dve", "sq"):
            E = self.e[en]
            for fn_ in ("pe", "act", "dve", "pool"):
                F = self.e[fn_]
                if F.count and F is not E:
                    self._wait(E, F, F.count)
            for qn in (("sq", "gq") if with_gq else ("sq",)):
                Q = self.e[qn]
                for i in range(max(0, Q.count - Q.NS), Q.count):
                    self._wait(E, Q, i + 1)

    def finish(self):
        self.barrier(with_gq=True)
        self.nc.kcounts = {k: e.count for k, e in self.e.items()}


C_Q, C_K, C_V, C_O, C_IG, C_FG, C_CQ, C_CKV, C_KR = 0, 1024, 2048, 4096, 6144, 6148, 6152, 7176, 7688


def build_program(SEQ=8192, PAST=1024, debug=False, stop_after=99, NHB=64):
    nc = bass.Bass("TRN2", target_bir_lowering=False)
    NBLK, NTA, OWN = SEQ // 128, SEQ // 512, SEQ // 4
    NOB, NTO, NQ = OWN // 128, OWN // 512, OWN + 64
    NKS = PAST + 16
    NPT = PAST // 512

    def din(name, shape, dt=F32):
        return nc.dram_tensor(name, list(shape), dt, kind="ExternalInput").ap()

    def dout(name, shape, dt=F32):
        return nc.dram_tensor(name, list(shape), dt, kind="ExternalOutput").ap()

    def dscr(name, shape, dt):
        return nc.dram_tensor(name, list(shape), dt, kind="ExternalOutput" if debug else "Internal").ap()

    x_all = din("x_all", [SEQ, 4096]); x_own = din("x_own", [OWN, 4096]); x_smp = din("x_smp", [64, 4096])
    c_T = din("c_T", [128, 32, 5])
    ckv_past = din("ckv_past", [4, PAST, 512]); kr_past = din("kr_past", [4, PAST, 64])
    C0_in = din("C0", [16, 256, 512]); n0_in = din("n0", [16, 256]); m0_in = din("m0T", [4, 4])
    w_ada = din("w_ada", [4096, 24576]); b_ada_fm = din("b_ada_fm", [128, 192])
    gpre1_d = din("g_pre1_fm", [128, 32]); gpost1_d = din("g_post1_fm", [128, 32])
    gpre2_d = din("g_pre2_fm", [128, 32]); gpost2_d = din("g_post2_fm", [128, 32])
    w_in = din("w_in", [4096, 7752]); b_ig_d = din("b_ig", [4, 1]); b_fg_d = din("b_fg", [4, 1])
    g_ml_d = din("g_mlnorm", [1, 2048]); g_cq_d = din("g_cq_fm", [128, 8]); w_uq = din("w_uq", [1024, 3072])
    g_ckv_d = din("g_ckv", [1, 512]); w_uk = din("w_uk", [512, 2048]); w_uv = din("w_uv", [512, 2048])
    w_out = din("w_out", [4096, 4096]); w_ff1 = din("w_ff1", [4096, 16384]); w_ff2 = din("w_ff2", [16384, 4096])
    ident_d = din("ident", [128, 128]); cs_all = din("cs_all", [SEQ, 64]); cs_own = din("cs_own", [NQ, 64])
    bias_mla_d = din("bias_mla", [128, 512]); mask_ml_d = din("mask_ml", [128, 512])
    sel_d = din("sel", [NBLK, NOB]); tril16_d = din("tril16", [16, 16])
    y_own = dout("y_own", [OWN, 4096]); y_smp = dout("y_smp", [64, 4096])
    ckv_o = dout("ckv_o", [SEQ, 512]); kr_o = dout("kr_o", [SEQ, 64])
    C_o = dout("C_o", [4, 256, 512]); n_o = dout("n_o", [4, 256]); m_o = dout("m_o", [4, 1])
    s_ckv = dout("s_ckv", [64, 512]); s_kr = dout("s_kr", [64, 64])
    s_C = dout("s_C", [16, 256, 512]); s_n = dout("s_n", [16, 256]); s_m = dout("s_m", [4, 4])
    NWB = 19 + 19 + 16 + 128
    WB_A = dscr("WB_A", [54, 128, 8192], BF16)
    WB_F1 = dscr("WB_F1", [64, 128, 8192], BF16)
    WB_F2 = dscr("WB_F2", [64, 128, 8192], BF16)

    class _WB:
        def __getitem__(self, i):
            if i < 54:
                return WB_A[i]
            j = i - 54
            return WB_F1[j // 2] if j % 2 == 0 else WB_F2[j // 2]
    WBLK = _WB()
    KT_ml = dscr("KT_ml", [1024, SEQ], BF16); K_tm = dscr("K_tm", [SEQ, 1024], BF16); V_ml = dscr("V_ml", [SEQ, 2048], BF16)
    GATES = dscr("GATES", [8, SEQ], F32); ROWS = dscr("ROWS", [3, 4, SEQ], F32)
    KT_mla = dscr("KT_mla", [2048, SEQ], BF16); KRT = dscr("KRT", [64, SEQ], BF16); V_mla = dscr("V_mla", [SEQ, 2048], BF16)
    QT_ml = dscr("QT_ml", [1024, NQ], BF16); OG = dscr("OG", [NQ, 2048], BF16)
    QN_T = dscr("QN_T", [2048, NQ], BF16); QR_T = dscr("QR_T", [1024, NQ], BF16)
    A_s = dscr("A_s", [NQ, 4096], BF16); X1 = dscr("X1", [NQ, 4096], F32)
    KT_ml_s = dscr("KT_ml_s", [1024, 64], BF16); K_tm_s = dscr("K_tm_s", [64, 1024], BF16); V_ml_s = dscr("V_ml_s", [64, 2048], BF16)
    GATES_s = dscr("GATES_s", [8, 64], F32); ROWS_s = dscr("ROWS_s", [3, 4, 64], F32)
    KT_mla_s = dscr("KT_mla_s", [2048, 64], BF16); KRT_s = dscr("KRT_s", [64, 64], BF16); V_mla_s = dscr("V_mla_s", [64, 2048], BF16)
    KT_mla_p = dscr("KT_mla_p", [4, 2048, PAST], BF16); KRT_p = dscr("KRT_p", [4, 64, PAST], BF16)
    V_mla_p = dscr("V_mla_p", [4, PAST, 2048], BF16)
    junk_ckv = dscr("junk_ckv", [PAST, 512], F32); junk_kr = dscr("junk_kr", [PAST, 64], F32)

    with contextlib.ExitStack() as gst:
        K = Kern(nc, gst)
        op = K.op

        def dma(q, out, in_, reads=(), writes=()):
            h = nc.sync if q == "sq" else nc.gpsimd
            return op(q, lambda: h.dma_start(out=out, in_=in_, allow_slow_non_contiguous=True), reads, writes)

        def copy(en, out, in_, reads, writes):
            if en == "act":
                return op("act", lambda: nc.scalar.copy(out=out, in_=in_), reads, writes)
            h = nc.vector if en == "dve" else nc.gpsimd
            return op(en, lambda: h.tensor_copy(out=out, in_=in_), reads, writes)

        def mm(out, lhsT, rhs, start, stop, reads, writes, inc=None):
            return op("pe", lambda: nc.tensor.matmul(out, lhsT=lhsT, rhs=rhs, start=start, stop=stop),
                      reads, writes, inc=(stop if inc is None else inc))

        def tr(out, in_, idn, reads, writes, inc=True):
            return op("pe", lambda: nc.tensor.transpose(out, in_, idn), reads, writes, inc=inc)

        psA = Ring([K.ps([128, 512], F32, f"psA{i}") for i in range(3)])
        psT = Ring([K.ps([128, 512], F32, f"psT{i}") for i in range(2)])
        psO = Ring([K.ps([128, 512], F32, f"psO{i}") for i in range(2)])
        psX = K.ps([128, 512], F32, "psX")

        def bfv(ps):
            return ps.t[:].bitcast(BF16)

        ident = K.sb([128, 128], F32, "ident"); identb = K.sb([128, 128], BF16, "identb")
        ones_bf = K.sb([128, 128], BF16, "ones_bf")
        mods = K.sb([128, 192, 5], F32, "mods")
        A1 = K.sb([128, 32, 5], F32, "A1"); G1 = K.sb([128, 32, 5], F32, "G1")
        A2 = K.sb([128, 32, 5], F32, "A2"); G2 = K.sb([128, 32, 5], F32, "G2")
        gfm = K.sb([128, 4, 32], F32, "gfm")
        gcq = K.sb([128, 8], F32, "gcq")
        bada = K.sb([128, 192], F32, "bada")
        bigfg = K.sb([4, 2], F32, "bigfg")
        Mown = K.sb([128, 4, NOB], F32, "Mown"); bown = K.sb([128, 4, NOB], F32, "bown")
        negMown = K.sb([128, 4, NOB], F32, "negMown"); ebown = K.sb([128, 4, NOB], F32, "ebown")
        dma("sq", ident.t[:], ident_d, writes=[ident.r])
        dma("gq", identb.t[:], ident_d, writes=[identb.r])
        op("pool", lambda: nc.gpsimd.memset(ones_bf.t[:], 1.0), writes=[ones_bf.r])
        eps_t = K.sb([128, 1], F32, "eps_t")
        op("pool", lambda: nc.gpsimd.memset(eps_t.t[:], EPS), writes=[eps_t.r])
        for i, g in enumerate((gpre1_d, gpost1_d, gpre2_d, gpost2_d)):
            dma("sq", gfm.t[:, i, :], g, writes=[gfm.r])
        dma("sq", gcq.t[:], g_cq_d, writes=[gcq.r])
        dma("sq", bada.t[:], b_ada_fm, writes=[bada.r])
        dma("sq", bigfg.t[:, 0:1], b_ig_d, writes=[bigfg.r])
        dma("sq", bigfg.t[:, 1:2], b_fg_d, writes=[bigfg.r])

        wres = {}
        wshape = {}
        wnext = [0]

        def wconv(name, src3, KC, ncols):
            i = wnext[0]
            wnext[0] += 1
            assert i < NWB and KC * ncols <= 8192
            r = Res(name)
            dst = WBLK[i][:, 0:KC * ncols].rearrange("p (k c) -> p k c", c=ncols)
            dma("gq", dst, src3, writes=[r])
            wres[name] = (i, r)
            wshape[name] = (KC, ncols)

        def w3(src, c0, ncols):
            return src.rearrange("(k p) c -> p k c", p=128)[:, :, c0:c0 + ncols]

        def conv_stage1():
            for i in range(4):
                wconv(f"k{i}", w3(w_in, C_K + 256 * i, 256), 32, 256)
            for i in range(8):
                wconv(f"v{i}", w3(w_in, C_V + 256 * i, 256), 32, 256)
            for i in range(2):
                wconv(f"ckv{i}", w3(w_in, C_CKV + 256 * i, 256), 32, 256)
            wconv("ig", w3(w_in, C_IG, 4), 32, 4)
            wconv("fg", w3(w_in, C_FG, 4), 32, 4)
            wconv("kr", w3(w_in, C_KR, 64), 32, 64)
            wconv("wuk", w3(w_uk, 0, 2048), 4, 2048)
            wconv("wuv", w3(w_uv, 0, 2048), 4, 2048)

        def conv_rest():
            for i in range(4):
                wconv(f"q{i}", w3(w_in, C_Q + 256 * i, 256), 32, 256)
            for i in range(8):
                wconv(f"o{i}", w3(w_in, C_O + 256 * i, 256), 32, 256)
            for i in range(4):
                wconv(f"cq{i}", w3(w_in, C_CQ + 256 * i, 256), 32, 256)
            for i in range(3):
                wconv(f"uq{i}", w3(w_uq, 1024 * i, 1024), 8, 1024)
            for i in range(16):
                wconv(f"wo{i}", w3(w_out, 256 * i, 256), 32, 256)
            for i in range(64):
                wconv(f"f1_{i}", w3(w_ff1, 256 * i, 256), 32, 256)
                wconv(f"f2_{i}", w_ff2[256 * i:256 * (i + 1), :].rearrange("(k p) c -> p k c", p=128), 2, 4096)

        class WLoader:
            def __init__(self, nbuf):
                self.ring = Ring([K.sb([128, 8192], BF16) for _ in range(nbuf)])

            def load(self, name, buf=None):
                i, r = wres[name]
                KC, ncols = wshape[name]
                b = buf or self.ring.next()
                dma("sq", b.t[:, 0:KC * ncols], WBLK[i][:, 0:KC * ncols], reads=[r], writes=[b.r])
                return b, b.t[:, 0:KC * ncols].rearrange("p (k c) -> p k c", c=ncols)

        def rstd_from_ss(out, ss, inv_n, reads, writes):
            P = out.shape[0]
            op("act", lambda: nc.scalar.activation(out=out, in_=ss, func=AF.Sqrt, bias=eps_t.t[0:P, :], scale=inv_n),
               list(reads) + [eps_t.r], writes)
            op("dve", lambda: nc.vector.reciprocal(out=out, in_=out), list(writes), writes)

        sample_groups = [(16 * s, 16, 1 + s) for s in range(4)]

        def prep(src, T, Amod, Bmod, groups, uT, rows_ring, junk, ss, rstd):
            nsub = (T + 127) // 128
            for s in range(nsub):
                rows = min(128, T - s * 128)
                xt = rows_ring.next()
                dma("sq", xt.t[:rows, :], src[s * 128:s * 128 + rows, :], writes=[xt.r])
                op("act", lambda: nc.scalar.activation(out=junk.t[:rows, :], in_=xt.t[:rows, :], func=AF.Square,
                                                       accum_out=ss.t[:rows, :]), [xt.r], [junk.r, ss.r])
                rstd_from_ss(rstd.t[:rows, :], ss.t[:rows, :], 1.0 / 4096, [ss.r], [rstd.r])
                op("act", lambda: nc.scalar.activation(out=xt.t[:rows, :], in_=xt.t[:rows, :], func=AF.Copy,
                                                       scale=rstd.t[:rows, :]), [rstd.r, xt.r], [xt.r])
                for g in range(8):
                    pt = psT.next()
                    for i in range(4):
                        k = g * 4 + i
                        tr(pt.t[:, i * 128:i * 128 + rows], xt.t[:rows, k * 128:(k + 1) * 128], ident.t[:rows, :rows],
                           [xt.r, ident.r], [pt.r], inc=(i == 3))
                    for i in range(4):
                        k = g * 4 + i
                        for (c0, n, r) in groups:
                            lo, hi = max(c0, s * 128), min(c0 + n, s * 128 + rows)
                            if lo >= hi:
                                continue
                            o_ = uT.t[:, k, lo:hi]
                            i_ = pt.t[:, i * 128 + lo - s * 128:i * 128 + hi - s * 128]
                            a_, b_ = Amod.t[:, k, r:r + 1], Bmod[:, k, r:r + 1]
                            if K.alt() == "act":
                                op("act", lambda: nc.scalar.activation(out=o_, in_=i_, func=AF.Identity, bias=b_, scale=a_),
                                   [pt.r, Amod.r, mods.r], [uT.r])
                            else:
                                op("dve", lambda: nc.vector.tensor_scalar(out=o_, in0=i_, scalar1=a_, scalar2=b_,
                                                                          op0=ALU.mult, op1=ALU.add),
                                   [pt.r, Amod.r, mods.r], [uT.r])

        def prep_plain(src, T, aT, rowsb_ring):
            nsub = (T + 127) // 128
            for s in range(nsub):
                rows = min(128, T - s * 128)
                xt = rowsb_ring.next()
                dma("sq", xt.t[:rows, :], src[s * 128:s * 128 + rows, :], writes=[xt.r])
                for g in range(4):
                    pt = psT.next()
                    pv = bfv(pt)
                    for i in range(8):
                        k = g * 8 + i
                        tr(pv[:, i * 128:i * 128 + rows], xt.t[:rows, k * 128:(k + 1) * 128], identb.t[:rows, :rows],
                           [xt.r, identb.r], [pt.r], inc=(i == 7))
                    o_ = aT.t[:, g * 8:(g + 1) * 8, s * 128:s * 128 + rows]
                    i_ = pv.rearrange("p (a b) -> p a b", b=128)[:, :, 0:rows]
                    copy(K.alt(), o_, i_, [pt.r], [aT.r])

        def post(yT, T, Gmod, groups, res_src, dst, rows_ring, sq_ring, rbc):
            for k in range(32):
                sqb = sq_ring.next()
                op("act", lambda: nc.scalar.activation(out=sqb.t[:, 0:T], in_=yT.t[:, k, 0:T], func=AF.Square),
                   [yT.r], [sqb.r])
                mm(psX.t[:, 0:T], ones_bf.t[:], sqb.t[:, 0:T], k == 0, k == 31, [ones_bf.r, sqb.r], [psX.r], inc=True)
            rstd_from_ss(rbc.t[:, 0:T], psX.t[:, 0:T], 1.0 / 4096, [psX.r], [rbc.r])
            for k in range(32):
                for (c0, n, r) in groups:
                    y_ = yT.t[:, k, c0:c0 + n]
                    op("dve", lambda: nc.vector.scalar_tensor_tensor(out=y_, in0=y_, scalar=Gmod.t[:, k, r:r + 1],
                                                          in1=rbc.t[:, c0:c0 + n], op0=ALU.mult, op1=ALU.mult),
                       [yT.r, Gmod.r, rbc.r], [yT.r])
            nsub = (T + 127) // 128
            for s in range(nsub):
                rows = min(128, T - s * 128)
                xr = rows_ring.next()
                dma("sq", xr.t[:rows, :], res_src[s * 128:s * 128 + rows, :], writes=[xr.r])
                for g in range(8):
                    pt = psT.next()
                    for i in range(4):
                        tr(pt.t[:rows, i * 128:(i + 1) * 128], yT.t[:, g * 4 + i, s * 128:s * 128 + rows], ident.t[:],
                           [yT.r, ident.r], [pt.r], inc=(i == 3))
                    x_ = xr.t[:rows, g * 512:(g + 1) * 512]
                    op("dve", lambda: nc.vector.tensor_tensor(out=x_, in0=pt.t[:rows, 0:512], in1=x_, op=ALU.add),
                       [pt.r, xr.r], [xr.r])
                dma("sq", dst[s * 128:s * 128 + rows, :], xr.t[:rows, :], reads=[xr.r])

        conv_stage1()
        with K.scope():
            cT = K.sb([128, 32, 5], F32); cs = K.sb([128, 32, 5], BF16)
            dma("sq", cT.t[:], c_T, writes=[cT.r])
            op("act", lambda: nc.scalar.activation(out=cs.t[:], in_=cT.t[:], func=AF.Silu), [cT.r], [cs.r])
            wr = Ring([K.sb([128, 32, 256], BF16) for _ in range(3)])
            t5r = Ring([K.sb([8, 256], F32) for _ in range(2)])
            wav = w_ada.rearrange("(k p) c -> p k c", p=128)
            for cb in range(96):
                wb = wr.next()
                dma("gq", wb.t[:], wav[:, :, cb * 256:(cb + 1) * 256], writes=[wb.r])
                ps = psA.next()
                for k in range(32):
                    mm(ps.t[0:5, 0:256], cs.t[:, k, :], wb.t[:, k, :], k == 0, k == 31, [cs.r, wb.r], [ps.r])
                t5 = t5r.next()
                copy("act", t5.t[0:5, :], ps.t[0:5, 0:256], [ps.r], [t5.r])
                pt = psT.next()
                for i in range(2):
                    tr(pt.t[:, i * 5:(i + 1) * 5], t5.t[0:5, i * 128:(i + 1) * 128], ident.t[0:5, 0:5],
                       [t5.r, ident.r], [pt.r], inc=(i == 1))
                op("dve", lambda: nc.vector.tensor_tensor(
                    out=mods.t[:, 2 * cb:2 * cb + 2, :], in0=pt.t[:, 0:10].rearrange("p (a b) -> p a b", b=5),
                    in1=bada.t[:, 2 * cb:2 * cb + 2, None].to_broadcast([128, 2, 5]), op=ALU.add),
                   [pt.r, bada.r], [mods.r])
            conv_rest()

            def bc5(i):
                return gfm.t[:, i, :, None].to_broadcast([128, 32, 5])
            op("dve", lambda: nc.vector.scalar_tensor_tensor(out=A1.t[:], in0=mods.t[:, 32:64, :], scalar=1.0, in1=bc5(0),
                                                             op0=ALU.add, op1=ALU.mult), [mods.r, gfm.r], [A1.r])
            op("dve", lambda: nc.vector.tensor_tensor(out=G1.t[:], in0=mods.t[:, 64:96, :], in1=bc5(1), op=ALU.mult),
               [mods.r, gfm.r], [G1.r])
            op("dve", lambda: nc.vector.scalar_tensor_tensor(out=A2.t[:], in0=mods.t[:, 128:160, :], scalar=1.0, in1=bc5(2),
                                                             op0=ALU.add, op1=ALU.mult), [mods.r, gfm.r], [A2.r])
            op("dve", lambda: nc.vector.tensor_tensor(out=G2.t[:], in0=mods.t[:, 160:192, :], in1=bc5(3), op=ALU.mult),
               [mods.r, gfm.r], [G2.r])
        B1 = mods.t[:, 0:32, :]
        B2 = mods.t[:, 96:128, :]
        if stop_after <= 0:
            K.finish()
            return nc

        def mla_up(ckv32, kro, T, wuk, wuv, dKT, dKRT, dV, c0, L):
            nsub = (T + 127) // 128
            rws = [min(128, T - s * 128) for s in range(nsub)]
            op("dve", lambda: nc.vector.tensor_copy(out=L["ckvb"].t[:, 0:nsub, :], in_=ckv32.t[:, 0:nsub, :]),
               [ckv32.r], [L["ckvb"].r])
            op("dve", lambda: nc.vector.tensor_copy(out=L["krb"].t[:, 0:nsub, :], in_=kro.t[:, 0:nsub, :]),
               [kro.r], [L["krb"].r])
            for c in range(4):
                pt = psT.next()
                pv = bfv(pt)
                for s in range(nsub):
                    tr(pv[:, s * 128:s * 128 + rws[s]], L["ckvb"].t[:rws[s], s, c * 128:(c + 1) * 128],
                       identb.t[:rws[s], :rws[s]], [L["ckvb"].r, identb.r], [pt.r], inc=(s == nsub - 1))
                copy(K.alt(), L["ckvT"].t[:, c, 0:T], pv[:, 0:T], [pt.r], [L["ckvT"].r])
            pt = psT.next()
            pv = bfv(pt)
            for s in range(nsub):
                tr(pv[0:64, s * 128:s * 128 + rws[s]], L["krb"].t[:rws[s], s, :], identb.t[:rws[s], :rws[s]],
                   [L["krb"].r, identb.r], [pt.r], inc=(s == nsub - 1))
            copy(K.alt(), L["krT"].t[:, 0:T], pv[0:64, 0:T], [pt.r], [L["krT"].r])
            dma("sq", dKRT[:, c0:c0 + T], L["krT"].t[:, 0:T], reads=[L["krT"].r])
            for h in range(16):
                ps = psA.next()
                for c in range(4):
                    mm(ps.t[:, 0:T], wuk[1][:, c, h * 128:(h + 1) * 128], L["ckvT"].t[:, c, 0:T], c == 0, c == 3,
                       [wuk[0].r, L["ckvT"].r], [ps.r])
                st = L["st512"].next()
                copy(K.alt(), st.t[:, 0:T], ps.t[:, 0:T], [ps.r], [st.r])
                dma("sq", dKT[h * 128:(h + 1) * 128, c0:c0 + T], st.t[:, 0:T], reads=[st.r])
            vst = L["vst"]
            for s in range(nsub):
                for cb in range(4):
                    ps = psA.next()
                    for c in range(4):
                        mm(ps.t[:rws[s], 0:512], L["ckvT"].t[:, c, s * 128:s * 128 + rws[s]], wuv[1][:, c, cb * 512:(cb + 1) * 512],
                           c == 0, c == 3, [wuv[0].r, L["ckvT"].r], [ps.r])
                    copy(K.alt(), vst.t[:rws[s], s, cb * 512:(cb + 1) * 512], ps.t[:rws[s], 0:512], [ps.r], [vst.r])
            if T % 128 == 0:
                dma("sq", dV[c0:c0 + T, :].rearrange("(s p) f -> p s f", p=128), vst.t[:, 0:nsub, :], reads=[vst.r])
            else:
                dma("sq", dV[c0:c0 + T, :], vst.t[:T, 0, :], reads=[vst.r])

        def tm_out(dst, c0, T, buf, width):
            if T % 128 == 0:
                dma("sq", dst[c0:c0 + T, :].rearrange("(s p) f -> p s f", p=128), buf.t[:, 0:T // 128, 0:width], reads=[buf.r])
            else:
                dma("sq", dst[c0:c0 + T, :], buf.t[:T, 0, 0:width], reads=[buf.r])

        def tm_in(buf, src, c0, T, width):
            if T % 128 == 0:
                dma("sq", buf.t[:, 0:T // 128, 0:width], src[c0:c0 + T, :].rearrange("(s p) f -> p s f", p=128), writes=[buf.r])
            else:
                dma("sq", buf.t[:T, 0, 0:width], src[c0:c0 + T, :], writes=[buf.r])

        def rope_tm(x, cst, nsub, rows, L, out):
            x1, x2 = x.t[:rows, 0:nsub, 0:32], x.t[:rows, 0:nsub, 32:64]
            co, si = cst.t[:rows, 0:nsub, 0:32], cst.t[:rows, 0:nsub, 32:64]
            t = [L["rt"].t[:rows, i, 0:nsub, :] for i in range(4)]
            rr = [x.r, cst.r]
            op("dve", lambda: nc.vector.tensor_tensor(out=t[0], in0=x1, in1=co, op=ALU.mult), rr, [L["rt"].r])
            op("dve", lambda: nc.vector.tensor_tensor(out=t[1], in0=x2, in1=si, op=ALU.mult), rr, [L["rt"].r])
            op("dve", lambda: nc.vector.tensor_tensor(out=t[2], in0=x2, in1=co, op=ALU.mult), rr, [L["rt"].r])
            op("dve", lambda: nc.vector.tensor_tensor(out=t[3], in0=x1, in1=si, op=ALU.mult), rr, [L["rt"].r])
            op("dve", lambda: nc.vector.tensor_tensor(out=out.t[:rows, 0:nsub, 0:32], in0=t[0], in1=t[1], op=ALU.subtract),
               [L["rt"].r], [out.r])
            op("dve", lambda: nc.vector.tensor_tensor(out=out.t[:rows, 0:nsub, 32:64], in0=t[2], in1=t[3], op=ALU.add),
               [L["rt"].r], [out.r])

        def kv_side(uT, T, c0, D, WL, wuk, wuv, L, cs_src):
            nsub = (T + 127) // 128
            rws = [min(128, T - s * 128) for s in range(nsub)]
            ktm = L["ktm"]
            for cb in range(4):
                wb, wv = WL.load(f"k{cb}")
                for cc in range(2):
                    ch = cb * 2 + cc
                    ps = psA.next()
                    for k in range(32):
                        mm(ps.t[:, 0:T], wv[:, k, cc * 128:(cc + 1) * 128], uT.t[:, k, 0:T], k == 0, k == 31, [wb.r, uT.r], [ps.r])
                    st = L["st512"].next()
                    if K.alt() == "act":
                        op("act", lambda: nc.scalar.activation(out=st.t[:, 0:T], in_=ps.t[:, 0:T], func=AF.Copy, scale=0.0625),
                           [ps.r], [st.r])
                    else:
                        op("dve", lambda: nc.vector.tensor_scalar(out=st.t[:, 0:T], in0=ps.t[:, 0:T], scalar1=0.0625,
                                                                  scalar2=None, op0=ALU.mult), [ps.r], [st.r])
                    dma("sq", D["KT_ml"][ch * 128:(ch + 1) * 128, c0:c0 + T], st.t[:, 0:T], reads=[st.r])
                    pt = psT.next()
                    pv = bfv(pt)
                    for s in range(nsub):
                        tr(pv[:rws[s], s * 128:(s + 1) * 128], st.t[:, s * 128:s * 128 + rws[s]], identb.t[:],
                           [st.r, identb.r], [pt.r], inc=(s == nsub - 1))
                    rmax = rws[0]
                    copy(K.alt(), ktm.t[:rmax, 0:nsub, ch * 128:(ch + 1) * 128],
                         pv.rearrange("p (a b) -> p a b", b=128)[:rmax, 0:nsub, :], [pt.r], [ktm.r])
            tm_out(D["K_tm"], c0, T, ktm, 1024)
            vst = L["vst"]
            for cb in range(8):
                wb, wv = WL.load(f"v{cb}")
                for s in range(nsub):
                    ps = psA.next()
                    for k in range(32):
                        mm(ps.t[:rws[s], 0:256], uT.t[:, k, s * 128:s * 128 + rws[s]], wv[:, k, :], k == 0, k == 31, [wb.r, uT.r], [ps.r])
                    copy(K.alt(), vst.t[:rws[s], s, cb * 256:(cb + 1) * 256], ps.t[:rws[s], 0:256], [ps.r], [vst.r])
            tm_out(D["V_ml"], c0, T, vst, 2048)
            for gi, nm in enumerate(("ig", "fg")):
                wb, wv = WL.load(nm)
                ps = psA.next()
                for k in range(32):
                    mm(ps.t[0:4, 0:T], wv[:, k, :], uT.t[:, k, 0:T], k == 0, k == 31, [wb.r, uT.r], [ps.r])
                gs = L["gst"].next()
                op("act", lambda: nc.scalar.activation(out=gs.t[0:4, 0:T], in_=ps.t[0:4, 0:T], func=AF.Identity,
                                                       bias=bigfg.t[:, gi:gi + 1], scale=1.0), [ps.r, bigfg.r], [gs.r])
                dma("sq", D["GATES"][4 * gi:4 * gi + 4, c0:c0 + T], gs.t[0:4, 0:T], reads=[gs.r])
            ckv32 = L["ckv32"]
            for cb in range(2):
                wb, wv = WL.load(f"ckv{cb}")
                for s in range(nsub):
                    ps = psA.next()
                    for k in range(32):
                        mm(ps.t[:rws[s], 0:256], uT.t[:, k, s * 128:s * 128 + rws[s]], wv[:, k, :], k == 0, k == 31, [wb.r, uT.r], [ps.r])
                    copy(K.alt(), ckv32.t[:rws[s], s, cb * 256:(cb + 1) * 256], ps.t[:rws[s], 0:256], [ps.r], [ckv32.r])
            wb, wv = WL.load("kr")
            kr32 = L["kr32"]
            for s in range(nsub):
                ps = psA.next()
                for k in range(32):
                    mm(ps.t[:rws[s], 0:64], uT.t[:, k, s * 128:s * 128 + rws[s]], wv[:, k, :], k == 0, k == 31, [wb.r, uT.r], [ps.r])
                copy(K.alt(), kr32.t[:rws[s], s, :], ps.t[:rws[s], 0:64], [ps.r], [kr32.r])
            ss4, rs4 = L["ss4"], L["rs4"]
            for s in range(nsub):
                op("act", lambda: nc.scalar.activation(out=L["junk512"].t[:rws[s], :], in_=ckv32.t[:rws[s], s, :], func=AF.Square,
                                                       accum_out=ss4.t[:rws[s], s:s + 1]), [ckv32.r], [L["junk512"].r, ss4.r])
            rstd_from_ss(rs4.t[:rws[0], 0:nsub], ss4.t[:rws[0], 0:nsub], 1.0 / 512, [ss4.r], [rs4.r])
            for s in range(nsub):
                c_ = ckv32.t[:rws[s], s, :]
                op("dve", lambda: nc.vector.scalar_tensor_tensor(out=c_, in0=c_, scalar=rs4.t[:rws[s], s:s + 1],
                                                                 in1=L["gckv"].t[:rws[s], :], op0=ALU.mult, op1=ALU.mult),
                   [ckv32.r, rs4.r, L["gckv"].r], [ckv32.r])
            tm_out(D["ckv_o"], c0, T, ckv32, 512)
            cst = L["cst"]
            tm_in(cst, cs_src, 0, T, 64)
            rope_tm(kr32, cst, nsub, rws[0], L, L["kro"])
            tm_out(D["kr_o"], c0, T, L["kro"], 64)
            mla_up(ckv32, L["kro"], T, wuk, wuv, D["KT_mla"], D["KRT"], D["V_mla"], c0, L)

        def kv_locals():
            L = {}
            L["ktm"] = K.sb([128, 4, 1024], BF16)
            L["vst"] = K.sb([128, 4, 2048], BF16)
            L["st512"] = Ring([K.sb([128, 512], BF16) for _ in range(3)])
            L["gst"] = Ring([K.sb([4, 512], F32) for _ in range(2)])
            L["ckv32"] = K.sb([128, 4, 512], F32)
            L["kr32"] = K.sb([128, 4, 64], F32)
            L["kro"] = K.sb([128, 4, 64], F32)
            L["cst"] = K.sb([128, 4, 64], F32)
            L["rt"] = K.sb([128, 4, 4, 32], F32)
            L["ss4"] = K.sb([128, 4], F32)
            L["rs4"] = K.sb([128, 4], F32)
            L["junk512"] = K.sb([128, 512], BF16)
            L["gckv"] = K.sb([128, 512], F32)
            L["ckvb"] = K.sb([128, 4, 512], BF16)
            L["krb"] = K.sb([128, 4, 64], BF16)
            L["ckvT"] = K.sb([128, 4, 512], BF16)
            L["krT"] = K.sb([64, 512], BF16)
            dma("sq", L["gckv"].t[:], g_ckv_d.partition_broadcast(128), writes=[L["gckv"].r])
            return L

        Dp = dict(KT_ml=KT_ml, K_tm=K_tm, V_ml=V_ml, GATES=GATES, ckv_o=ckv_o, kr_o=kr_o, KT_mla=KT_mla, KRT=KRT, V_mla=V_mla)
        Ds = dict(KT_ml=KT_ml_s, K_tm=K_tm_s, V_ml=V_ml_s, GATES=GATES_s, ckv_o=s_ckv, kr_o=s_kr, KT_mla=KT_mla_s, KRT=KRT_s,
                  V_mla=V_mla_s)

        with K.scope():
            uT = K.sb([128, 32, 512], BF16)
            rows_ring = Ring([K.sb([128, 4096], F32) for _ in range(1)])
            ss = K.sb([128, 1], F32); rstd = K.sb([128, 1], F32)
            WL = WLoader(3)
            L = kv_locals()
            junk = View(L["vst"].t[:, 0:2, :].rearrange("p a b -> p (a b)"), L["vst"].r)
            wukb = K.sb([128, 8192], BF16); wuvb = K.sb([128, 8192], BF16)
            wuk = WL.load("wuk", wukb); wuv = WL.load("wuv", wuvb)
            for ti in range(NTA):
                prep(x_all[ti * 512:(ti + 1) * 512, :], 512, A1, B1, [(0, 512, 0)], uT, rows_ring, junk, ss, rstd)
                kv_side(uT, 512, ti * 512, Dp, WL, wuk, wuv, L, cs_all[ti * 512:(ti + 1) * 512, :])
            prep(x_smp, 64, A1, B1, sample_groups, uT, rows_ring, junk, ss, rstd)
            kv_side(uT, 64, 0, Ds, WL, wuk, wuv, L, cs_own[OWN:OWN + 64, :])
            for sq_ in range(4):
                for ti in range(NPT):
                    tm_in(L["ckv32"], ckv_past[sq_], ti * 512, 512, 512)
                    tm_in(L["kro"], kr_past[sq_], ti * 512, 512, 64)
                    mla_up(L["ckv32"], L["kro"], 512, wuk, wuv, KT_mla_p[sq_], KRT_p[sq_], V_mla_p[sq_], ti * 512, L)
        if stop_after <= 1:
            K.finish()
            return nc

        def vmemset(buf, ap, val):
            op("dve", lambda: nc.vector.memset(ap, val), [], [buf.r])

        with K.scope():
            SG = min(2048, SEQ)
            ones4 = K.sb([4, SG], F32); vmemset(ones4, ones4.t[:], 1.0)
            one1 = K.sb([4, 1], F32); vmemset(one1, one1.t[:], 1.0)
            bprev = K.sb([4, 1], F32); Mprev = K.sb([4, 1], F32)
            igt = K.sb([4, SG], F32); fgt = K.sb([4, SG], F32); bt = K.sb([4, SG], F32)
            ct = K.sb([4, SG], F32); Mt = K.sb([4, SG], F32)
            m0T = K.sb([4, 4], F32); smt = K.sb([4, 4], F32); mf = K.sb([4, 1], F32)
            dma("sq", m0T.t[:], m0_in, writes=[m0T.r])

            def scan_seg(ig_src, fg_src, n, c_dst, b_dst, M_dst):
                dma("sq", igt.t[:, 0:n], ig_src, writes=[igt.r])
                dma("sq", fgt.t[:, 0:n], fg_src, writes=[fgt.r])
                f_ = fgt.t[:, 0:n]
                op("act", lambda: nc.scalar.activation(out=f_, in_=f_, func=AF.Exp, scale=-1.0), [fgt.r], [fgt.r])
                op("act", lambda: nc.scalar.activation(out=f_, in_=f_, func=AF.Ln, bias=one1.t[:, 0:1], scale=1.0),
                   [fgt.r, one1.r], [fgt.r])
                op("dve", lambda: nc.vector.tensor_tensor_scan(out=bt.t[:, 0:n], data0=ones4.t[:, 0:n], data1=f_,
                                                               initial=bprev.t[:, 0:1], op0=ALU.mult, op1=ALU.subtract),
                   [ones4.r, fgt.r, bprev.r], [bt.r])
                op("dve", lambda: nc.vector.tensor_tensor(out=ct.t[:, 0:n], in0=igt.t[:, 0:n], in1=bt.t[:, 0:n], op=ALU.subtract),
                   [igt.r, bt.r], [ct.r])
                op("dve", lambda: nc.vector.tensor_tensor_scan(out=Mt.t[:, 0:n], data0=ones4.t[:, 0:n], data1=ct.t[:, 0:n],
                                                               initial=Mprev.t[:, 0:1], op0=ALU.mult, op1=ALU.max),
                   [ones4.r, ct.r, Mprev.r], [Mt.r])
                op("dve", lambda: nc.vector.tensor_copy(out=bprev.t[:], in_=bt.t[:, n - 1:n]), [bt.r], [bprev.r])
                op("dve", lambda: nc.vector.tensor_copy(out=Mprev.t[:], in_=Mt.t[:, n - 1:n]), [Mt.r], [Mprev.r])
                dma("sq", c_dst, ct.t[:, 0:n], reads=[ct.r])
                dma("sq", b_dst, bt.t[:, 0:n], reads=[bt.r])
                dma("sq", M_dst, Mt.t[:, 0:n], reads=[Mt.r])

            vmemset(bprev, bprev.t[:], 0.0); vmemset(Mprev, Mprev.t[:], 0.0)
            for g0 in range(0, SEQ, SG):
                scan_seg(GATES[0:4, g0:g0 + SG], GATES[4:8, g0:g0 + SG], SG,
                         ROWS[0, :, g0:g0 + SG], ROWS[1, :, g0:g0 + SG], ROWS[2, :, g0:g0 + SG])
            op("dve", lambda: nc.vector.tensor_tensor(out=mf.t[:], in0=bprev.t[:], in1=Mprev.t[:], op=ALU.add),
               [bprev.r, Mprev.r], [mf.r])
            dma("sq", m_o, mf.t[:], reads=[mf.r])
            for s_ in range(4):
                vmemset(bprev, bprev.t[:], 0.0)
                op("dve", lambda: nc.vector.tensor_copy(out=Mprev.t[:], in_=m0T.t[:, s_:s_ + 1]), [m0T.r], [Mprev.r])
                scan_seg(GATES_s[0:4, 16 * s_:16 * s_ + 16], GATES_s[4:8, 16 * s_:16 * s_ + 16], 16,
                         ROWS_s[0, :, 16 * s_:16 * s_ + 16], ROWS_s[1, :, 16 * s_:16 * s_ + 16], ROWS_s[2, :, 16 * s_:16 * s_ + 16])
                op("dve", lambda: nc.vector.tensor_tensor(out=smt.t[:, s_:s_ + 1], in0=bprev.t[:], in1=Mprev.t[:], op=ALU.add),
                   [bprev.r, Mprev.r], [smt.r])
            dma("sq", s_m, smt.t[:], reads=[smt.r])
        if stop_after <= 2:
            K.finish()
            return nc

        with K.scope():
            selsb = K.sb([NBLK, NOB], F32)
            dma("sq", selsb.t[:], sel_d, writes=[selsb.r])
            rowr = Ring([K.sb([NBLK, 128], F32) for _ in range(2)])
            for h in range(4):
                for ridx, dst in ((2, Mown), (1, bown)):
                    rb = rowr.next()
                    dma("sq", rb.t[:], ROWS[ridx, h, :].rearrange("(b t) -> b t", t=128), writes=[rb.r])
                    mm(psX.t[:, 0:NOB], rb.t[:, :], selsb.t[:, :], True, True, [rb.r, selsb.r], [psX.r])
                    copy("dve", dst.t[:, h, :], psX.t[:, 0:NOB], [psX.r], [dst.r])
            op("dve", lambda: nc.vector.tensor_scalar(out=negMown.t[:], in0=Mown.t[:], scalar1=-1.0, scalar2=None, op0=ALU.mult),
               [Mown.r], [negMown.r])
            op("dve", lambda: nc.vector.tensor_tensor(out=ebown.t[:], in0=bown.t[:], in1=Mown.t[:], op=ALU.add),
               [bown.r, Mown.r], [ebown.r])
            op("act", lambda: nc.scalar.activation(out=ebown.t[:], in_=ebown.t[:], func=AF.Exp, scale=-1.0), [ebown.r], [ebown.r])
            wcol = K.sb([128, 4, NBLK], F32); wcolb = K.sb([128, 4, NBLK], BF16)
            mlb = K.sb([128, 4], F32)
            for h in range(4):
                rb = rowr.next()
                dma("sq", rb.t[:], ROWS[0, h, :].rearrange("(b t) -> b t", t=128), writes=[rb.r])
                dma("sq", mlb.t[:, h:h + 1], ROWS[2, h, SEQ - 1:SEQ].partition_broadcast(128), writes=[mlb.r])
                pt = psT.next()
                tr(pt.t[:, 0:NBLK], rb.t[:, :], ident.t[:NBLK, :NBLK], [rb.r, ident.r], [pt.r])
                op("dve", lambda: nc.vector.tensor_scalar(out=wcol.t[:, h, :], in0=pt.t[:, 0:NBLK], scalar1=mlb.t[:, h:h + 1],
                                                          scalar2=None, op0=ALU.subtract), [pt.r, mlb.r], [wcol.r])
            op("act", lambda: nc.scalar.activation(out=wcol.t[:], in_=wcol.t[:], func=AF.Exp), [wcol.r], [wcol.r])
            copy("dve", wcolb.t[:], wcol.t[:], [wcol.r], [wcolb.r])
            kbr = Ring([K.sb([128, 256], BF16) for _ in range(3)])
            vbr = Ring([K.sb([128, 512], BF16) for _ in range(3)])
            kgr = Ring([K.sb([128, 256], BF16) for _ in range(3)])
            cstr = Ring([K.sb([128, 512], F32) for _ in range(2)])
            nst = K.sb([1, 256], F32)
            for h in range(4):
                po = [psO.next(), psO.next()]
                for blk in range(NBLK):
                    kb, vb, kg = kbr.next(), vbr.next(), kgr.next()
                    dma("sq", kb.t[:], K_tm[blk * 128:(blk + 1) * 128, h * 256:(h + 1) * 256], writes=[kb.r])
                    dma("sq", vb.t[:], V_ml[blk * 128:(blk + 1) * 128, h * 512:(h + 1) * 512], writes=[vb.r])
                    op("dve", lambda: nc.vector.tensor_scalar(out=kg.t[:], in0=kb.t[:], scalar1=wcol.t[:, h, blk:blk + 1],
                                                              scalar2=None, op0=ALU.mult), [kb.r, wcol.r], [kg.r])
                    for dc in range(2):
                        op("pe", lambda: nc.tensor.matmul(po[dc].t[:, 0:512], lhsT=kg.t[:, dc * 128:(dc + 1) * 128], rhs=vb.t[:],
                                                          start=(blk == 0), stop=(blk == NBLK - 1)),
                           [kg.r, vb.r], [po[dc].r], inc=True)
                    op("pe", lambda: nc.tensor.matmul(psX.t[0:1, 0:256], lhsT=wcolb.t[:, h, blk:blk + 1], rhs=kb.t[:],
                                                      start=(blk == 0), stop=(blk == NBLK - 1)),
                       [wcolb.r, kb.r], [psX.r], inc=True)
                for dc in range(2):
                    cs_ = cstr.next()
                    copy(K.alt(), cs_.t[:], po[dc].t[:, 0:512], [po[dc].r], [cs_.r])
                    dma("sq", C_o[h, dc * 128:(dc + 1) * 128, :], cs_.t[:], reads=[cs_.r])
                copy("dve", nst.t[:], psX.t[0:1, 0:256], [psX.r], [nst.r])
                dma("sq", n_o[h:h + 1, :], nst.t[:], reads=[nst.r])
        if stop_after <= 3:
            K.finish()
            return nc

        def q_side(uT, T, c0, WL, L):
            nsub = (T + 127) // 128
            rws = [min(128, T - s * 128) for s in range(nsub)]
            for cb in range(4):
                wb, wv = WL.load(f"q{cb}")
                for cc in range(2):
                    ch = cb * 2 + cc
                    ps = psA.next()
                    for k in range(32):
                        mm(ps.t[:, 0:T], wv[:, k, cc * 128:(cc + 1) * 128], uT.t[:, k, 0:T], k == 0, k == 31, [wb.r, uT.r], [ps.r])
                    st = L["st512"].next()
                    copy(K.alt(), st.t[:, 0:T], ps.t[:, 0:T], [ps.r], [st.r])
                    dma("sq", QT_ml[ch * 128:(ch + 1) * 128, c0:c0 + T], st.t[:, 0:T], reads=[st.r])
            og = L["og"]
            for cb in range(8):
                wb, wv = WL.load(f"o{cb}")
                for s in range(nsub):
                    ps = psA.next()
                    for k in range(32):
                        mm(ps.t[:rws[s], 0:256], uT.t[:, k, s * 128:s * 128 + rws[s]], wv[:, k, :], k == 0, k == 31, [wb.r, uT.r], [ps.r])
                    op("act", lambda: nc.scalar.activation(out=og.t[:rws[s], s, cb * 256:(cb + 1) * 256], in_=ps.t[:rws[s], 0:256],
                                                           func=AF.Sigmoid), [ps.r], [og.r])
            tm_out(OG, c0, T, og, 2048)
            cq32, cqnT, rbc = L["cq32"], L["cqnT"], L["rbc"]
            for cb in range(4):
                wb, wv = WL.load(f"cq{cb}")
                for cc in range(2):
                    ch = cb * 2 + cc
                    ps = psA.next()
                    for k in range(32):
                        mm(ps.t[:, 0:T], wv[:, k, cc * 128:(cc + 1) * 128], uT.t[:, k, 0:T], k == 0, k == 31, [wb.r, uT.r], [ps.r])
                    copy("dve", cq32.t[:, ch, 0:T], ps.t[:, 0:T], [ps.r], [cq32.r])
                    sqb = L["sqb"].next()
                    op("act", lambda: nc.scalar.activation(out=sqb.t[:, 0:T], in_=cq32.t[:, ch, 0:T], func=AF.Square), [cq32.r], [sqb.r])
                    mm(psX.t[:, 0:T], ones_bf.t[:], sqb.t[:, 0:T], ch == 0, ch == 7, [ones_bf.r, sqb.r], [psX.r], inc=True)
            rstd_from_ss(rbc.t[:, 0:T], psX.t[:, 0:T], 1.0 / 1024, [psX.r], [rbc.r])
            for ch in range(8):
                op("dve", lambda: nc.vector.scalar_tensor_tensor(out=cqnT.t[:, ch, 0:T], in0=cq32.t[:, ch, 0:T], scalar=gcq.t[:, ch:ch + 1],
                                                                 in1=rbc.t[:, 0:T], op0=ALU.mult, op1=ALU.mult),
                   [cq32.r, gcq.r, rbc.r], [cqnT.r])
            qm32, qmb, rt2, cso, qnst, qrst = L["qm32"], L["qmb"], L["rt2"], L["cso"], L["qnst"], L["qrst"]
            for s in range(nsub):
                rows = rws[s]
                for cb in range(3):
                    wb, wv = WL.load(f"uq{cb}")
                    for half in range(2):
                        ps = psA.next()
                        for k in range(8):
                            mm(ps.t[:rows, 0:512], cqnT.t[:, k, s * 128:s * 128 + rows], wv[:, k, half * 512:(half + 1) * 512],
                               k == 0, k == 7, [wb.r, cqnT.r], [ps.r])
                        copy(K.alt(), qm32.t[:rows, cb * 1024 + half * 512:cb * 1024 + (half + 1) * 512], ps.t[:rows, 0:512], [ps.r], [qm32.r])
                dma("sq", cso.t[:rows, :], cs_own[c0 + s * 128:c0 + s * 128 + rows, :], writes=[cso.r])
                qv = qm32.t[:rows, :].rearrange("p (h e) -> p h e", e=192)
                qb = qmb.t[:rows, :].rearrange("p (h e) -> p h e", e=192)
                x1, x2 = qv[:, :, 128:160], qv[:, :, 160:192]
                co = cso.t[:rows, None, 0:32].to_broadcast([rows, 16, 32])
                si = cso.t[:rows, None, 32:64].to_broadcast([rows, 16, 32])
                t = [rt2.t[:rows, i, :, :] for i in range(4)]
                rr = [qm32.r, cso.r]
                op("dve", lambda: nc.vector.tensor_tensor(out=t[0], in0=x1, in1=co, op=ALU.mult), rr, [rt2.r])
                op("dve", lambda: nc.vector.tensor_tensor(out=t[1], in0=x2, in1=si, op=ALU.mult), rr, [rt2.r])
                op("dve", lambda: nc.vector.tensor_tensor(out=t[2], in0=x2, in1=co, op=ALU.mult), rr, [rt2.r])
                op("dve", lambda: nc.vector.tensor_tensor(out=t[3], in0=x1, in1=si, op=ALU.mult), rr, [rt2.r])
                op("act", lambda: nc.scalar.copy(out=qb[:, :, 0:128], in_=qv[:, :, 0:128]), [qm32.r], [qmb.r])
                op("dve", lambda: nc.vector.tensor_tensor(out=qb[:, :, 128:160], in0=t[0], in1=t[1], op=ALU.subtract), [rt2.r], [qmb.r])
                op("dve", lambda: nc.vector.tensor_tensor(out=qb[:, :, 160:192], in0=t[2], in1=t[3], op=ALU.add), [rt2.r], [qmb.r])
                for g in range(2):
                    pt = psT.next()
                    pv = bfv(pt)
                    for i in range(8):
                        tr(pv[:, i * 128:i * 128 + rows], qb[:, g * 8 + i, 0:128], identb.t[:rows, :rows], [qmb.r, identb.r], [pt.r], inc=(i == 7))
                    copy(K.alt(), qnst.t[:, g * 8:(g + 1) * 8, 0:rows], pv.rearrange("p (a b) -> p a b", b=128)[:, :, 0:rows], [pt.r], [qnst.r])
                for g in range(2):
                    pt = psT.next()
                    pv = bfv(pt)
                    for i in range(8):
                        tr(pv[0:64, i * 128:i * 128 + rows], qb[:, g * 8 + i, 128:192], identb.t[:rows, :rows], [qmb.r, identb.r], [pt.r], inc=(i == 7))
                    copy(K.alt(), qrst.t[:, g * 8:(g + 1) * 8, 0:rows], pv.rearrange("p (a b) -> p a b", b=128)[0:64, :, 0:rows], [pt.r], [qrst.r])
                n0_ = c0 + s * 128
                dma("sq", QN_T.rearrange("(h d) n -> d h n", d=128)[:, :, n0_:n0_ + rows], qnst.t[:, :, 0:rows], reads=[qnst.r])
                dma("sq", QR_T.rearrange("(h d) n -> d h n", d=64)[:, :, n0_:n0_ + rows], qrst.t[:, :, 0:rows], reads=[qrst.r])

        with K.scope():
            uT = K.sb([128, 32, 512], BF16)
            rows_ring = Ring([K.sb([128, 4096], F32) for _ in range(1)])
            ss = K.sb([128, 1], F32); rstd = K.sb([128, 1], F32)
            WL = WLoader(3)
            L = dict(st512=Ring([K.sb([128, 512], BF16) for _ in range(3)]), og=K.sb([128, 4, 2048], BF16),
                     cq32=K.sb([128, 8, 512], F32), cqnT=K.sb([128, 8, 512], BF16), rbc=K.sb([128, 512], F32),
                     sqb=Ring([K.sb([128, 512], BF16) for _ in range(3)]), qm32=K.sb([128, 3072], F32), qmb=K.sb([128, 3072], BF16),
                     rt2=K.sb([128, 4, 16, 32], F32), cso=K.sb([128, 64], F32), qnst=K.sb([128, 16, 128], BF16),
                     qrst=K.sb([64, 16, 128], BF16))
            junk = View(L["og"].t[:, 0:2, :].rearrange("p a b -> p (a b)"), L["og"].r)
            for ti in range(NTO):
                prep(x_own[ti * 512:(ti + 1) * 512, :], 512, A1, B1, [(0, 512, 0)], uT, rows_ring, junk, ss, rstd)
                q_side(uT, 512, ti * 512, WL, L)
            prep(x_smp, 64, A1, B1, sample_groups, uT, rows_ring, junk, ss, rstd)
            q_side(uT, 64, OWN, WL, L)
        if stop_after <= 4:
            K.finish()
            return nc

        NPG = (max(NBLK, (NKS + 127) // 128) + 7) // 8

        def pv_blocks(nk):
            return [(k0, min(128, nk - k0)) for k0 in range(0, nk, 128)]

        def transposes_and_pv(nq, Pb, nk, PT, PTres, po, ncols, vget):
            blks = pv_blocks(nk)
            nb = len(blks)
            for g0 in range(0, nb, 8):
                grp = blks[g0:g0 + 8]
                pt = psT.next()
                pv = bfv(pt)
                pr = PTres[g0 // 8]
                for i, (k0, w) in enumerate(grp):
                    tr(pv[:w, i * 128:i * 128 + nq], Pb.t[:nq, k0:k0 + w], identb.t[:nq, :nq], [Pb.r, identb.r], [pt.r],
                       inc=(i == len(grp) - 1))
                full = [i for i, (k0, w) in enumerate(grp) if w == 128]
                if full:
                    nf = len(full)
                    copy(K.alt(), PT.t[:, g0:g0 + nf, 0:nq], pv.rearrange("p (a b) -> p a b", b=128)[:, 0:nf, 0:nq], [pt.r], [pr])
                for i, (k0, w) in enumerate(grp):
                    if w < 128:
                        copy(K.alt(), PT.t[:w, g0 + i, 0:nq], pv[:w, i * 128:i * 128 + nq], [pt.r], [pr])
                for i, (k0, w) in enumerate(grp):
                    bi = g0 + i
                    v_ap, v_res = vget(bi, w)
                    op("pe", lambda: nc.tensor.matmul(po.t[:nq, 0:ncols], lhsT=PT.t[:w, bi, 0:nq], rhs=v_ap,
                                                      start=(bi == 0), stop=(bi == nb - 1)),
                       [pr] + v_res, [po.r], inc=(bi == nb - 1))

        def attn_A(nq, qn, qr, qres, KTs, KRTs, tiles, Sb, Pb, sm):
            nk = sum(w for (_, w, _) in tiles)
            for (k0, w, bias) in tiles:
                ps = psA.next()
                mm(ps.t[:nq, 0:w], qn, KTs.t[:, k0:k0 + w], True, False, qres + [KTs.r], [ps.r])
                mm(ps.t[:nq, 0:w], qr, KRTs.t[0:64, k0:k0 + w], False, True, qres + [KRTs.r], [ps.r])
                if bias is None:
                    copy(K.alt(), Sb.t[:nq, k0:k0 + w], ps.t[:nq, 0:w], [ps.r], [Sb.r])
                else:
                    op("dve", lambda: nc.vector.tensor_tensor(out=Sb.t[:nq, k0:k0 + w], in0=ps.t[:nq, 0:w], in1=bias.t[:nq, 0:w],
                                                              op=ALU.add), [ps.r, bias.r], [Sb.r])
            mx, negm, rsum, rinv = sm
            op("dve", lambda: nc.vector.reduce_max(out=mx.t[:nq, :], in_=Sb.t[:nq, 0:nk], axis=AX.X), [Sb.r], [mx.r])
            op("dve", lambda: nc.vector.tensor_scalar(out=negm.t[:nq, :], in0=mx.t[:nq, :], scalar1=-MLA_SCALE, scalar2=None,
                                                      op0=ALU.mult), [mx.r], [negm.r])
            op("act", lambda: nc.scalar.activation(out=Pb.t[:nq, 0:nk], in_=Sb.t[:nq, 0:nk], func=AF.Exp, bias=negm.t[:nq, :],
                                                   scale=MLA_SCALE, accum_out=rsum.t[:nq, :]), [Sb.r, negm.r], [Pb.r, rsum.r])
            op("dve", lambda: nc.vector.reciprocal(out=rinv.t[:nq, :], in_=rsum.t[:nq, :]), [rsum.r], [rinv.r])
            return (nq, nk, Pb, rinv)

        def attn_B(st, Vs, PT, PTres, out_ap, out_res):
            nq, nk, Pb, rinv = st
            po = psO.next()
            transposes_and_pv(nq, Pb, nk, PT, PTres, po, 128, lambda bi, w: (Vs.t[:w, bi, :], [Vs.r]))
            op("act", lambda: nc.scalar.activation(out=out_ap, in_=po.t[:nq, 0:128], func=AF.Copy, scale=rinv.t[:nq, :]),
               [po.r, rinv.r], [out_res])

        def attn_tile(nq, qn, qr, qres, KTs, KRTs, Vs, tiles, Sb, Pb, PT, PTres, sm, out_ap, out_res):
            st = attn_A(nq, qn, qr, qres, KTs, KRTs, tiles, Sb, Pb, sm)
            attn_B(st, Vs, PT, PTres, out_ap, out_res)

        KMAX = max(SEQ, NKS)
        with K.scope():
            KRTs = K.sb([64, KMAX], BF16); KTs = K.sb([128, KMAX], BF16)
            Vs = K.sb([128, (KMAX + 127) // 128, 128], BF16)
            QN = K.sb([128, OWN], BF16); QR = K.sb([64, OWN], BF16)
            biasm = K.sb([128, 512], F32)
            Sbr = Ring([K.sb([128, KMAX], F32) for _ in range(2)])
            Pbr = Ring([K.sb([128, KMAX], BF16) for _ in range(2)])
            PT = K.sb([128, NPG * 8, 128], BF16)
            PTres = [Res() for _ in range(NPG)]
            Aacc = K.sb([128, NOB, 128], BF16)
            Asmp = K.sb([16, 16, 128], BF16)
            smr = Ring([tuple(K.sb([128, 1], F32) for _ in range(4)) for _ in range(2)])
            dma("sq", biasm.t[:], bias_mla_d, writes=[biasm.r])
            dma("sq", KRTs.t[:, 0:SEQ], KRT, writes=[KRTs.r])
            for h in range(16):
                dma("sq", KTs.t[:, 0:SEQ], KT_mla[h * 128:(h + 1) * 128, :], writes=[KTs.r])
                dma("sq", Vs.t[:, 0:NBLK, :], V_mla.rearrange("(b p) f -> p b f", p=128)[:, :, h * 128:(h + 1) * 128], writes=[Vs.r])
                dma("sq", QN.t[:], QN_T[h * 128:(h + 1) * 128, 0:OWN], writes=[QN.r])
                dma("sq", QR.t[:], QR_T[h * 64:(h + 1) * 64, 0:OWN], writes=[QR.r])
                prev = None
                for m in range(NOB):
                    tiles = [(kt * 512, 512, None) for kt in range(m)] + [(m * 512, 512, biasm)]
                    cur = attn_A(128, QN.t[:, m * 128:(m + 1) * 128], QR.t[:, m * 128:(m + 1) * 128], [QN.r, QR.r], KTs, KRTs, tiles,
                                 Sbr.next(), Pbr.next(), smr.next())
                    if prev is not None:
                        attn_B(prev[0], Vs, PT, PTres, Aacc.t[:, prev[1], :], Aacc.r)
                    prev = (cur, m)
                attn_B(prev[0], Vs, PT, PTres, Aacc.t[:, prev[1], :], Aacc.r)
                dma("sq", A_s[0:OWN, :].rearrange("(m p) f -> p m f", p=128)[:, :, 2048 + h * 128:2048 + (h + 1) * 128], Aacc.t[:],
                    reads=[Aacc.r])
            for sq_ in range(4):
                dma("sq", KRTs.t[:, 0:PAST], KRT_p[sq_], writes=[KRTs.r])
                dma("sq", KRTs.t[:, PAST:NKS], KRT_s[:, 16 * sq_:16 * sq_ + 16], writes=[KRTs.r])
                for h in range(16):
                    dma("sq", KTs.t[:, 0:PAST], KT_mla_p[sq_][h * 128:(h + 1) * 128, :], writes=[KTs.r])
                    dma("sq", KTs.t[:, PAST:NKS], KT_mla_s[h * 128:(h + 1) * 128, 16 * sq_:16 * sq_ + 16], writes=[KTs.r])
                    dma("sq", Vs.t[:, 0:PAST // 128, :], V_mla_p[sq_].rearrange("(b p) f -> p b f", p=128)[:, :, h * 128:(h + 1) * 128],
                        writes=[Vs.r])
                    dma("sq", Vs.t[0:16, PAST // 128, :], V_mla_s[16 * sq_:16 * sq_ + 16, h * 128:(h + 1) * 128], writes=[Vs.r])
                    dma("sq", QN.t[:, 0:16], QN_T[h * 128:(h + 1) * 128, OWN + 16 * sq_:OWN + 16 * sq_ + 16], writes=[QN.r])
                    dma("sq", QR.t[:, 0:16], QR_T[h * 64:(h + 1) * 64, OWN + 16 * sq_:OWN + 16 * sq_ + 16], writes=[QR.r])
                    tiles = [(i * 512, 512, None) for i in range(NPT)] + [(PAST, 16, None)]
                    attn_tile(16, QN.t[:, 0:16], QR.t[:, 0:16], [QN.r, QR.r], KTs, KRTs, Vs, tiles,
                              Sbr.next(), Pbr.next(), PT, PTres, smr.next(), Asmp.t[:, h, :], Asmp.r)
                dma("sq", A_s[OWN + 16 * sq_:OWN + 16 * sq_ + 16, 2048:4096], Asmp.t[:].rearrange("p h d -> p (h d)"), reads=[Asmp.r])
        if stop_after <= 5:
            K.finish()
            return nc

        with K.scope():
            maskm = K.sb([128, 512], F32); tril16 = K.sb([16, 16], F32)
            gml = K.sb([128, 2048], F32)
            KT2 = K.sb([128, 2, SEQ], BF16); QT2 = K.sb([128, 2, OWN], BF16)
            cbc = K.sb([128, SEQ], F32); Eb = K.sb([128, SEQ], BF16); tmpE = K.sb([128, 512], F32)
            Pbr = Ring([K.sb([128, SEQ], BF16) for _ in range(2)])
            PT = K.sb([128, NPG * 8, 128], BF16)
            PTres = [Res() for _ in range(NPG)]
            vring = Ring([K.sb([128, 4, 512], BF16) for _ in range(2)])
            hbr = Ring([K.sb([128, 512], F32) for _ in range(2)])
            ogr = Ring([K.sb([128, 512], BF16) for _ in range(2)])
            astr = Ring([K.sb([128, 512], BF16) for _ in range(2)])
            junk5 = K.sb([128, 512], BF16)
            smr = Ring([tuple(K.sb([128, 1], F32) for _ in range(6)) for _ in range(2)])
            dma("sq", maskm.t[:], mask_ml_d, writes=[maskm.r])
            dma("sq", tril16.t[:], tril16_d, writes=[tril16.r])
            dma("sq", gml.t[:], g_ml_d.partition_broadcast(128), writes=[gml.r])

            def ml_A(nq, qT, qres, tiles, negM, colres):
                nk = sum(w for (_, w, _) in tiles)
                Pb = Pbr.next()
                sm = smr.next()
                nqv = sm[0]
                lastk0 = tiles[-1][0]
                lw = nk - lastk0
                if lastk0 > 0:
                    op("act", lambda: nc.scalar.activation(out=Eb.t[:nq, 0:lastk0], in_=cbc.t[:nq, 0:lastk0], func=AF.Exp, bias=negM,
                                                           scale=1.0), [cbc.r] + colres, [Eb.r])
                op("dve", lambda: nc.vector.tensor_scalar(out=tmpE.t[:nq, 0:lw], in0=cbc.t[:nq, lastk0:nk], scalar1=negM, scalar2=0.0,
                                                          op0=ALU.add, op1=ALU.min), [cbc.r] + colres, [tmpE.r])
                op("act", lambda: nc.scalar.activation(out=Eb.t[:nq, lastk0:nk], in_=tmpE.t[:nq, 0:lw], func=AF.Exp), [tmpE.r], [Eb.r])
                for (k0, w, mask) in tiles:
                    ps = psA.next()
                    for c in range(2):
                        mm(ps.t[:nq, 0:w], qT[:, c, :], KT2.t[:, c, k0:k0 + w], c == 0, c == 1, qres + [KT2.r], [ps.r])
                    op("dve", lambda: nc.vector.tensor_tensor(out=Pb.t[:nq, k0:k0 + w], in0=ps.t[:nq, 0:w], in1=Eb.t[:nq, k0:k0 + w],
                                                              op=ALU.mult), [ps.r, Eb.r], [Pb.r])
                    if mask is not None:
                        op("dve", lambda: nc.vector.tensor_tensor(out=Pb.t[:nq, k0:k0 + w], in0=Pb.t[:nq, k0:k0 + w],
                                                                  in1=mask.t[:nq, 0:w], op=ALU.mult), [Pb.r, mask.r], [Pb.r])
                op("dve", lambda: nc.vector.reduce_sum(out=nqv.t[:nq, :], in_=Pb.t[:nq, 0:nk], axis=AX.X), [Pb.r], [nqv.r])
                return (nq, nk, Pb, sm, tiles)

            def ml_B(st, eb, colres, vsrc, h, init, og_src, a_dst):
                nq, nk, Pb, sm, tiles = st
                nqv, an, den, rden, ssq, rs = sm
                po = psO.next()
                nb = (nk + 127) // 128
                for ti_, (k0, w, _) in enumerate(tiles):
                    vb = vring.next()
                    if w % 128 == 0:
                        dma("sq", vb.t[:, 0:w // 128, :], vsrc(k0, w).rearrange("(b p) f -> p b f", p=128), writes=[vb.r])
                    else:
                        dma("sq", vb.t[:w, 0, :], vsrc(k0, w), writes=[vb.r])
                    tb = pv_blocks(w)
                    pt = psT.next()
                    pv = bfv(pt)
                    pr = PTres[(k0 // 512) % NPG]
                    for i, (b0, bw) in enumerate(tb):
                        tr(pv[:bw, i * 128:i * 128 + nq], Pb.t[:nq, k0 + b0:k0 + b0 + bw], identb.t[:nq, :nq], [Pb.r, identb.r], [pt.r],
                           inc=(i == len(tb) - 1))
                    g0 = (k0 // 128)
                    if all(bw == 128 for (_, bw) in tb):
                        copy(K.alt(), PT.t[:, g0:g0 + len(tb), 0:nq], pv.rearrange("p (a b) -> p a b", b=128)[:, 0:len(tb), 0:nq], [pt.r], [pr])
                    else:
                        for i, (b0, bw) in enumerate(tb):
                            copy(K.alt(), PT.t[:bw, g0 + i, 0:nq], pv[:bw, i * 128:i * 128 + nq], [pt.r], [pr])
                    for i, (b0, bw) in enumerate(tb):
                        bi = g0 + i
                        op("pe", lambda: nc.tensor.matmul(po.t[:nq, 0:512], lhsT=PT.t[:bw, bi, 0:nq], rhs=vb.t[:bw, i, :],
                                                          start=(bi == 0), stop=(bi == nb - 1)),
                           [pr, vb.r], [po.r], inc=(i == len(tb) - 1))
                if init is not None:
                    op("dve", lambda: nc.vector.tensor_tensor(out=nqv.t[:nq, :], in0=nqv.t[:nq, :], in1=init[1].t[:nq, :], op=ALU.add),
                       [nqv.r, init[1].r], [nqv.r])
                op("dve", lambda: nc.vector.tensor_scalar(out=an.t[:nq, :], in0=nqv.t[:nq, :], scalar1=-1.0, scalar2=None, op0=ALU.mult),
                   [nqv.r], [an.r])
                op("dve", lambda: nc.vector.tensor_tensor(out=an.t[:nq, :], in0=an.t[:nq, :], in1=nqv.t[:nq, :], op=ALU.max),
                   [nqv.r, an.r], [an.r])
                op("dve", lambda: nc.vector.tensor_tensor(out=den.t[:nq, :], in0=an.t[:nq, :], in1=eb, op=ALU.max), [an.r] + colres, [den.r])
                op("dve", lambda: nc.vector.reciprocal(out=rden.t[:nq, :], in_=den.t[:nq, :]), [den.r], [rden.r])
                hb = hbr.next()
                if init is None:
                    op("act", lambda: nc.scalar.activation(out=hb.t[:nq, :], in_=po.t[:nq, 0:512], func=AF.Copy, scale=rden.t[:nq, :]),
                       [po.r, rden.r], [hb.r])
                else:
                    op("dve", lambda: nc.vector.tensor_tensor(out=hb.t[:nq, :], in0=po.t[:nq, 0:512], in1=init[0].t[:nq, :], op=ALU.add),
                       [po.r, init[0].r], [hb.r])
                    op("act", lambda: nc.scalar.activation(out=hb.t[:nq, :], in_=hb.t[:nq, :], func=AF.Copy, scale=rden.t[:nq, :]),
                       [hb.r, rden.r], [hb.r])
                op("act", lambda: nc.scalar.activation(out=junk5.t[:nq, :], in_=hb.t[:nq, :], func=AF.Square, accum_out=ssq.t[:nq, :]),
                   [hb.r], [junk5.r, ssq.r])
                rstd_from_ss(rs.t[:nq, :], ssq.t[:nq, :], 1.0 / 512, [ssq.r], [rs.r])
                og = ogr.next()
                dma("sq", og.t[:nq, :], og_src, writes=[og.r])
                op("dve", lambda: nc.vector.scalar_tensor_tensor(out=hb.t[:nq, :], in0=hb.t[:nq, :], scalar=rs.t[:nq, :],
                                                                 in1=gml.t[:nq, h * 512:(h + 1) * 512], op0=ALU.mult, op1=ALU.mult),
                   [hb.r, rs.r, gml.r], [hb.r])
                ast = astr.next()
                op("dve", lambda: nc.vector.tensor_tensor(out=ast.t[:nq, :], in0=hb.t[:nq, :], in1=og.t[:nq, :], op=ALU.mult),
                   [hb.r, og.r], [ast.r])
                dma("sq", a_dst, ast.t[:nq, :], reads=[ast.r])

            for h in range(4):
                dma("sq", KT2.t[:], KT_ml[h * 256:(h + 1) * 256, :].rearrange("(c p) n -> p c n", p=128), writes=[KT2.r])
                dma("sq", QT2.t[:], QT_ml[h * 256:(h + 1) * 256, 0:OWN].rearrange("(c p) n -> p c n", p=128), writes=[QT2.r])
                dma("sq", cbc.t[:], ROWS[0, h, :].partition_broadcast(128), writes=[cbc.r])
                prev = None
                cres_ = [negMown.r, ebown.r]
                for m in range(NOB):
                    tiles = [(kt * 512, 512, None) for kt in range(m)] + [(m * 512, 512, maskm)]
                    cur = ml_A(128, QT2.t[:, :, m * 128:(m + 1) * 128], [QT2.r], tiles, negMown.t[:, h, m:m + 1], cres_)
                    if prev is not None:
                        pm = prev[1]
                        ml_B(prev[0], ebown.t[:, h, pm:pm + 1], cres_, lambda k0, w: V_ml[k0:k0 + w, h * 512:(h + 1) * 512], h, None,
                             OG[pm * 128:(pm + 1) * 128, h * 512:(h + 1) * 512], A_s[pm * 128:(pm + 1) * 128, h * 512:(h + 1) * 512])
                    prev = (cur, m)
                pm = prev[1]
                ml_B(prev[0], ebown.t[:, h, pm:pm + 1], cres_, lambda k0, w: V_ml[k0:k0 + w, h * 512:(h + 1) * 512], h, None,
                     OG[pm * 128:(pm + 1) * 128, h * 512:(h + 1) * 512], A_s[pm * 128:(pm + 1) * 128, h * 512:(h + 1) * 512])
            colr = Ring([tuple(K.sb([128, 1], F32) for _ in range(8)) for _ in range(2)])
            C0f = Ring([K.sb([128, 2, 512], F32) for _ in range(1)]); C0b = K.sb([128, 2, 512], BF16)
            n0f = K.sb([128, 2], F32); n0b = K.sb([128, 2], BF16); n0row = K.sb([1, 256], F32)
            inter = Ring([K.sb([16, 512], F32) for _ in range(2)])
            kb16 = Ring([K.sb([16, 256], BF16) for _ in range(2)]); vb16 = Ring([K.sb([16, 512], BF16) for _ in range(2)])
            kg16 = Ring([K.sb([16, 256], BF16) for _ in range(2)]); wtb = K.sb([16, 1], BF16)
            cstr = Ring([K.sb([128, 512], F32) for _ in range(2)]); nst = K.sb([1, 256], F32)
            for sq_ in range(4):
                t0_ = 16 * sq_
                for h in range(4):
                    sh = sq_ * 4 + h
                    Mc, bc_, nM, ebc, m0b, wint, cc_, mlb16 = colr.next()
                    colres = [Mc.r]
                    dma("sq", KT2.t[:, :, 0:16], KT_ml_s[h * 256:(h + 1) * 256, t0_:t0_ + 16].rearrange("(c p) n -> p c n", p=128), writes=[KT2.r])
                    dma("sq", QT2.t[:, :, 0:16], QT_ml[h * 256:(h + 1) * 256, OWN + t0_:OWN + t0_ + 16].rearrange("(c p) n -> p c n", p=128),
                        writes=[QT2.r])
                    dma("sq", cbc.t[0:16, 0:16], ROWS_s[0, h, t0_:t0_ + 16].partition_broadcast(16), writes=[cbc.r])
                    dma("sq", Mc.t[0:16, :], ROWS_s[2, h, t0_:t0_ + 16].rearrange("(t o) -> t o", o=1), writes=[Mc.r])
                    dma("sq", bc_.t[0:16, :], ROWS_s[1, h, t0_:t0_ + 16].rearrange("(t o) -> t o", o=1), writes=[Mc.r])
                    dma("sq", cc_.t[0:16, :], ROWS_s[0, h, t0_:t0_ + 16].rearrange("(t o) -> t o", o=1), writes=[Mc.r])
                    dma("sq", m0b.t[:, :], m0_in[h, sq_:sq_ + 1].partition_broadcast(128), writes=[Mc.r])
                    dma("sq", mlb16.t[:, :], ROWS_s[2, h, t0_ + 15:t0_ + 16].partition_broadcast(128), writes=[Mc.r])
                    op("dve", lambda: nc.vector.tensor_scalar(out=nM.t[0:16, :], in0=Mc.t[0:16, :], scalar1=-1.0, scalar2=None, op0=ALU.mult),
                       colres, colres)
                    op("dve", lambda: nc.vector.tensor_tensor(out=ebc.t[0:16, :], in0=bc_.t[0:16, :], in1=Mc.t[0:16, :], op=ALU.add), colres, colres)
                    op("act", lambda: nc.scalar.activation(out=ebc.t[0:16, :], in_=ebc.t[0:16, :], func=AF.Exp, scale=-1.0), colres, colres)
                    op("dve", lambda: nc.vector.tensor_tensor(out=wint.t[0:16, :], in0=m0b.t[0:16, :], in1=Mc.t[0:16, :], op=ALU.subtract),
                       colres, colres)
                    op("act", lambda: nc.scalar.activation(out=wint.t[0:16, :], in_=wint.t[0:16, :], func=AF.Exp), colres, colres)
                    c0f = C0f.next()
                    dma("sq", c0f.t[:], C0_in[sh].rearrange("(c p) f -> p c f", p=128), writes=[c0f.r])
                    copy("dve", C0b.t[:], c0f.t[:], [c0f.r], [C0b.r])
                    for c_ in range(2):
                        dma("sq", n0f.t[:, c_:c_ + 1], n0_in[sh, c_ * 128:(c_ + 1) * 128].rearrange("(p o) -> p o", o=1), writes=[n0f.r])
                    copy("dve", n0b.t[:], n0f.t[:], [n0f.r], [n0b.r])
                    dma("sq", n0row.t[:], n0_in[sh:sh + 1, :], writes=[n0row.r])
                    ps = psA.next()
                    for c in range(2):
                        mm(ps.t[0:16, 0:512], QT2.t[:, c, 0:16], C0b.t[:, c, :], c == 0, c == 1, [QT2.r, C0b.r], [ps.r])
                    it = inter.next()
                    op("act", lambda: nc.scalar.activation(out=it.t[:, :], in_=ps.t[0:16, 0:512], func=AF.Copy, scale=wint.t[0:16, :]),
                       [ps.r] + colres, [it.r])
                    ps2 = psA.next()
                    for c in range(2):
                        mm(ps2.t[0:16, 0:1], QT2.t[:, c, 0:16], n0b.t[:, c:c + 1], c == 0, c == 1, [QT2.r, n0b.r], [ps2.r])
                    qn0w = Buf(None); qn0w.t = bc_.t
                    op("dve", lambda: nc.vector.tensor_tensor(out=bc_.t[0:16, :], in0=ps2.t[0:16, 0:1], in1=wint.t[0:16, :], op=ALU.mult),
                       [ps2.r] + colres, colres)
                    qn0w.r = Mc.r
                    st_ = ml_A(16, QT2.t[:, :, 0:16], [QT2.r], [(0, 16, tril16)], nM.t[0:16, :], colres)
                    ml_B(st_, ebc.t[0:16, :], colres, lambda k0, w: V_ml_s[t0_:t0_ + 16, h * 512:(h + 1) * 512], h, (it, qn0w),
                         OG[OWN + t0_:OWN + t0_ + 16, h * 512:(h + 1) * 512], A_s[OWN + t0_:OWN + t0_ + 16, h * 512:(h + 1) * 512])
                    kb, vb, kg = kb16.next(), vb16.next(), kg16.next()
                    dma("sq", kb.t[:], K_tm_s[t0_:t0_ + 16, h * 256:(h + 1) * 256], writes=[kb.r])
                    dma("sq", vb.t[:], V_ml_s[t0_:t0_ + 16, h * 512:(h + 1) * 512], writes=[vb.r])
                    op("dve", lambda: nc.vector.tensor_tensor(out=cc_.t[0:16, :], in0=cc_.t[0:16, :], in1=mlb16.t[0:16, :], op=ALU.subtract),
                       colres, colres)
                    op("act", lambda: nc.scalar.activation(out=cc_.t[0:16, :], in_=cc_.t[0:16, :], func=AF.Exp), colres, colres)
                    op("dve", lambda: nc.vector.tensor_tensor(out=m0b.t[:, :], in0=m0b.t[:, :], in1=mlb16.t[:, :], op=ALU.subtract), colres, colres)
                    op("act", lambda: nc.scalar.activation(out=m0b.t[:, :], in_=m0b.t[:, :], func=AF.Exp), colres, colres)
                    copy("dve", wtb.t[:], cc_.t[0:16, :], colres, [wtb.r])
                    op("dve", lambda: nc.vector.tensor_scalar(out=kg.t[:], in0=kb.t[:], scalar1=cc_.t[0:16, :], scalar2=None, op0=ALU.mult),
                       [kb.r] + colres, [kg.r])
                    for dc in range(2):
                        ps = psA.next()
                        mm(ps.t[:, 0:512], kg.t[:, dc * 128:(dc + 1) * 128], vb.t[:], True, True, [kg.r, vb.r], [ps.r])
                        cs_ = cstr.next()
                        op("dve", lambda: nc.vector.scalar_tensor_tensor(out=cs_.t[:], in0=c0f.t[:, dc, :], scalar=m0b.t[:, :], in1=ps.t[:, 0:512],
                                                                         op0=ALU.mult, op1=ALU.add), [c0f.r, ps.r] + colres, [cs_.r])
                        dma("sq", s_C[sh, dc * 128:(dc + 1) * 128, :], cs_.t[:], reads=[cs_.r])
                    ps = psA.next()
                    mm(ps.t[0:1, 0:256], wtb.t[:, :], kb.t[:], True, True, [wtb.r, kb.r], [ps.r])
                    op("dve", lambda: nc.vector.scalar_tensor_tensor(out=nst.t[:], in0=n0row.t[:], scalar=m0b.t[0:1, :], in1=ps.t[0:1, 0:256],
                                                                     op0=ALU.mult, op1=ALU.add), [n0row.r, ps.r] + colres, [nst.r])
                    dma("sq", s_n[sh:sh + 1, :], nst.t[:], reads=[nst.r])
        if stop_after <= 6:
            K.finish()
            return nc

        tiles_own = [(ti * 512, 512, [(0, 512, 0)]) for ti in range(NTO)] + [(OWN, 64, sample_groups)]

        with K.scope():
            aT = K.sb([128, 32, 512], BF16)
            yT = K.sb([128, 32, 512], F32)
            rowsb = Ring([K.sb([128, 4096], BF16) for _ in range(1)])
            rows_ring = Ring([K.sb([128, 4096], F32) for _ in range(1)])
            sq_ring = Ring([K.sb([128, 512], BF16) for _ in range(3)])
            rbc = K.sb([128, 512], F32)
            WL = WLoader(3)
            for (c0, T, groups) in tiles_own:
                prep_plain(A_s[c0:c0 + T, :], T, aT, rowsb)
                for cb in range(16):
                    wb, wv = WL.load(f"wo{cb}")
                    for cc in range(2):
                        ch = cb * 2 + cc
                        ps = psA.next()
                        for k in range(32):
                            mm(ps.t[:, 0:T], wv[:, k, cc * 128:(cc + 1) * 128], aT.t[:, k, 0:T], k == 0, k == 31, [wb.r, aT.r], [ps.r])
                        copy(K.alt(), yT.t[:, ch, 0:T], ps.t[:, 0:T], [ps.r], [yT.r])
                src = x_own[c0:c0 + T, :] if c0 < OWN else x_smp
                post(yT, T, G1, groups, src, X1[c0:c0 + T, :], rows_ring, sq_ring, rbc)
        if stop_after <= 7:
            K.finish()
            return nc

        with K.scope():
            uT = K.sb([128, 32, 512], BF16)
            facc = K.sb([128, 32, 512], F32)
            rows_ring = Ring([K.sb([128, 4096], F32) for _ in range(1)])
            sq_ring = Ring([K.sb([128, 512], BF16) for _ in range(2)])
            rbc = K.sb([128, 512], F32)
            ss = K.sb([128, 1], F32); rstd = K.sb([128, 1], F32)
            r32 = Ring([K.sb([128, 512], F32) for _ in range(2)])
            hT = K.sb([128, 4, 512], BF16)
            WL = WLoader(4)
            jk = View(facc.t[:, 0:4, :].rearrange("p a b -> p (a b)").bitcast(BF16), facc.r)
            for (c0, T, groups) in tiles_own:
                prep(X1[c0:c0 + T, :], T, A2, B2, groups, uT, rows_ring, jk, ss, rstd)
                for hp in range(NHB // 2):
                    w1p = [WL.load(f"f1_{2 * hp + j}") for j in range(2)]
                    w2p = [WL.load(f"f2_{2 * hp + j}") for j in range(2)]
                    for j in range(2):
                        w1b, w1 = w1p[j]
                        for cc in range(2):
                            ps = psA.next()
                            for k in range(32):
                                mm(ps.t[:, 0:T], w1[:, k, cc * 128:(cc + 1) * 128], uT.t[:, k, 0:T], k == 0, k == 31, [w1b.r, uT.r], [ps.r])
                            r_ = r32.next()
                            op("act", lambda: nc.scalar.activation(out=r_.t[:, 0:T], in_=ps.t[:, 0:T], func=AF.Relu), [ps.r], [r_.r])
                            op("dve", lambda: nc.vector.tensor_tensor(out=hT.t[:, 2 * j + cc, 0:T], in0=r_.t[:, 0:T], in1=r_.t[:, 0:T], op=ALU.mult),
                               [r_.r], [hT.r])
                    for oc in range(32):
                        ps = psA.next()
                        for kk in range(4):
                            j, k = divmod(kk, 2)
                            w2b, w2 = w2p[j]
                            mm(ps.t[:, 0:T], w2[:, k, oc * 128:(oc + 1) * 128], hT.t[:, kk, 0:T], kk == 0, kk == 3, [w2b.r, hT.r], [ps.r])
                        f_ = facc.t[:, oc, 0:T]
                        if hp == 0:
                            copy(K.alt(), f_, ps.t[:, 0:T], [ps.r], [facc.r])
                        else:
                            op("dve", lambda: nc.vector.tensor_tensor(out=f_, in0=ps.t[:, 0:T], in1=f_, op=ALU.add), [ps.r, facc.r], [facc.r])
                dst = y_own[c0:c0 + T, :] if c0 < OWN else y_smp
                post(facc, T, G2, groups, X1[c0:c0 + T, :], dst, rows_ring, sq_ring, rbc)
        K.finish()
    return nc


def _fm(v, n):
    return np.ascontiguousarray(np.asarray(v, np.float32).reshape(n, 128).T)


def host_inputs(inp, SEQ, PAST):
    f = lambda a: np.ascontiguousarray(np.asarray(a, np.float32))
    NBLK, OWN = SEQ // 128, SEQ // 4
    NOB, NQ = OWN // 128, OWN // 4 * 0 + OWN + 64
    inv = (10000.0 ** (-np.arange(32, dtype=np.float32) / 32)).astype(np.float32)

    def cs_table(pos):
        ang = pos.astype(np.float32)[:, None] * inv[None, :]
        return np.concatenate([np.cos(ang), np.sin(ang)], axis=1).astype(np.float32)

    shared = dict(
        w_ada=f(inp["w_ada"][0]), b_ada_fm=_fm(inp["b_ada"][0], 192),
        g_pre1_fm=_fm(inp["g_pre1"][0], 32), g_post1_fm=_fm(inp["g_post1"][0], 32),
        g_pre2_fm=_fm(inp["g_pre2"][0], 32), g_post2_fm=_fm(inp["g_post2"][0], 32),
        w_in=f(inp["w_in"][0]), b_ig=f(inp["b_ig"][0]).reshape(4, 1), b_fg=f(inp["b_fg"][0]).reshape(4, 1),
        g_mlnorm=f(inp["g_mlnorm"][0]).reshape(1, 2048), g_cq_fm=_fm(inp["g_cq"][0], 8),
        w_uq=f(inp["w_uq"][0]).reshape(1024, 3072), g_ckv=f(inp["g_ckv"][0]).reshape(1, 512),
        w_uk=f(inp["w_uk"][0]).reshape(512, 2048), w_uv=f(inp["w_uv"][0]).reshape(512, 2048),
        w_out=f(inp["w_out"][0]), w_ff1=f(inp["w_ff1"][0]), w_ff2=f(inp["w_ff2"][0]),
        ident=np.eye(128, dtype=np.float32), cs_all=cs_table(np.arange(SEQ)),
        tril16=np.tril(np.ones((16, 16), np.float32)),
    )
    xp, xs = f(inp["x_prompt"]), f(inp["x_sample"])
    maps = []
    kk = np.arange(512)[None, :]
    for c in range(8):
        b, j = c // 4, c % 4
        qq = (j * 128 + np.arange(128))[:, None]
        c5 = np.concatenate([f(inp["c_prompt"])[b:b + 1], f(inp["c_sample"])[4 * c:4 * c + 4]], axis=0)
        own_pos = (np.arange(NOB)[:, None] * 4 + j) * 128 + np.arange(128)[None, :]
        sel = np.zeros((NBLK, NOB), np.float32)
        sel[np.arange(NOB) * 4 + j, np.arange(NOB)] = 1.0
        m = dict(shared)
        m.update(
            x_all=xp[b], x_own=np.ascontiguousarray(xp[b].reshape(NBLK, 128, 4096)[j::4].reshape(OWN, 4096)),
            x_smp=np.ascontiguousarray(xs[4 * c:4 * c + 4].reshape(64, 4096)),
            c_T=np.ascontiguousarray(c5.T.reshape(32, 128, 5).transpose(1, 0, 2)),
            ckv_past=f(inp["cache_mla_ckv"][0, 4 * c:4 * c + 4]), kr_past=f(inp["cache_mla_krope"][0, 4 * c:4 * c + 4]),
            C0=f(inp["state_mlstm_C"][0, 4 * c:4 * c + 4]).reshape(16, 256, 512),
            n0=f(inp["state_mlstm_n"][0, 4 * c:4 * c + 4]).reshape(16, 256),
            m0T=np.ascontiguousarray(f(inp["state_mlstm_m"][0, 4 * c:4 * c + 4]).reshape(4, 4).T),
            cs_own=np.concatenate([cs_table(own_pos.reshape(-1)), cs_table(PAST + np.tile(np.arange(16), 4))], axis=0),
            bias_mla=np.where(kk // 64 <= qq // 64, 0.0, -1e9).astype(np.float32),
            mask_ml=(kk <= qq).astype(np.float32),
            sel=sel,
        )
        maps.append(m)
    return maps


def assemble(results, SEQ, PAST, BATCH=2):
    NBLK, OWN = SEQ // 128, SEQ // 4
    NOB = OWN // 128
    y_p = np.zeros((BATCH, SEQ, 4096), np.float32)
    y_s = np.zeros((32, 16, 4096), np.float32)
    p_ckv = np.zeros((1, BATCH, SEQ, 512), np.float32); p_kr = np.zeros((1, BATCH, SEQ, 64), np.float32)
    p_C = np.zeros((1, BATCH, 4, 256, 512), np.float32); p_n = np.zeros((1, BATCH, 4, 256), np.float32)
    p_m = np.zeros((1, BATCH, 4), np.float32)
    s_ckv = np.zeros((1, 32, 16, 512), np.float32); s_kr = np.zeros((1, 32, 16, 64), np.float32)
    s_C = np.zeros((1, 32, 4, 256, 512), np.float32); s_n = np.zeros((1, 32, 4, 256), np.float32)
    s_m = np.zeros((1, 32, 4), np.float32)
    for c, r in enumerate(results):
        b, j = c // 4, c % 4
        y_p[b].reshape(NBLK, 128, 4096)[j::4] = np.asarray(r["y_own"], np.float32).reshape(NOB, 128, 4096)
        y_s[4 * c:4 * c + 4] = np.asarray(r["y_smp"], np.float32).reshape(4, 16, 4096)
        if j == 0:
            p_ckv[0, b] = r["ckv_o"]; p_kr[0, b] = r["kr_o"]; p_C[0, b] = r["C_o"]; p_n[0, b] = r["n_o"]
            p_m[0, b] = np.asarray(r["m_o"]).reshape(4)
        s_ckv[0, 4 * c:4 * c + 4] = np.asarray(r["s_ckv"]).reshape(4, 16, 512)
        s_kr[0, 4 * c:4 * c + 4] = np.asarray(r["s_kr"]).reshape(4, 16, 64)
        s_C[0, 4 * c:4 * c + 4] = np.asarray(r["s_C"]).reshape(4, 4, 256, 512)
        s_n[0, 4 * c:4 * c + 4] = np.asarray(r["s_n"]).reshape(4, 4, 256)
        s_m[0, 4 * c:4 * c + 4] = np.asarray(r["s_m"]).reshape(4, 4).T
    return (y_p, y_s, p_ckv, p_kr, p_C, p_n, p_m, s_ckv, s_kr, s_C, s_n, s_m)


def kernel(**inputs):
    inp = {k: np.asarray(v) for k, v in inputs.items()}
    SEQ = inp["x_prompt"].shape[1]
    PAST = inp["cache_mla_ckv"].shape[2]
    nc = build_program(SEQ=SEQ, PAST=PAST)
    maps = host_inputs(inp, SEQ, PAST)
    res = run_bass_kernel_spmd(nc, maps, core_ids=list(range(8)))
    return assemble(res.results, SEQ, PAST, BATCH=inp["x_prompt"].shape[0])
```

```python
import contextlib
import numpy as np
import concourse.bass as bass
import concourse.mybir as mybir
from concourse.bass_utils import run_bass_kernel_spmd

F32 = mybir.dt.float32
BF16 = mybir.dt.bfloat16
ALU = mybir.AluOpType
AF = mybir.ActivationFunctionType
AX = mybir.AxisListType
EPS = 1e-6
MLA_SCALE = 192.0 ** -0.5


class Res:
    __slots__ = ("name", "w", "r")

    def __init__(self, name=None):
        self.name = name
        self.w = None
        self.r = []


class Buf:
    __slots__ = ("t", "r")

    def __init__(self, t):
        self.t = t
        self.r = Res()


class View:
    __slots__ = ("t", "r")

    def __init__(self, ap, res):
        self.t = ap
        self.r = res


class Ring:
    def __init__(self, bufs):
        self.b = bufs
        self.i = 0

    def next(self):
        b = self.b[self.i % len(self.b)]
        self.i += 1
        return b


class _Eng:
    W = 24000

    def __init__(self, K, name, handle):
        self.K, self.name, self.h = K, name, handle
        self.count = 0
        self.sems = []
        self.seen = {}

    def sem_for(self, idx):
        w = (idx - 1) // self.W
        while len(self.sems) <= w:
            self.sems.append(self.K.new_sem(f"{self.name}{len(self.sems)}"))
        return self.sems[w], (idx - 1) % self.W + 1

    def key(self, idx):
        return (self.name, (idx - 1) // self.W)


class _Dma(_Eng):
    NS = 8

    def __init__(self, K, name, handle):
        super().__init__(K, name, handle)
        self.slots = [K.new_sem(f"{name}s{i}") for i in range(self.NS)]

    def sem_for(self, idx):
        i = idx - 1
        return self.slots[i % self.NS], 16 * (i // self.NS + 1)

    def key(self, idx):
        return (self.name, (idx - 1) % self.NS)


class Kern:
    def __init__(self, nc, stack):
        self.nc = nc
        self.gstack = stack
        self.stacks = [stack]
        self.e = {
            "pe": _Eng(self, "pe", nc.tensor),
            "act": _Eng(self, "act", nc.scalar),
            "dve": _Eng(self, "dve", nc.vector),
            "pool": _Eng(self, "pool", nc.gpsimd),
            "sq": _Dma(self, "sq", nc.sync),
            "gq": _Dma(self, "gq", nc.gpsimd),
        }
        self._alt = 0
        self.nsb = 0

    def new_sem(self, name):
        return self.gstack.enter_context(self.nc.semaphore(name))

    def sb(self, shape, dt, name=None):
        self.nsb += 1
        t = self.stacks[-1].enter_context(self.nc.sbuf_tensor(f"t{self.nsb}_{name or 'x'}", list(shape), dt))
        return Buf(t)

    def ps(self, shape, dt, name):
        return Buf(self.gstack.enter_context(self.nc.psum_tensor(name, list(shape), dt)))

    def alt(self):
        self._alt ^= 1
        return "act" if self._alt else "dve"

    @contextlib.contextmanager
    def scope(self):
        st = contextlib.ExitStack()
        self.stacks.append(st)
        try:
            yield
            self.barrier()
        finally:
            self.stacks.pop()
            st.close()

    def _wait(self, E, F, n):
        if E is F and E.name == "pe":
            return
        k = F.key(n)
        if E.seen.get(k, 0) >= n:
            return
        sem, val = F.sem_for(n)
        E.h.wait_ge(sem, val)
        E.seen[k] = n

    def op(self, en, fn, reads=(), writes=(), inc=True):
        E = self.e[en]
        idx = E.count + 1
        isd = isinstance(E, _Dma)
        for r in reads:
            if r.w is not None:
                F, n = r.w
                if not (F is E and n >= idx):
                    self._wait(E, F, n)
        for r in writes:
            if r.w is not None:
                F, n = r.w
                if not (F is E and n >= idx):
                    self._wait(E, F, n)
            for (F, n) in r.r:
                if not (F is E and n >= idx):
                    self._wait(E, F, n)
        if isd and idx > E.NS:
            self._wait(E, E, idx - E.NS)
        ins = fn()
        if inc or isd:
            E.count = idx
            sem, _ = E.sem_for(idx)
            ins.then_inc(sem, 16 if isd else 1)
        for r in writes:
            r.w = (E, idx)
            r.r = []
        for r in reads:
            r.r.append((E, idx))
        return ins

    def barrier(self, with_gq=False):
        for en in ("pe", "act", "dve", "sq"):
            E = self.e[en]
            for fn_ in ("pe", "act", "dve", "pool"):
                F = self.e[fn_]
                if F.count and F is not E:
                    self._wait(E, F, F.count)
            for qn in (("sq", "gq") if with_gq else ("sq",)):
                Q = self.e[qn]
                for i in range(max(0, Q.count - Q.NS), Q.count):
                    self._wait(E, Q, i + 1)

    def finish(self):
        self.barrier(with_gq=True)
        self.nc.kcounts = {k: e.count for k, e in self.e.items()}


C_Q, C_K, C_V, C_O, C_IG, C_FG, C_CQ, C_CKV, C_KR = 0, 1024, 2048, 4096, 6144, 6148, 6152, 7176, 7688


def build_program(SEQ=8192, PAST=1024, debug=False, stop_after=99, NHB=64):
    nc = bass.Bass("TRN2", target_bir_lowering=False)
    NBLK, NTA, OWN = SEQ // 128, SEQ // 512, SEQ // 4
    NOB, NTO, NQ = OWN // 128, OWN // 512, OWN + 64
    NKS = PAST + 16
    NPT = PAST // 512

    def din(name, shape, dt=F32):
        return nc.dram_tensor(name, list(shape), dt, kind="ExternalInput").ap()

    def dout(name, shape, dt=F32):
        return nc.dram_tensor(name, list(shape), dt, kind="ExternalOutput").ap()

    def dscr(name, shape, dt):
        return nc.dram_tensor(name, list(shape), dt, kind="ExternalOutput" if debug else "Internal").ap()

    x_all = din("x_all", [SEQ, 4096]); x_own = din("x_own", [OWN, 4096]); x_smp = din("x_smp", [64, 4096])
    c_T = din("c_T", [128, 32, 5])
    ckv_past = din("ckv_past", [4, PAST, 512]); kr_past = din("kr_past", [4, PAST, 64])
    C0_in = din("C0", [16, 256, 512]); n0_in = din("n0", [16, 256]); m0_in = din("m0T", [4, 4])
    w_ada = din("w_ada", [4096, 24576]); b_ada_fm = din("b_ada_fm", [128, 192])
    gpre1_d = din("g_pre1_fm", [128, 32]); gpost1_d = din("g_post1_fm", [128, 32])
    gpre2_d = din("g_pre2_fm", [128, 32]); gpost2_d = din("g_post2_fm", [128, 32])
    w_in = din("w_in", [4096, 7752]); b_ig_d = din("b_ig", [4, 1]); b_fg_d = din("b_fg", [4, 1])
    g_ml_d = din("g_mlnorm", [1, 2048]); g_cq_d = din("g_cq_fm", [128, 8]); w_uq = din("w_uq", [1024, 3072])
    g_ckv_d = din("g_ckv", [1, 512]); w_uk = din("w_uk", [512, 2048]); w_uv = din("w_uv", [512, 2048])
    w_out = din("w_out", [4096, 4096]); w_ff1 = din("w_ff1", [4096, 16384]); w_ff2 = din("w_ff2", [16384, 4096])
    ident_d = din("ident", [128, 128]); cs_all = din("cs_all", [SEQ, 64]); cs_own = din("cs_own", [NQ, 64])
    bias_mla_d = din("bias_mla", [128, 512]); mask_ml_d = din("mask_ml", [128, 512])
    sel_d = din("sel", [NBLK, NOB]); tril16_d = din("tril16", [16, 16])
    y_own = dout("y_own", [OWN, 4096]); y_smp = dout("y_smp", [64, 4096])
    ckv_o = dout("ckv_o", [SEQ, 512]); kr_o = dout("kr_o", [SEQ, 64])
    C_o = dout("C_o", [4, 256, 512]); n_o = dout("n_o", [4, 256]); m_o = dout("m_o", [4, 1])
    s_ckv = dout("s_ckv", [64, 512]); s_kr = dout("s_kr", [64, 64])
    s_C = dout("s_C", [16, 256, 512]); s_n = dout("s_n", [16, 256]); s_m = dout("s_m", [4, 4])
    NWB = 19 + 19 + 16 + 128
    WB_A = dscr("WB_A", [54, 128, 8192], BF16)
    WB_F1 = dscr("WB_F1", [64, 128, 8192], BF16)
    WB_F2 = dscr("WB_F2", [64, 128, 8192], BF16)

    class _WB:
        def __getitem__(self, i):
            if i < 54:
                return WB_A[i]
            j = i - 54
            return WB_F1[j // 2] if j % 2 == 0 else WB_F2[j // 2]
    WBLK = _WB()
    KT_ml = dscr("KT_ml", [1024, SEQ], BF16); K_tm = dscr("K_tm", [SEQ, 1024], BF16); V_ml = dscr("V_ml", [SEQ, 2048], BF16)
    GATES = dscr("GATES", [8, SEQ], F32); ROWS = dscr("ROWS", [3, 4, SEQ], F32)
    KT_mla = dscr("KT_mla", [2048, SEQ], BF16); KRT = dscr("KRT", [64, SEQ], BF16); V_mla = dscr("V_mla", [SEQ, 2048], BF16)
    QT_ml = dscr("QT_ml", [1024, NQ], BF16); OG = dscr("OG", [NQ, 2048], BF16)
    QN_T = dscr("QN_T", [2048, NQ], BF16); QR_T = dscr("QR_T", [1024, NQ], BF16)
    A_s = dscr("A_s", [NQ, 4096], BF16); X1 = dscr("X1", [NQ, 4096], F32)
    KT_ml_s = dscr("KT_ml_s", [1024, 64], BF16); K_tm_s = dscr("K_tm_s", [64, 1024], BF16); V_ml_s = dscr("V_ml_s", [64, 2048], BF16)
    GATES_s = dscr("GATES_s", [8, 64], F32); ROWS_s = dscr("ROWS_s", [3, 4, 64], F32)
    KT_mla_s = dscr("KT_mla_s", [2048, 64], BF16); KRT_s = dscr("KRT_s", [64, 64], BF16); V_mla_s = dscr("V_mla_s", [64, 2048], BF16)
    KT_mla_p = dscr("KT_mla_p", [4, 2048, PAST], BF16); KRT_p = dscr("KRT_p", [4, 64, PAST], BF16)
    V_mla_p = dscr("V_mla_p", [4, PAST, 2048], BF16)
    junk_ckv = dscr("junk_ckv", [PAST, 512], F32); junk_kr = dscr("junk_kr", [PAST, 64], F32)

    with contextlib.ExitStack() as gst:
        K = Kern(nc, gst)
        op = K.op

        def dma(q, out, in_, reads=(), writes=()):
            h = nc.sync if q == "sq" else nc.gpsimd
            return op(q, lambda: h.dma_start(out=out, in_=in_, allow_slow_non_contiguous=True), reads, writes)

        def copy(en, out, in_, reads, writes):
            if en == "act":
                return op("act", lambda: nc.scalar.copy(out=out, in_=in_), reads, writes)
            h = nc.vector if en == "dve" else nc.gpsimd
            return op(en, lambda: h.tensor_copy(out=out, in_=in_), reads, writes)

        def mm(out, lhsT, rhs, start, stop, reads, writes, inc=None):
            return op("pe", lambda: nc.tensor.matmul(out, lhsT=lhsT, rhs=rhs, start=start, stop=stop),
                      reads, writes, inc=(stop if inc is None else inc))

        def tr(out, in_, idn, reads, writes, inc=True):
            return op("pe", lambda: nc.tensor.transpose(out, in_, idn), reads, writes, inc=inc)

        psA = Ring([K.ps([128, 512], F32, f"psA{i}") for i in range(3)])
        psT = Ring([K.ps([128, 512], F32, f"psT{i}") for i in range(2)])
        psO = Ring([K.ps([128, 512], F32, f"psO{i}") for i in range(2)])
        psX = K.ps([128, 512], F32, "psX")

        def bfv(ps):
            return ps.t[:].bitcast(BF16)

        ident = K.sb([128, 128], F32, "ident"); identb = K.sb([128, 128], BF16, "identb")
        ones_bf = K.sb([128, 128], BF16, "ones_bf")
        mods = K.sb([128, 192, 5], F32, "mods")
        A1 = K.sb([128, 32, 5], F32, "A1"); G1 = K.sb([128, 32, 5], F32, "G1")
        A2 = K.sb([128, 32, 5], F32, "A2"); G2 = K.sb([128, 32, 5], F32, "G2")
        gfm = K.sb([128, 4, 32], F32, "gfm")
        gcq = K.sb([128, 8], F32, "gcq")
        bada = K.sb([128, 192], F32, "bada")
        bigfg = K.sb([4, 2], F32, "bigfg")
        Mown = K.sb([128, 4, NOB], F32, "Mown"); bown = K.sb([128, 4, NOB], F32, "bown")
        negMown = K.sb([128, 4, NOB], F32, "negMown"); ebown = K.sb([128, 4, NOB], F32, "ebown")
        dma("sq", ident.t[:], ident_d, writes=[ident.r])
        dma("gq", identb.t[:], ident_d, writes=[identb.r])
        op("pool", lambda: nc.gpsimd.memset(ones_bf.t[:], 1.0), writes=[ones_bf.r])
        eps_t = K.sb([128, 1], F32, "eps_t")
        op("pool", lambda: nc.gpsimd.memset(eps_t.t[:], EPS), writes=[eps_t.r])
        for i, g in enumerate((gpre1_d, gpost1_d, gpre2_d, gpost2_d)):
            dma("sq", gfm.t[:, i, :], g, writes=[gfm.r])
        dma("sq", gcq.t[:], g_cq_d, writes=[gcq.r])
        dma("sq", bada.t[:], b_ada_fm, writes=[bada.r])
        dma("sq", bigfg.t[:, 0:1], b_ig_d, writes=[bigfg.r])
        dma("sq", bigfg.t[:, 1:2], b_fg_d, writes=[bigfg.r])

        wres = {}
        wshape = {}
        wnext = [0]

        def wconv(name, src3, KC, ncols):
            i = wnext[0]
            wnext[0] += 1
            assert i < NWB and KC * ncols <= 8192
            r = Res(name)
            dst = WBLK[i][:, 0:KC * ncols].rearrange("p (k c) -> p k c", c=ncols)
            dma("gq", dst, src3, writes=[r])
            wres[name] = (i, r)
            wshape[name] = (KC, ncols)

        def w3(src, c0, ncols):
            return src.rearrange("(k p) c -> p k c", p=128)[:, :, c0:c0 + ncols]

        def conv_stage1():
            for i in range(4):
                wconv(f"k{i}", w3(w_in, C_K + 256 * i, 256), 32, 256)
            for i in range(8):
                wconv(f"v{i}", w3(w_in, C_V + 256 * i, 256), 32, 256)
            for i in range(2):
                wconv(f"ckv{i}", w3(w_in, C_CKV + 256 * i, 256), 32, 256)
            wconv("ig", w3(w_in, C_IG, 4), 32, 4)
            wconv("fg", w3(w_in, C_FG, 4), 32, 4)
            wconv("kr", w3(w_in, C_KR, 64), 32, 64)
            wconv("wuk", w3(w_uk, 0, 2048), 4, 2048)
            wconv("wuv", w3(w_uv, 0, 2048), 4, 2048)

        def conv_rest():
            for i in range(4):
                wconv(f"q{i}", w3(w_in, C_Q + 256 * i, 256), 32, 256)
            for i in range(8):
                wconv(f"o{i}", w3(w_in, C_O + 256 * i, 256), 32, 256)
            for i in range(4):
                wconv(f"cq{i}", w3(w_in, C_CQ + 256 * i, 256), 32, 256)
            for i in range(3):
                wconv(f"uq{i}", w3(w_uq, 1024 * i, 1024), 8, 1024)
            for i in range(16):
                wconv(f"wo{i}", w3(w_out, 256 * i, 256), 32, 256)
            for i in range(64):
                wconv(f"f1_{i}", w3(w_ff1, 256 * i, 256), 32, 256)
                wconv(f"f2_{i}", w_ff2[256 * i:256 * (i + 1), :].rearrange("(k p) c -> p k c", p=128), 2, 4096)

        class WLoader:
            def __init__(self, nbuf):
                self.ring = Ring([K.sb([128, 8192], BF16) for _ in range(nbuf)])

            def load(self, name, buf=None):
                i, r = wres[name]
                KC, ncols = wshape[name]
                b = buf or self.ring.next()
                dma("sq", b.t[:, 0:KC * ncols], WBLK[i][:, 0:KC * ncols], reads=[r], writes=[b.r])
                return b, b.t[:, 0:KC * ncols].rearrange("p (k c) -> p k c", c=ncols)

        def rstd_from_ss(out, ss, inv_n, reads, writes):
            P = out.shape[0]
            op("act", lambda: nc.scalar.activation(out=out, in_=ss, func=AF.Sqrt, bias=eps_t.t[0:P, :], scale=inv_n),
               list(reads) + [eps_t.r], writes)
            op("dve", lambda: nc.vector.reciprocal(out=out, in_=out), list(writes), writes)

        sample_groups = [(16 * s, 16, 1 + s) for s in range(4)]

        def prep(src, T, Amod, Bmod, groups, uT, rows_ring, junk, ss, rstd):
            nsub = (T + 127) // 128
            for s in range(nsub):
                rows = min(128, T - s * 128)
                xt = rows_ring.next()
                dma("sq", xt.t[:rows, :], src[s * 128:s * 128 + rows, :], writes=[xt.r])
                op("act", lambda: nc.scalar.activation(out=junk.t[:rows, :], in_=xt.t[:rows, :], func=AF.Square,
                                                       accum_out=ss.t[:rows, :]), [xt.r], [junk.r, ss.r])
                rstd_from_ss(rstd.t[:rows, :], ss.t[:rows, :], 1.0 / 4096, [ss.r], [rstd.r])
                op("act", lambda: nc.scalar.activation(out=xt.t[:rows, :], in_=xt.t[:rows, :], func=AF.Copy,
                                                       scale=rstd.t[:rows, :]), [rstd.r, xt.r], [xt.r])
                for g in range(8):
                    pt = psT.next()
                    for i in range(4):
                        k = g * 4 + i
                        tr(pt.t[:, i * 128:i * 128 + rows], xt.t[:rows, k * 128:(k + 1) * 128], ident.t[:rows, :rows],
                           [xt.r, ident.r], [pt.r], inc=(i == 3))
                    for i in range(4):
                        k = g * 4 + i
                        for (c0, n, r) in groups:
                            lo, hi = max(c0, s * 128), min(c0 + n, s * 128 + rows)
                            if lo >= hi:
                                continue
                            o_ = uT.t[:, k, lo:hi]
                            i_ = pt.t[:, i * 128 + lo - s * 128:i * 128 + hi - s * 128]
                            a_, b_ = Amod.t[:, k, r:r + 1], Bmod[:, k, r:r + 1]
                            if K.alt() == "act":
                                op("act", lambda: nc.scalar.activation(out=o_, in_=i_, func=AF.Identity, bias=b_, scale=a_),
                                   [pt.r, Amod.r, mods.r], [uT.r])
                            else:
                                op("dve", lambda: nc.vector.tensor_scalar(out=o_, in0=i_, scalar1=a_, scalar2=b_,
                                                                          op0=ALU.mult, op1=ALU.add),
                                   [pt.r, Amod.r, mods.r], [uT.r])

        def prep_plain(src, T, aT, rowsb_ring):
            nsub = (T + 127) // 128
            for s in range(nsub):
                rows = min(128, T - s * 128)
                xt = rowsb_ring.next()
                dma("sq", xt.t[:rows, :], src[s * 128:s * 128 + rows, :], writes=[xt.r])
                for g in range(4):
                    pt = psT.next()
                    pv = bfv(pt)
                    for i in range(8):
                        k = g * 8 + i
                        tr(pv[:, i * 128:i * 128 + rows], xt.t[:rows, k * 128:(k + 1) * 128], identb.t[:rows, :rows],
                           [xt.r, identb.r], [pt.r], inc=(i == 7))
                    o_ = aT.t[:, g * 8:(g + 1) * 8, s * 128:s * 128 + rows]
                    i_ = pv.rearrange("p (a b) -> p a b", b=128)[:, :, 0:rows]
                    copy(K.alt(), o_, i_, [pt.r], [aT.r])

        def post(yT, T, Gmod, groups, res_src, dst, rows_ring, sq_ring, rbc):
            for k in range(32):
                sqb = sq_ring.next()
                op("act", lambda: nc.scalar.activation(out=sqb.t[:, 0:T], in_=yT.t[:, k, 0:T], func=AF.Square),
                   [yT.r], [sqb.r])
                mm(psX.t[:, 0:T], ones_bf.t[:], sqb.t[:, 0:T], k == 0, k == 31, [ones_bf.r, sqb.r], [psX.r], inc=True)
            rstd_from_ss(rbc.t[:, 0:T], psX.t[:, 0:T], 1.0 / 4096, [psX.r], [rbc.r])
            for k in range(32):
                for (c0, n, r) in groups:
                    y_ = yT.t[:, k, c0:c0 + n]
                    op("dve", lambda: nc.vector.scalar_tensor_tensor(out=y_, in0=y_, scalar=Gmod.t[:, k, r:r + 1],
                                                          in1=rbc.t[:, c0:c0 + n], op0=ALU.mult, op1=ALU.mult),
                       [yT.r, Gmod.r, rbc.r], [yT.r])
            nsub = (T + 127) // 128
            for s in range(nsub):
                rows = min(128, T - s * 128)
                xr = rows_ring.next()
                dma("sq", xr.t[:rows, :], res_src[s * 128:s * 128 + rows, :], writes=[xr.r])
                for g in range(8):
                    pt = psT.next()
                    for i in range(4):
                        tr(pt.t[:rows, i * 128:(i + 1) * 128], yT.t[:, g * 4 + i, s * 128:s * 128 + rows], ident.t[:],
                           [yT.r, ident.r], [pt.r], inc=(i == 3))
                    x_ = xr.t[:rows, g * 512:(g + 1) * 512]
                    op("dve", lambda: nc.vector.tensor_tensor(out=x_, in0=pt.t[:rows, 0:512], in1=x_, op=ALU.add),
                       [pt.r, xr.r], [xr.r])
                dma("sq", dst[s * 128:s * 128 + rows, :], xr.t[:rows, :], reads=[xr.r])

        conv_stage1()
        with K.scope():
            cT = K.sb([128, 32, 5], F32); cs = K.sb([128, 32, 5], F32)
            dma("sq", cT.t[:], c_T, writes=[cT.r])
            op("act", lambda: nc.scalar.activation(out=cs.t[:], in_=cT.t[:], func=AF.Silu), [cT.r], [cs.r])
            wr = Ring([K.sb([128, 32, 512], F32) for _ in range(2)])
            t5r = Ring([K.sb([8, 512], F32) for _ in range(2)])
            wav = w_ada.rearrange("(k p) c -> p k c", p=128)
            conv_rest()
            for cb in range(48):
                wb = wr.next()
                dma("sq", wb.t[:], wav[:, :, cb * 512:(cb + 1) * 512], writes=[wb.r])
                ps = psA.next()
                for k in range(32):
                    mm(ps.t[0:5, 0:512], cs.t[:, k, :], wb.t[:, k, :], k == 0, k == 31, [cs.r, wb.r], [ps.r])
                t5 = t5r.next()
                copy("act", t5.t[0:5, :], ps.t[0:5, 0:512], [ps.r], [t5.r])
                pt = psT.next()
                for i in range(4):
                    tr(pt.t[:, i * 5:(i + 1) * 5], t5.t[0:5, i * 128:(i + 1) * 128], ident.t[0:5, 0:5],
                       [t5.r, ident.r], [pt.r], inc=(i == 3))
                op("dve", lambda: nc.vector.tensor_tensor(
                    out=mods.t[:, 4 * cb:4 * cb + 4, :], in0=pt.t[:, 0:20].rearrange("p (a b) -> p a b", b=5),
                    in1=bada.t[:, 4 * cb:4 * cb + 4, None].to_broadcast([128, 4, 5]), op=ALU.add),
                   [pt.r, bada.r], [mods.r])

            def bc5(i):
                return gfm.t[:, i, :, None].to_broadcast([128, 32, 5])
            op("dve", lambda: nc.vector.scalar_tensor_tensor(out=A1.t[:], in0=mods.t[:, 32:64, :], scalar=1.0, in1=bc5(0),
                                                             op0=ALU.add, op1=ALU.mult), [mods.r, gfm.r], [A1.r])
            op("dve", lambda: nc.vector.tensor_tensor(out=G1.t[:], in0=mods.t[:, 64:96, :], in1=bc5(1), op=ALU.mult),
               [mods.r, gfm.r], [G1.r])
            op("dve", lambda: nc.vector.scalar_tensor_tensor(out=A2.t[:], in0=mods.t[:, 128:160, :], scalar=1.0, in1=bc5(2),
                                                             op0=ALU.add, op1=ALU.mult), [mods.r, gfm.r], [A2.r])
            op("dve", lambda: nc.vector.tensor_tensor(out=G2.t[:], in0=mods.t[:, 160:192, :], in1=bc5(3), op=ALU.mult),
               [mods.r, gfm.r], [G2.r])
        B1 = mods.t[:, 0:32, :]
        B2 = mods.t[:, 96:128, :]
        if stop_after <= 0:
            K.finish()
            return nc

        def mla_up(ckv32, kro, T, wuk, wuv, dKT, dKRT, dV, c0, L):
            nsub = (T + 127) // 128
            rws = [min(128, T - s * 128) for s in range(nsub)]
            op("dve", lambda: nc.vector.tensor_copy(out=L["ckvb"].t[:, 0:nsub, :], in_=ckv32.t[:, 0:nsub, :]),
               [ckv32.r], [L["ckvb"].r])
            op("dve", lambda: nc.vector.tensor_copy(out=L["krb"].t[:, 0:nsub, :], in_=kro.t[:, 0:nsub, :]),
               [kro.r], [L["krb"].r])
            for c in range(4):
                pt = psT.next()
                pv = bfv(pt)
                for s in range(nsub):
                    tr(pv[:, s * 128:s * 128 + rws[s]], L["ckvb"].t[:rws[s], s, c * 128:(c + 1) * 128],
                       identb.t[:rws[s], :rws[s]], [L["ckvb"].r, identb.r], [pt.r], inc=(s == nsub - 1))
                copy(K.alt(), L["ckvT"].t[:, c, 0:T], pv[:, 0:T], [pt.r], [L["ckvT"].r])
            pt = psT.next()
            pv = bfv(pt)
            for s in range(nsub):
                tr(pv[0:64, s * 128:s * 128 + rws[s]], L["krb"].t[:rws[s], s, :], identb.t[:rws[s], :rws[s]],
                   [L["krb"].r, identb.r], [pt.r], inc=(s == nsub - 1))
            copy(K.alt(), L["krT"].t[:, 0:T], pv[0:64, 0:T], [pt.r], [L["krT"].r])
            dma("sq", dKRT[:, c0:c0 + T], L["krT"].t[:, 0:T], reads=[L["krT"].r])
            for h in range(16):
                ps = psA.next()
                for c in range(4):
                    mm(ps.t[:, 0:T], wuk[1][:, c, h * 128:(h + 1) * 128], L["ckvT"].t[:, c, 0:T], c == 0, c == 3,
                       [wuk[0].r, L["ckvT"].r], [ps.r])
                st = L["st512"].next()
                copy(K.alt(), st.t[:, 0:T], ps.t[:, 0:T], [ps.r], [st.r])
                dma("sq", dKT[h * 128:(h + 1) * 128, c0:c0 + T], st.t[:, 0:T], reads=[st.r])
            vst = L["vst"]
            for s in range(nsub):
                for cb in range(4):
                    ps = psA.next()
                    for c in range(4):
                        mm(ps.t[:rws[s], 0:512], L["ckvT"].t[:, c, s * 128:s * 128 + rws[s]], wuv[1][:, c, cb * 512:(cb + 1) * 512],
                           c == 0, c == 3, [wuv[0].r, L["ckvT"].r], [ps.r])
                    copy(K.alt(), vst.t[:rws[s], s, cb * 512:(cb + 1) * 512], ps.t[:rws[s], 0:512], [ps.r], [vst.r])
            if T % 128 == 0:
                dma("sq", dV[c0:c0 + T, :].rearrange("(s p) f -> p s f", p=128), vst.t[:, 0:nsub, :], reads=[vst.r])
            else:
                dma("sq", dV[c0:c0 + T, :], vst.t[:T, 0, :], reads=[vst.r])

        def tm_out(dst, c0, T, buf, width):
            if T % 128 == 0:
                dma("sq", dst[c0:c0 + T, :].rearrange("(s p) f -> p s f", p=128), buf.t[:, 0:T // 128, 0:width], reads=[buf.r])
            else:
                dma("sq", dst[c0:c0 + T, :], buf.t[:T, 0, 0:width], reads=[buf.r])

        def tm_in(buf, src, c0, T, width):
            if T % 128 == 0:
                dma("sq", buf.t[:, 0:T // 128, 0:width], src[c0:c0 + T, :].rearrange("(s p) f -> p s f", p=128), writes=[buf.r])
            else:
                dma("sq", buf.t[:T, 0, 0:width], src[c0:c0 + T, :], writes=[buf.r])

        def rope_tm(x, cst, nsub, rows, L, out):
            x1, x2 = x.t[:rows, 0:nsub, 0:32], x.t[:rows, 0:nsub, 32:64]
            co, si = cst.t[:rows, 0:nsub, 0:32], cst.t[:rows, 0:nsub, 32:64]
            t = [L["rt"].t[:rows, i, 0:nsub, :] for i in range(4)]
            rr = [x.r, cst.r]
            op("dve", lambda: nc.vector.tensor_tensor(out=t[0], in0=x1, in1=co, op=ALU.mult), rr, [L["rt"].r])
            op("dve", lambda: nc.vector.tensor_tensor(out=t[1], in0=x2, in1=si, op=ALU.mult), rr, [L["rt"].r])
            op("dve", lambda: nc.vector.tensor_tensor(out=t[2], in0=x2, in1=co, op=ALU.mult), rr, [L["rt"].r])
            op("dve", lambda: nc.vector.tensor_tensor(out=t[3], in0=x1, in1=si, op=ALU.mult), rr, [L["rt"].r])
            op("dve", lambda: nc.vector.tensor_tensor(out=out.t[:rows, 0:nsub, 0:32], in0=t[0], in1=t[1], op=ALU.subtract),
               [L["rt"].r], [out.r])
            op("dve", lambda: nc.vector.tensor_tensor(out=out.t[:rows, 0:nsub, 32:64], in0=t[2], in1=t[3], op=ALU.add),
               [L["rt"].r], [out.r])

        def kv_side(uT, T, c0, D, WL, wuk, wuv, L, cs_src):
            nsub = (T + 127) // 128
            rws = [min(128, T - s * 128) for s in range(nsub)]
            ktm = L["ktm"]
            for cb in range(4):
                wb, wv = WL.load(f"k{cb}")
                for cc in range(2):
                    ch = cb * 2 + cc
                    ps = psA.next()
                    for k in range(32):
                        mm(ps.t[:, 0:T], wv[:, k, cc * 128:(cc + 1) * 128], uT.t[:, k, 0:T], k == 0, k == 31, [wb.r, uT.r], [ps.r])
                    st = L["st512"].next()
                    if K.alt() == "act":
                        op("act", lambda: nc.scalar.activation(out=st.t[:, 0:T], in_=ps.t[:, 0:T], func=AF.Copy, scale=0.0625),
                           [ps.r], [st.r])
                    else:
                        op("dve", lambda: nc.vector.tensor_scalar(out=st.t[:, 0:T], in0=ps.t[:, 0:T], scalar1=0.0625,
                                                                  scalar2=None, op0=ALU.mult), [ps.r], [st.r])
                    dma("sq", D["KT_ml"][ch * 128:(ch + 1) * 128, c0:c0 + T], st.t[:, 0:T], reads=[st.r])
                    pt = psT.next()
                    pv = bfv(pt)
                    for s in range(nsub):
                        tr(pv[:rws[s], s * 128:(s + 1) * 128], st.t[:, s * 128:s * 128 + rws[s]], identb.t[:],
                           [st.r, identb.r], [pt.r], inc=(s == nsub - 1))
                    rmax = rws[0]
                    copy(K.alt(), ktm.t[:rmax, 0:nsub, ch * 128:(ch + 1) * 128],
                         pv.rearrange("p (a b) -> p a b", b=128)[:rmax, 0:nsub, :], [pt.r], [ktm.r])
            tm_out(D["K_tm"], c0, T, ktm, 1024)
            vst = L["vst"]
            for cb in range(8):
                wb, wv = WL.load(f"v{cb}")
                for s in range(nsub):
                    ps = psA.next()
                    for k in range(32):
                        mm(ps.t[:rws[s], 0:256], uT.t[:, k, s * 128:s * 128 + rws[s]], wv[:, k, :], k == 0, k == 31, [wb.r, uT.r], [ps.r])
                    copy(K.alt(), vst.t[:rws[s], s, cb * 256:(cb + 1) * 256], ps.t[:rws[s], 0:256], [ps.r], [vst.r])
            tm_out(D["V_ml"], c0, T, vst, 2048)
            for gi, nm in enumerate(("ig", "fg")):
                wb, wv = WL.load(nm)
                ps = psA.next()
                for k in range(32):
                    mm(ps.t[0:4, 0:T], wv[:, k, :], uT.t[:, k, 0:T], k == 0, k == 31, [wb.r, uT.r], [ps.r])
                gs = L["gst"].next()
                op("act", lambda: nc.scalar.activation(out=gs.t[0:4, 0:T], in_=ps.t[0:4, 0:T], func=AF.Identity,
                                                       bias=bigfg.t[:, gi:gi + 1], scale=1.0), [ps.r, bigfg.r], [gs.r])
                dma("sq", D["GATES"][4 * gi:4 * gi + 4, c0:c0 + T], gs.t[0:4, 0:T], reads=[gs.r])
            ckv32 = L["ckv32"]
            for cb in range(2):
                wb, wv = WL.load(f"ckv{cb}")
                for s in range(nsub):
                    ps = psA.next()
                    for k in range(32):
                        mm(ps.t[:rws[s], 0:256], uT.t[:, k, s * 128:s * 128 + rws[s]], wv[:, k, :], k == 0, k == 31, [wb.r, uT.r], [ps.r])
                    copy(K.alt(), ckv32.t[:rws[s], s, cb * 256:(cb + 1) * 256], ps.t[:rws[s], 0:256], [ps.r], [ckv32.r])
            wb, wv = WL.load("kr")
            kr32 = L["kr32"]
            for s in range(nsub):
                ps = psA.next()
                for k in range(32):
                    mm(ps.t[:rws[s], 0:64], uT.t[:, k, s * 128:s * 128 + rws[s]], wv[:, k, :], k == 0, k == 31, [wb.r, uT.r], [ps.r])
                copy(K.alt(), kr32.t[:rws[s], s, :], ps.t[:rws[s], 0:64], [ps.r], [kr32.r])
            ss4, rs4 = L["ss4"], L["rs4"]
            for s in range(nsub):
                op("act", lambda: nc.scalar.activation(out=L["junk512"].t[:rws[s], :], in_=ckv32.t[:rws[s], s, :], func=AF.Square,
                                                       accum_out=ss4.t[:rws[s], s:s + 1]), [ckv32.r], [L["junk512"].r, ss4.r])
            rstd_from_ss(rs4.t[:rws[0], 0:nsub], ss4.t[:rws[0], 0:nsub], 1.0 / 512, [ss4.r], [rs4.r])
            for s in range(nsub):
                c_ = ckv32.t[:rws[s], s, :]
                op("dve", lambda: nc.vector.scalar_tensor_tensor(out=c_, in0=c_, scalar=rs4.t[:rws[s], s:s + 1],
                                                                 in1=L["gckv"].t[:rws[s], :], op0=ALU.mult, op1=ALU.mult),
                   [ckv32.r, rs4.r, L["gckv"].r], [ckv32.r])
            tm_out(D["ckv_o"], c0, T, ckv32, 512)
            cst = L["cst"]
            tm_in(cst, cs_src, 0, T, 64)
            rope_tm(kr32, cst, nsub, rws[0], L, L["kro"])
            tm_out(D["kr_o"], c0, T, L["kro"], 64)
            mla_up(ckv32, L["kro"], T, wuk, wuv, D["KT_mla"], D["KRT"], D["V_mla"], c0, L)

        def kv_locals():
            L = {}
            L["ktm"] = K.sb([128, 4, 1024], BF16)
            L["vst"] = K.sb([128, 4, 2048], BF16)
            L["st512"] = Ring([K.sb([128, 512], BF16) for _ in range(3)])
            L["gst"] = Ring([K.sb([4, 512], F32) for _ in range(2)])
            L["ckv32"] = K.sb([128, 4, 512], F32)
            L["kr32"] = K.sb([128, 4, 64], F32)
            L["kro"] = K.sb([128, 4, 64], F32)
            L["cst"] = K.sb([128, 4, 64], F32)
            L["rt"] = K.sb([128, 4, 4, 32], F32)
            L["ss4"] = K.sb([128, 4], F32)
            L["rs4"] = K.sb([128, 4], F32)
            L["junk512"] = K.sb([128, 512], BF16)
            L["gckv"] = K.sb([128, 512], F32)
            L["ckvb"] = K.sb([128, 4, 512], BF16)
            L["krb"] = K.sb([128, 4, 64], BF16)
            L["ckvT"] = K.sb([128, 4, 512], BF16)
            L["krT"] = K.sb([64, 512], BF16)
            dma("sq", L["gckv"].t[:], g_ckv_d.partition_broadcast(128), writes=[L["gckv"].r])
            return L

        Dp = dict(KT_ml=KT_ml, K_tm=K_tm, V_ml=V_ml, GATES=GATES, ckv_o=ckv_o, kr_o=kr_o, KT_mla=KT_mla, KRT=KRT, V_mla=V_mla)
        Ds = dict(KT_ml=KT_ml_s, K_tm=K_tm_s, V_ml=V_ml_s, GATES=GATES_s, ckv_o=s_ckv, kr_o=s_kr, KT_mla=KT_mla_s, KRT=KRT_s,
                  V_mla=V_mla_s)

        with K.scope():
            uT = K.sb([128, 32, 512], BF16)
            rows_ring = Ring([K.sb([128, 4096], F32) for _ in range(1)])
            ss = K.sb([128, 1], F32); rstd = K.sb([128, 1], F32)
            WL = WLoader(3)
            L = kv_locals()
            junk = View(L["vst"].t[:, 0:2, :].rearrange("p a b -> p (a b)"), L["vst"].r)
            wukb = K.sb([128, 8192], BF16); wuvb = K.sb([128, 8192], BF16)
            wuk = WL.load("wuk", wukb); wuv = WL.load("wuv", wuvb)
            for ti in range(NTA):
                prep(x_all[ti * 512:(ti + 1) * 512, :], 512, A1, B1, [(0, 512, 0)], uT, rows_ring, junk, ss, rstd)
                kv_side(uT, 512, ti * 512, Dp, WL, wuk, wuv, L, cs_all[ti * 512:(ti + 1) * 512, :])
            prep(x_smp, 64, A1, B1, sample_groups, uT, rows_ring, junk, ss, rstd)
            kv_side(uT, 64, 0, Ds, WL, wuk, wuv, L, cs_own[OWN:OWN + 64, :])
            for sq_ in range(4):
                for ti in range(NPT):
                    tm_in(L["ckv32"], ckv_past[sq_], ti * 512, 512, 512)
                    tm_in(L["kro"], kr_past[sq_], ti * 512, 512, 64)
                    mla_up(L["ckv32"], L["kro"], 512, wuk, wuv, KT_mla_p[sq_], KRT_p[sq_], V_mla_p[sq_], ti * 512, L)
        if stop_after <= 1:
            K.finish()
            return nc

        def vmemset(buf, ap, val):
            op("dve", lambda: nc.vector.memset(ap, val), [], [buf.r])

        with K.scope():
            SG = min(2048, SEQ)
            ones4 = K.sb([4, SG], F32); vmemset(ones4, ones4.t[:], 1.0)
            one1 = K.sb([4, 1], F32); vmemset(one1, one1.t[:], 1.0)
            bprev = K.sb([4, 1], F32); Mprev = K.sb([4, 1], F32)
            igt = K.sb([4, SG], F32); fgt = K.sb([4, SG], F32); bt = K.sb([4, SG], F32)
            ct = K.sb([4, SG], F32); Mt = K.sb([4, SG], F32)
            m0T = K.sb([4, 4], F32); smt = K.sb([4, 4], F32); mf = K.sb([4, 1], F32)
            dma("sq", m0T.t[:], m0_in, writes=[m0T.r])

            def scan_seg(ig_src, fg_src, n, c_dst, b_dst, M_dst):
                dma("sq", igt.t[:, 0:n], ig_src, writes=[igt.r])
                dma("sq", fgt.t[:, 0:n], fg_src, writes=[fgt.r])
                f_ = fgt.t[:, 0:n]
                op("act", lambda: nc.scalar.activation(out=f_, in_=f_, func=AF.Exp, scale=-1.0), [fgt.r], [fgt.r])
                op("act", lambda: nc.scalar.activation(out=f_, in_=f_, func=AF.Ln, bias=one1.t[:, 0:1], scale=1.0),
                   [fgt.r, one1.r], [fgt.r])
                op("dve", lambda: nc.vector.tensor_tensor_scan(out=bt.t[:, 0:n], data0=ones4.t[:, 0:n], data1=f_,
                                                               initial=bprev.t[:, 0:1], op0=ALU.mult, op1=ALU.subtract),
                   [ones4.r, fgt.r, bprev.r], [bt.r])
                op("dve", lambda: nc.vector.tensor_tensor(out=ct.t[:, 0:n], in0=igt.t[:, 0:n], in1=bt.t[:, 0:n], op=ALU.subtract),
                   [igt.r, bt.r], [ct.r])
                op("dve", lambda: nc.vector.tensor_tensor_scan(out=Mt.t[:, 0:n], data0=ones4.t[:, 0:n], data1=ct.t[:, 0:n],
                                                               initial=Mprev.t[:, 0:1], op0=ALU.mult, op1=ALU.max),
                   [ones4.r, ct.r, Mprev.r], [Mt.r])
                op("dve", lambda: nc.vector.tensor_copy(out=bprev.t[:], in_=bt.t[:, n - 1:n]), [bt.r], [bprev.r])
                op("dve", lambda: nc.vector.tensor_copy(out=Mprev.t[:], in_=Mt.t[:, n - 1:n]), [Mt.r], [Mprev.r])
                dma("sq", c_dst, ct.t[:, 0:n], reads=[ct.r])
                dma("sq", b_dst, bt.t[:, 0:n], reads=[bt.r])
                dma("sq", M_dst, Mt.t[:, 0:n], reads=[Mt.r])

            vmemset(bprev, bprev.t[:], 0.0); vmemset(Mprev, Mprev.t[:], 0.0)
            for g0 in range(0, SEQ, SG):
                scan_seg(GATES[0:4, g0:g0 + SG], GATES[4:8, g0:g0 + SG], SG,
                         ROWS[0, :, g0:g0 + SG], ROWS[1, :, g0:g0 + SG], ROWS[2, :, g0:g0 + SG])
            op("dve", lambda: nc.vector.tensor_tensor(out=mf.t[:], in0=bprev.t[:], in1=Mprev.t[:], op=ALU.add),
               [bprev.r, Mprev.r], [mf.r])
            dma("sq", m_o, mf.t[:], reads=[mf.r])
            for s_ in range(4):
                vmemset(bprev, bprev.t[:], 0.0)
                op("dve", lambda: nc.vector.tensor_copy(out=Mprev.t[:], in_=m0T.t[:, s_:s_ + 1]), [m0T.r], [Mprev.r])
                scan_seg(GATES_s[0:4, 16 * s_:16 * s_ + 16], GATES_s[4:8, 16 * s_:16 * s_ + 16], 16,
                         ROWS_s[0, :, 16 * s_:16 * s_ + 16], ROWS_s[1, :, 16 * s_:16 * s_ + 16], ROWS_s[2, :, 16 * s_:16 * s_ + 16])
                op("dve", lambda: nc.vector.tensor_tensor(out=smt.t[:, s_:s_ + 1], in0=bprev.t[:], in1=Mprev.t[:], op=ALU.add),
                   [bprev.r, Mprev.r], [smt.r])
            dma("sq", s_m, smt.t[:], reads=[smt.r])
        if stop_after <= 2:
            K.finish()
            return nc

        with K.scope():
            selsb = K.sb([NBLK, NOB], F32)
            dma("sq", selsb.t[:], sel_d, writes=[selsb.r])
            rowr = Ring([K.sb([NBLK, 128], F32) for _ in range(2)])
            for h in range(4):
                for ridx, dst in ((2, Mown), (1, bown)):
                    rb = rowr.next()
                    dma("sq", rb.t[:], ROWS[ridx, h, :].rearrange("(b t) -> b t", t=128), writes=[rb.r])
                    mm(psX.t[:, 0:NOB], rb.t[:, :], selsb.t[:, :], True, True, [rb.r, selsb.r], [psX.r])
                    copy("dve", dst.t[:, h, :], psX.t[:, 0:NOB], [psX.r], [dst.r])
            op("dve", lambda: nc.vector.tensor_scalar(out=negMown.t[:], in0=Mown.t[:], scalar1=-1.0, scalar2=None, op0=ALU.mult),
               [Mown.r], [negMown.r])
            op("dve", lambda: nc.vector.tensor_tensor(out=ebown.t[:], in0=bown.t[:], in1=Mown.t[:], op=ALU.add),
               [bown.r, Mown.r], [ebown.r])
            op("act", lambda: nc.scalar.activation(out=ebown.t[:], in_=ebown.t[:], func=AF.Exp, scale=-1.0), [ebown.r], [ebown.r])
            wcol = K.sb([128, 4, NBLK], F32); wcolb = K.sb([128, 4, NBLK], BF16)
            mlb = K.sb([128, 4], F32)
            for h in range(4):
                rb = rowr.next()
                dma("sq", rb.t[:], ROWS[0, h, :].rearrange("(b t) -> b t", t=128), writes=[rb.r])
                dma("sq", mlb.t[:, h:h + 1], ROWS[2, h, SEQ - 1:SEQ].partition_broadcast(128), writes=[mlb.r])
                pt = psT.next()
                tr(pt.t[:, 0:NBLK], rb.t[:, :], ident.t[:NBLK, :NBLK], [rb.r, ident.r], [pt.r])
                op("dve", lambda: nc.vector.tensor_scalar(out=wcol.t[:, h, :], in0=pt.t[:, 0:NBLK], scalar1=mlb.t[:, h:h + 1],
                                                          scalar2=None, op0=ALU.subtract), [pt.r, mlb.r], [wcol.r])
            op("act", lambda: nc.scalar.activation(out=wcol.t[:], in_=wcol.t[:], func=AF.Exp), [wcol.r], [wcol.r])
            copy("dve", wcolb.t[:], wcol.t[:], [wcol.r], [wcolb.r])
            kbr = Ring([K.sb([128, 256], BF16) for _ in range(3)])
            vbr = Ring([K.sb([128, 512], BF16) for _ in range(3)])
            kgr = Ring([K.sb([128, 256], BF16) for _ in range(3)])
            cstr = Ring([K.sb([128, 512], F32) for _ in range(2)])
            nst = K.sb([1, 256], F32)
            for h in range(4):
                po = [psO.next(), psO.next()]
                for blk in range(NBLK):
                    kb, vb, kg = kbr.next(), vbr.next(), kgr.next()
                    dma("sq", kb.t[:], K_tm[blk * 128:(blk + 1) * 128, h * 256:(h + 1) * 256], writes=[kb.r])
                    dma("sq", vb.t[:], V_ml[blk * 128:(blk + 1) * 128, h * 512:(h + 1) * 512], writes=[vb.r])
                    op("dve", lambda: nc.vector.tensor_scalar(out=kg.t[:], in0=kb.t[:], scalar1=wcol.t[:, h, blk:blk + 1],
                                                              scalar2=None, op0=ALU.mult), [kb.r, wcol.r], [kg.r])
                    for dc in range(2):
                        op("pe", lambda: nc.tensor.matmul(po[dc].t[:, 0:512], lhsT=kg.t[:, dc * 128:(dc + 1) * 128], rhs=vb.t[:],
                                                          start=(blk == 0), stop=(blk == NBLK - 1)),
                           [kg.r, vb.r], [po[dc].r], inc=True)
                    op("pe", lambda: nc.tensor.matmul(psX.t[0:1, 0:256], lhsT=wcolb.t[:, h, blk:blk + 1], rhs=kb.t[:],
                                                      start=(blk == 0), stop=(blk == NBLK - 1)),
                       [wcolb.r, kb.r], [psX.r], inc=True)
                for dc in range(2):
                    cs_ = cstr.next()
                    copy(K.alt(), cs_.t[:], po[dc].t[:, 0:512], [po[dc].r], [cs_.r])
                    dma("sq", C_o[h, dc * 128:(dc + 1) * 128, :], cs_.t[:], reads=[cs_.r])
                copy("dve", nst.t[:], psX.t[0:1, 0:256], [psX.r], [nst.r])
                dma("sq", n_o[h:h + 1, :], nst.t[:], reads=[nst.r])
        if stop_after <= 3:
            K.finish()
            return nc

        def q_side(uT, T, c0, WL, L):
            nsub = (T + 127) // 128
            rws = [min(128, T - s * 128) for s in range(nsub)]
            for cb in range(4):
                wb, wv = WL.load(f"q{cb}")
                for cc in range(2):
                    ch = cb * 2 + cc
                    ps = psA.next()
                    for k in range(32):
                        mm(ps.t[:, 0:T], wv[:, k, cc * 128:(cc + 1) * 128], uT.t[:, k, 0:T], k == 0, k == 31, [wb.r, uT.r], [ps.r])
                    st = L["st512"].next()
                    copy(K.alt(), st.t[:, 0:T], ps.t[:, 0:T], [ps.r], [st.r])
                    dma("sq", QT_ml[ch * 128:(ch + 1) * 128, c0:c0 + T], st.t[:, 0:T], reads=[st.r])
            og = L["og"]
            for cb in range(8):
                wb, wv = WL.load(f"o{cb}")
                for s in range(nsub):
                    ps = psA.next()
                    for k in range(32):
                        mm(ps.t[:rws[s], 0:256], uT.t[:, k, s * 128:s * 128 + rws[s]], wv[:, k, :], k == 0, k == 31, [wb.r, uT.r], [ps.r])
                    op("act", lambda: nc.scalar.activation(out=og.t[:rws[s], s, cb * 256:(cb + 1) * 256], in_=ps.t[:rws[s], 0:256],
                                                           func=AF.Sigmoid), [ps.r], [og.r])
            tm_out(OG, c0, T, og, 2048)
            cq32, cqnT, rbc = L["cq32"], L["cqnT"], L["rbc"]
            for cb in range(4):
                wb, wv = WL.load(f"cq{cb}")
                for cc in range(2):
                    ch = cb * 2 + cc
                    ps = psA.next()
                    for k in range(32):
                        mm(ps.t[:, 0:T], wv[:, k, cc * 128:(cc + 1) * 128], uT.t[:, k, 0:T], k == 0, k == 31, [wb.r, uT.r], [ps.r])
                    copy("dve", cq32.t[:, ch, 0:T], ps.t[:, 0:T], [ps.r], [cq32.r])
                    sqb = L["sqb"].next()
                    op("act", lambda: nc.scalar.activation(out=sqb.t[:, 0:T], in_=cq32.t[:, ch, 0:T], func=AF.Square), [cq32.r], [sqb.r])
                    mm(psX.t[:, 0:T], ones_bf.t[:], sqb.t[:, 0:T], ch == 0, ch == 7, [ones_bf.r, sqb.r], [psX.r], inc=True)
            rstd_from_ss(rbc.t[:, 0:T], psX.t[:, 0:T], 1.0 / 1024, [psX.r], [rbc.r])
            for ch in range(8):
                op("dve", lambda: nc.vector.scalar_tensor_tensor(out=cqnT.t[:, ch, 0:T], in0=cq32.t[:, ch, 0:T], scalar=gcq.t[:, ch:ch + 1],
                                                                 in1=rbc.t[:, 0:T], op0=ALU.mult, op1=ALU.mult),
                   [cq32.r, gcq.r, rbc.r], [cqnT.r])
            qm32, qmb, rt2, cso, qnst, qrst = L["qm32"], L["qmb"], L["rt2"], L["cso"], L["qnst"], L["qrst"]
            for s in range(nsub):
                rows = rws[s]
                for cb in range(3):
                    wb, wv = WL.load(f"uq{cb}")
                    for half in range(2):
                        ps = psA.next()
                        for k in range(8):
                            mm(ps.t[:rows, 0:512], cqnT.t[:, k, s * 128:s * 128 + rows], wv[:, k, half * 512:(half + 1) * 512],
                               k == 0, k == 7, [wb.r, cqnT.r], [ps.r])
                        copy(K.alt(), qm32.t[:rows, cb * 1024 + half * 512:cb * 1024 + (half + 1) * 512], ps.t[:rows, 0:512], [ps.r], [qm32.r])
                dma("sq", cso.t[:rows, :], cs_own[c0 + s * 128:c0 + s * 128 + rows, :], writes=[cso.r])
                qv = qm32.t[:rows, :].rearrange("p (h e) -> p h e", e=192)
                qb = qmb.t[:rows, :].rearrange("p (h e) -> p h e", e=192)
                x1, x2 = qv[:, :, 128:160], qv[:, :, 160:192]
                co = cso.t[:rows, None, 0:32].to_broadcast([rows, 16, 32])
                si = cso.t[:rows, None, 32:64].to_broadcast([rows, 16, 32])
                t = [rt2.t[:rows, i, :, :] for i in range(4)]
                rr = [qm32.r, cso.r]
                op("dve", lambda: nc.vector.tensor_tensor(out=t[0], in0=x1, in1=co, op=ALU.mult), rr, [rt2.r])
                op("dve", lambda: nc.vector.tensor_tensor(out=t[1], in0=x2, in1=si, op=ALU.mult), rr, [rt2.r])
                op("dve", lambda: nc.vector.tensor_tensor(out=t[2], in0=x2, in1=co, op=ALU.mult), rr, [rt2.r])
                op("dve", lambda: nc.vector.tensor_tensor(out=t[3], in0=x1, in1=si, op=ALU.mult), rr, [rt2.r])
                op("act", lambda: nc.scalar.copy(out=qb[:, :, 0:128], in_=qv[:, :, 0:128]), [qm32.r], [qmb.r])
                op("dve", lambda: nc.vector.tensor_tensor(out=qb[:, :, 128:160], in0=t[0], in1=t[1], op=ALU.subtract), [rt2.r], [qmb.r])
                op("dve", lambda: nc.vector.tensor_tensor(out=qb[:, :, 160:192], in0=t[2], in1=t[3], op=ALU.add), [rt2.r], [qmb.r])
                for g in range(2):
                    pt = psT.next()
                    pv = bfv(pt)
                    for i in range(8):
                        tr(pv[:, i * 128:i * 128 + rows], qb[:, g * 8 + i, 0:128], identb.t[:rows, :rows], [qmb.r, identb.r], [pt.r], inc=(i == 7))
                    copy(K.alt(), qnst.t[:, g * 8:(g + 1) * 8, 0:rows], pv.rearrange("p (a b) -> p a b", b=128)[:, :, 0:rows], [pt.r], [qnst.r])
                for g in range(2):
                    pt = psT.next()
                    pv = bfv(pt)
                    for i in range(8):
                        tr(pv[0:64, i * 128:i * 128 + rows], qb[:, g * 8 + i, 128:192], identb.t[:rows, :rows], [qmb.r, identb.r], [pt.r], inc=(i == 7))
                    copy(K.alt(), qrst.t[:, g * 8:(g + 1) * 8, 0:rows], pv.rearrange("p (a b) -> p a b", b=128)[0:64, :, 0:rows], [pt.r], [qrst.r])
                n0_ = c0 + s * 128
                dma("sq", QN_T.rearrange("(h d) n -> d h n", d=128)[:, :, n0_:n0_ + rows], qnst.t[:, :, 0:rows], reads=[qnst.r])
                dma("sq", QR_T.rearrange("(h d) n -> d h n", d=64)[:, :, n0_:n0_ + rows], qrst.t[:, :, 0:rows], reads=[qrst.r])

        with K.scope():
            uT = K.sb([128, 32, 512], BF16)
            rows_ring = Ring([K.sb([128, 4096], F32) for _ in range(1)])
            ss = K.sb([128, 1], F32); rstd = K.sb([128, 1], F32)
            WL = WLoader(3)
            L = dict(st512=Ring([K.sb([128, 512], BF16) for _ in range(3)]), og=K.sb([128, 4, 2048], BF16),
                     cq32=K.sb([128, 8, 512], F32), cqnT=K.sb([128, 8, 512], BF16), rbc=K.sb([128, 512], F32),
                     sqb=Ring([K.sb([128, 512], BF16) for _ in range(3)]), qm32=K.sb([128, 3072], F32), qmb=K.sb([128, 3072], BF16),
                     rt2=K.sb([128, 4, 16, 32], F32), cso=K.sb([128, 64], F32), qnst=K.sb([128, 16, 128], BF16),
                     qrst=K.sb([64, 16, 128], BF16))
            junk = View(L["og"].t[:, 0:2, :].rearrange("p a b -> p (a b)"), L["og"].r)
            for ti in range(NTO):
                prep(x_own[ti * 512:(ti + 1) * 512, :], 512, A1, B1, [(0, 512, 0)], uT, rows_ring, junk, ss, rstd)
                q_side(uT, 512, ti * 512, WL, L)
            prep(x_smp, 64, A1, B1, sample_groups, uT, rows_ring, junk, ss, rstd)
            q_side(uT, 64, OWN, WL, L)
        if stop_after <= 4:
            K.finish()
            return nc

        NPG = (max(NBLK, (NKS + 127) // 128) + 7) // 8

        def pv_blocks(nk):
            return [(k0, min(128, nk - k0)) for k0 in range(0, nk, 128)]

        def transposes_and_pv(nq, Pb, nk, PT, PTres, po, ncols, vget, ceng=None):
            blks = pv_blocks(nk)
            nb = len(blks)
            for g0 in range(0, nb, 8):
                grp = blks[g0:g0 + 8]
                pt = psT.next()
                pv = bfv(pt)
                pr = PTres[g0 // 8]
                for i, (k0, w) in enumerate(grp):
                    tr(pv[:w, i * 128:i * 128 + nq], Pb.t[:nq, k0:k0 + w], identb.t[:nq, :nq], [Pb.r, identb.r], [pt.r],
                       inc=(i == len(grp) - 1))
                full = [i for i, (k0, w) in enumerate(grp) if w == 128]
                if full:
                    nf = len(full)
                    copy(ceng or K.alt(), PT.t[:, g0:g0 + nf, 0:nq], pv.rearrange("p (a b) -> p a b", b=128)[:, 0:nf, 0:nq], [pt.r], [pr])
                for i, (k0, w) in enumerate(grp):
                    if w < 128:
                        copy(ceng or K.alt(), PT.t[:w, g0 + i, 0:nq], pv[:w, i * 128:i * 128 + nq], [pt.r], [pr])
                for i, (k0, w) in enumerate(grp):
                    bi = g0 + i
                    v_ap, v_res = vget(bi, w)
                    op("pe", lambda: nc.tensor.matmul(po.t[:nq, 0:ncols], lhsT=PT.t[:w, bi, 0:nq], rhs=v_ap,
                                                      start=(bi == 0), stop=(bi == nb - 1)),
                       [pr] + v_res, [po.r], inc=(bi == nb - 1))

        def attn_A1(nq, qn, qr, qres, KTs, KRTs, tiles, Sb):
            for (k0, w, bias) in tiles:
                ps = psA.next()
                mm(ps.t[:nq, 0:w], qn, KTs.t[:, k0:k0 + w], True, False, qres + [KTs.r], [ps.r])
                mm(ps.t[:nq, 0:w], qr, KRTs.t[0:64, k0:k0 + w], False, True, qres + [KRTs.r], [ps.r])
                if bias is None:
                    copy("act", Sb.t[:nq, k0:k0 + w], ps.t[:nq, 0:w], [ps.r], [Sb.r])
                else:
                    op("dve", lambda: nc.vector.tensor_tensor(out=Sb.t[:nq, k0:k0 + w], in0=ps.t[:nq, 0:w], in1=bias.t[:nq, 0:w],
                                                              op=ALU.add), [ps.r, bias.r], [Sb.r])

        def attn_A2(nq, tiles, Sb, Pb, sm):
            nk = sum(w for (_, w, _) in tiles)
            mx, negm, rsum, rinv = sm
            op("dve", lambda: nc.vector.reduce_max(out=mx.t[:nq, :], in_=Sb.t[:nq, 0:nk], axis=AX.X), [Sb.r], [mx.r])
            op("dve", lambda: nc.vector.tensor_scalar(out=negm.t[:nq, :], in0=mx.t[:nq, :], scalar1=-MLA_SCALE, scalar2=None,
                                                      op0=ALU.mult), [mx.r], [negm.r])
            op("act", lambda: nc.scalar.activation(out=Pb.t[:nq, 0:nk], in_=Sb.t[:nq, 0:nk], func=AF.Exp, bias=negm.t[:nq, :],
                                                   scale=MLA_SCALE, accum_out=rsum.t[:nq, :]), [Sb.r, negm.r], [Pb.r, rsum.r])
            op("dve", lambda: nc.vector.reciprocal(out=rinv.t[:nq, :], in_=rsum.t[:nq, :]), [rsum.r], [rinv.r])
            return (nq, nk, Pb, rinv)

        def attn_A(nq, qn, qr, qres, KTs, KRTs, tiles, Sb, Pb, sm):
            attn_A1(nq, qn, qr, qres, KTs, KRTs, tiles, Sb)
            return attn_A2(nq, tiles, Sb, Pb, sm)

        def attn_B(st, Vs, PT, PTres, out_ap, out_res):
            nq, nk, Pb, rinv = st
            po = psO.next()
            transposes_and_pv(nq, Pb, nk, PT, PTres, po, 128, lambda bi, w: (Vs.t[:w, bi, :], [Vs.r]), ceng="dve")
            op("dve", lambda: nc.vector.tensor_scalar(out=out_ap, in0=po.t[:nq, 0:128], scalar1=rinv.t[:nq, :], scalar2=None, op0=ALU.mult),
               [po.r, rinv.r], [out_res])

        def attn_tile(nq, qn, qr, qres, KTs, KRTs, Vs, tiles, Sb, Pb, PT, PTres, sm, out_ap, out_res):
            st = attn_A(nq, qn, qr, qres, KTs, KRTs, tiles, Sb, Pb, sm)
            attn_B(st, Vs, PT, PTres, out_ap, out_res)

        KMAX = max(SEQ, NKS)
        with K.scope():
            KRTs = K.sb([64, KMAX], BF16); KTs = K.sb([128, KMAX], BF16)
            Vs = K.sb([128, (KMAX + 127) // 128, 128], BF16)
            QN = K.sb([128, OWN], BF16); QR = K.sb([64, OWN], BF16)
            biasm = K.sb([128, 512], F32)
            Sbr = Ring([K.sb([128, KMAX], F32) for _ in range(2)])
            Pbr = Ring([K.sb([128, KMAX], BF16) for _ in range(2)])
            PT = K.sb([128, NPG * 8, 128], BF16)
            PTres = [Res() for _ in range(NPG)]
            Aacc = K.sb([128, NOB, 128], BF16)
            Asmp = K.sb([16, 16, 128], BF16)
            smr = Ring([tuple(K.sb([128, 1], F32) for _ in range(4)) for _ in range(2)])
            dma("sq", biasm.t[:], bias_mla_d, writes=[biasm.r])
            dma("sq", KRTs.t[:, 0:SEQ], KRT, writes=[KRTs.r])
            for h in range(16):
                dma("sq", KTs.t[:, 0:SEQ], KT_mla[h * 128:(h + 1) * 128, :], writes=[KTs.r])
                dma("sq", Vs.t[:, 0:NBLK, :], V_mla.rearrange("(b p) f -> p b f", p=128)[:, :, h * 128:(h + 1) * 128], writes=[Vs.r])
                dma("sq", QN.t[:], QN_T[h * 128:(h + 1) * 128, 0:OWN], writes=[QN.r])
                dma("sq", QR.t[:], QR_T[h * 64:(h + 1) * 64, 0:OWN], writes=[QR.r])
                prev = None
                for m in range(NOB):
                    tiles = [(kt * 512, 512, None) for kt in range(m)] + [(m * 512, 512, biasm)]
                    Sb_, Pb_, sm_ = Sbr.next(), Pbr.next(), smr.next()
                    attn_A1(128, QN.t[:, m * 128:(m + 1) * 128], QR.t[:, m * 128:(m + 1) * 128], [QN.r, QR.r], KTs, KRTs, tiles, Sb_)
                    if prev is not None:
                        attn_B(prev[0], Vs, PT, PTres, Aacc.t[:, prev[1], :], Aacc.r)
                    cur = attn_A2(128, tiles, Sb_, Pb_, sm_)
                    prev = (cur, m)
                attn_B(prev[0], Vs, PT, PTres, Aacc.t[:, prev[1], :], Aacc.r)
                dma("sq", A_s[0:OWN, :].rearrange("(m p) f -> p m f", p=128)[:, :, 2048 + h * 128:2048 + (h + 1) * 128], Aacc.t[:],
                    reads=[Aacc.r])
            for sq_ in range(4):
                dma("sq", KRTs.t[:, 0:PAST], KRT_p[sq_], writes=[KRTs.r])
                dma("sq", KRTs.t[:, PAST:NKS], KRT_s[:, 16 * sq_:16 * sq_ + 16], writes=[KRTs.r])
                for h in range(16):
                    dma("sq", KTs.t[:, 0:PAST], KT_mla_p[sq_][h * 128:(h + 1) * 128, :], writes=[KTs.r])
                    dma("sq", KTs.t[:, PAST:NKS], KT_mla_s[h * 128:(h + 1) * 128, 16 * sq_:16 * sq_ + 16], writes=[KTs.r])
                    dma("sq", Vs.t[:, 0:PAST // 128, :], V_mla_p[sq_].rearrange("(b p) f -> p b f", p=128)[:, :, h * 128:(h + 1) * 128],
                        writes=[Vs.r])
                    dma("sq", Vs.t[0:16, PAST // 128, :], V_mla_s[16 * sq_:16 * sq_ + 16, h * 128:(h + 1) * 128], writes=[Vs.r])
                    dma("sq", QN.t[:, 0:16], QN_T[h * 128:(h + 1) * 128, OWN + 16 * sq_:OWN + 16 * sq_ + 16], writes=[QN.r])
                    dma("sq", QR.t[:, 0:16], QR_T[h * 64:(h + 1) * 64, OWN + 16 * sq_:OWN + 16 * sq_ + 16], writes=[QR.r])
                    tiles = [(i * 512, 512, None) for i in range(NPT)] + [(PAST, 16, None)]
                    attn_tile(16, QN.t[:, 0:16], QR.t[:, 0:16], [QN.r, QR.r], KTs, KRTs, Vs, tiles,
                              Sbr.next(), Pbr.next(), PT, PTres, smr.next(), Asmp.t[:, h, :], Asmp.r)
                dma("sq", A_s[OWN + 16 * sq_:OWN + 16 * sq_ + 16, 2048:4096], Asmp.t[:].rearrange("p h d -> p (h d)"), reads=[Asmp.r])
        if stop_after <= 5:
            K.finish()
            return nc

        with K.scope():
            maskm = K.sb([128, 512], F32); tril16 = K.sb([16, 16], F32)
            gml = K.sb([128, 2048], F32)
            KT2 = K.sb([128, 2, SEQ], BF16); QT2 = K.sb([128, 2, OWN], BF16)
            cbc = K.sb([128, SEQ], F32); Eb = K.sb([128, SEQ], BF16); tmpE = K.sb([128, 512], F32)
            Pbr = Ring([K.sb([128, SEQ], BF16) for _ in range(2)])
            PT = K.sb([128, NPG * 8, 128], BF16)
            PTres = [Res() for _ in range(NPG)]
            vring = Ring([K.sb([128, 4, 512], BF16) for _ in range(2)])
            hbr = Ring([K.sb([128, 512], F32) for _ in range(2)])
            ogr = Ring([K.sb([128, 512], BF16) for _ in range(2)])
            astr = Ring([K.sb([128, 512], BF16) for _ in range(2)])
            junk5 = K.sb([128, 512], BF16)
            smr = Ring([tuple(K.sb([128, 1], F32) for _ in range(6)) for _ in range(2)])
            dma("sq", maskm.t[:], mask_ml_d, writes=[maskm.r])
            dma("sq", tril16.t[:], tril16_d, writes=[tril16.r])
            dma("sq", gml.t[:], g_ml_d.partition_broadcast(128), writes=[gml.r])

            def ml_A(nq, qT, qres, tiles, negM, colres):
                nk = sum(w for (_, w, _) in tiles)
                Pb = Pbr.next()
                sm = smr.next()
                nqv = sm[0]
                lastk0 = tiles[-1][0]
                lw = nk - lastk0
                if lastk0 > 0:
                    op("act", lambda: nc.scalar.activation(out=Eb.t[:nq, 0:lastk0], in_=cbc.t[:nq, 0:lastk0], func=AF.Exp, bias=negM,
                                                           scale=1.0), [cbc.r] + colres, [Eb.r])
                op("dve", lambda: nc.vector.tensor_scalar(out=tmpE.t[:nq, 0:lw], in0=cbc.t[:nq, lastk0:nk], scalar1=negM, scalar2=0.0,
                                                          op0=ALU.add, op1=ALU.min), [cbc.r] + colres, [tmpE.r])
                op("act", lambda: nc.scalar.activation(out=Eb.t[:nq, lastk0:nk], in_=tmpE.t[:nq, 0:lw], func=AF.Exp), [tmpE.r], [Eb.r])
                for (k0, w, mask) in tiles:
                    ps = psA.next()
                    for c in range(2):
                        mm(ps.t[:nq, 0:w], qT[:, c, :], KT2.t[:, c, k0:k0 + w], c == 0, c == 1, qres + [KT2.r], [ps.r])
                    op("dve", lambda: nc.vector.tensor_tensor(out=Pb.t[:nq, k0:k0 + w], in0=ps.t[:nq, 0:w], in1=Eb.t[:nq, k0:k0 + w],
                                                              op=ALU.mult), [ps.r, Eb.r], [Pb.r])
                    if mask is not None:
                        op("dve", lambda: nc.vector.tensor_tensor(out=Pb.t[:nq, k0:k0 + w], in0=Pb.t[:nq, k0:k0 + w],
                                                                  in1=mask.t[:nq, 0:w], op=ALU.mult), [Pb.r, mask.r], [Pb.r])
                op("dve", lambda: nc.vector.reduce_sum(out=nqv.t[:nq, :], in_=Pb.t[:nq, 0:nk], axis=AX.X), [Pb.r], [nqv.r])
                return (nq, nk, Pb, sm, tiles)

            def ml_B(st, eb, colres, vsrc, h, init, og_src, a_dst):
                nq, nk, Pb, sm, tiles = st
                nqv, an, den, rden, ssq, rs = sm
                po = psO.next()
                nb = (nk + 127) // 128
                for ti_, (k0, w, _) in enumerate(tiles):
                    vb = vring.next()
                    if w % 128 == 0:
                        dma("sq", vb.t[:, 0:w // 128, :], vsrc(k0, w).rearrange("(b p) f -> p b f", p=128), writes=[vb.r])
                    else:
                        dma("sq", vb.t[:w, 0, :], vsrc(k0, w), writes=[vb.r])
                    tb = pv_blocks(w)
                    pt = psT.next()
                    pv = bfv(pt)
                    pr = PTres[(k0 // 512) % NPG]
                    for i, (b0, bw) in enumerate(tb):
                        tr(pv[:bw, i * 128:i * 128 + nq], Pb.t[:nq, k0 + b0:k0 + b0 + bw], identb.t[:nq, :nq], [Pb.r, identb.r], [pt.r],
                           inc=(i == len(tb) - 1))
                    g0 = (k0 // 128)
                    if all(bw == 128 for (_, bw) in tb):
                        copy("act", PT.t[:, g0:g0 + len(tb), 0:nq], pv.rearrange("p (a b) -> p a b", b=128)[:, 0:len(tb), 0:nq], [pt.r], [pr])
                    else:
                        for i, (b0, bw) in enumerate(tb):
                            copy("act", PT.t[:bw, g0 + i, 0:nq], pv[:bw, i * 128:i * 128 + nq], [pt.r], [pr])
                    for i, (b0, bw) in enumerate(tb):
                        bi = g0 + i
                        op("pe", lambda: nc.tensor.matmul(po.t[:nq, 0:512], lhsT=PT.t[:bw, bi, 0:nq], rhs=vb.t[:bw, i, :],
                                                          start=(bi == 0), stop=(bi == nb - 1)),
                           [pr, vb.r], [po.r], inc=(i == len(tb) - 1))
                if init is not None:
                    op("dve", lambda: nc.vector.tensor_tensor(out=nqv.t[:nq, :], in0=nqv.t[:nq, :], in1=init[1].t[:nq, :], op=ALU.add),
                       [nqv.r, init[1].r], [nqv.r])
                op("dve", lambda: nc.vector.tensor_scalar(out=an.t[:nq, :], in0=nqv.t[:nq, :], scalar1=-1.0, scalar2=None, op0=ALU.mult),
                   [nqv.r], [an.r])
                op("dve", lambda: nc.vector.tensor_tensor(out=an.t[:nq, :], in0=an.t[:nq, :], in1=nqv.t[:nq, :], op=ALU.max),
                   [nqv.r, an.r], [an.r])
                op("dve", lambda: nc.vector.tensor_tensor(out=den.t[:nq, :], in0=an.t[:nq, :], in1=eb, op=ALU.max), [an.r] + colres, [den.r])
                op("dve", lambda: nc.vector.reciprocal(out=rden.t[:nq, :], in_=den.t[:nq, :]), [den.r], [rden.r])
                hb = hbr.next()
                if init is None:
                    op("act", lambda: nc.scalar.activation(out=hb.t[:nq, :], in_=po.t[:nq, 0:512], func=AF.Copy, scale=rden.t[:nq, :]),
                       [po.r, rden.r], [hb.r])
                else:
                    op("dve", lambda: nc.vector.tensor_tensor(out=hb.t[:nq, :], in0=po.t[:nq, 0:512], in1=init[0].t[:nq, :], op=ALU.add),
                       [po.r, init[0].r], [hb.r])
                    op("act", lambda: nc.scalar.activation(out=hb.t[:nq, :], in_=hb.t[:nq, :], func=AF.Copy, scale=rden.t[:nq, :]),
                       [hb.r, rden.r], [hb.r])
                op("act", lambda: nc.scalar.activation(out=junk5.t[:nq, :], in_=hb.t[:nq, :], func=AF.Square, accum_out=ssq.t[:nq, :]),
                   [hb.r], [junk5.r, ssq.r])
                rstd_from_ss(rs.t[:nq, :], ssq.t[:nq, :], 1.0 / 512, [ssq.r], [rs.r])
                og = ogr.next()
                dma("sq", og.t[:nq, :], og_src, writes=[og.r])
                op("dve", lambda: nc.vector.scalar_tensor_tensor(out=hb.t[:nq, :], in0=hb.t[:nq, :], scalar=rs.t[:nq, :],
                                                                 in1=gml.t[:nq, h * 512:(h + 1) * 512], op0=ALU.mult, op1=ALU.mult),
                   [hb.r, rs.r, gml.r], [hb.r])
                ast = astr.next()
                op("dve", lambda: nc.vector.tensor_tensor(out=ast.t[:nq, :], in0=hb.t[:nq, :], in1=og.t[:nq, :], op=ALU.mult),
                   [hb.r, og.r], [ast.r])
                dma("sq", a_dst, ast.t[:nq, :], reads=[ast.r])

            for h in range(4):
                dma("sq", KT2.t[:], KT_ml[h * 256:(h + 1) * 256, :].rearrange("(c p) n -> p c n", p=128), writes=[KT2.r])
                dma("sq", QT2.t[:], QT_ml[h * 256:(h + 1) * 256, 0:OWN].rearrange("(c p) n -> p c n", p=128), writes=[QT2.r])
                dma("sq", cbc.t[:], ROWS[0, h, :].partition_broadcast(128), writes=[cbc.r])
                prev = None
                cres_ = [negMown.r, ebown.r]
                for m in range(NOB):
                    tiles = [(kt * 512, 512, None) for kt in range(m)] + [(m * 512, 512, maskm)]
                    cur = ml_A(128, QT2.t[:, :, m * 128:(m + 1) * 128], [QT2.r], tiles, negMown.t[:, h, m:m + 1], cres_)
                    if prev is not None:
                        pm = prev[1]
                        ml_B(prev[0], ebown.t[:, h, pm:pm + 1], cres_, lambda k0, w: V_ml[k0:k0 + w, h * 512:(h + 1) * 512], h, None,
                             OG[pm * 128:(pm + 1) * 128, h * 512:(h + 1) * 512], A_s[pm * 128:(pm + 1) * 128, h * 512:(h + 1) * 512])
                    prev = (cur, m)
                pm = prev[1]
                ml_B(prev[0], ebown.t[:, h, pm:pm + 1], cres_, lambda k0, w: V_ml[k0:k0 + w, h * 512:(h + 1) * 512], h, None,
                     OG[pm * 128:(pm + 1) * 128, h * 512:(h + 1) * 512], A_s[pm * 128:(pm + 1) * 128, h * 512:(h + 1) * 512])
            colr = Ring([tuple(K.sb([128, 1], F32) for _ in range(8)) for _ in range(2)])
            C0f = Ring([K.sb([128, 2, 512], F32) for _ in range(1)]); C0b = K.sb([128, 2, 512], BF16)
            n0f = K.sb([128, 2], F32); n0b = K.sb([128, 2], BF16); n0row = K.sb([1, 256], F32)
            inter = Ring([K.sb([16, 512], F32) for _ in range(2)])
            kb16 = Ring([K.sb([16, 256], BF16) for _ in range(2)]); vb16 = Ring([K.sb([16, 512], BF16) for _ in range(2)])
            kg16 = Ring([K.sb([16, 256], BF16) for _ in range(2)]); wtb = K.sb([16, 1], BF16)
            cstr = Ring([K.sb([128, 512], F32) for _ in range(2)]); nst = K.sb([1, 256], F32)
            for sq_ in range(4):
                t0_ = 16 * sq_
                for h in range(4):
                    sh = sq_ * 4 + h
                    Mc, bc_, nM, ebc, m0b, wint, cc_, mlb16 = colr.next()
                    colres = [Mc.r]
                    dma("sq", KT2.t[:, :, 0:16], KT_ml_s[h * 256:(h + 1) * 256, t0_:t0_ + 16].rearrange("(c p) n -> p c n", p=128), writes=[KT2.r])
                    dma("sq", QT2.t[:, :, 0:16], QT_ml[h * 256:(h + 1) * 256, OWN + t0_:OWN + t0_ + 16].rearrange("(c p) n -> p c n", p=128),
                        writes=[QT2.r])
                    dma("sq", cbc.t[0:16, 0:16], ROWS_s[0, h, t0_:t0_ + 16].partition_broadcast(16), writes=[cbc.r])
                    dma("sq", Mc.t[0:16, :], ROWS_s[2, h, t0_:t0_ + 16].rearrange("(t o) -> t o", o=1), writes=[Mc.r])
                    dma("sq", bc_.t[0:16, :], ROWS_s[1, h, t0_:t0_ + 16].rearrange("(t o) -> t o", o=1), writes=[Mc.r])
                    dma("sq", cc_.t[0:16, :], ROWS_s[0, h, t0_:t0_ + 16].rearrange("(t o) -> t o", o=1), writes=[Mc.r])
                    dma("sq", m0b.t[:, :], m0_in[h, sq_:sq_ + 1].partition_broadcast(128), writes=[Mc.r])
                    dma("sq", mlb16.t[:, :], ROWS_s[2, h, t0_ + 15:t0_ + 16].partition_broadcast(128), writes=[Mc.r])
                    op("dve", lambda: nc.vector.tensor_scalar(out=nM.t[0:16, :], in0=Mc.t[0:16, :], scalar1=-1.0, scalar2=None, op0=ALU.mult),
                       colres, colres)
                    op("dve", lambda: nc.vector.tensor_tensor(out=ebc.t[0:16, :], in0=bc_.t[0:16, :], in1=Mc.t[0:16, :], op=ALU.add), colres, colres)
                    op("act", lambda: nc.scalar.activation(out=ebc.t[0:16, :], in_=ebc.t[0:16, :], func=AF.Exp, scale=-1.0), colres, colres)
                    op("dve", lambda: nc.vector.tensor_tensor(out=wint.t[0:16, :], in0=m0b.t[0:16, :], in1=Mc.t[0:16, :], op=ALU.subtract),
                       colres, colres)
                    op("act", lambda: nc.scalar.activation(out=wint.t[0:16, :], in_=wint.t[0:16, :], func=AF.Exp), colres, colres)
                    c0f = C0f.next()
                    dma("sq", c0f.t[:], C0_in[sh].rearrange("(c p) f -> p c f", p=128), writes=[c0f.r])
                    copy("dve", C0b.t[:], c0f.t[:], [c0f.r], [C0b.r])
                    for c_ in range(2):
                        dma("sq", n0f.t[:, c_:c_ + 1], n0_in[sh, c_ * 128:(c_ + 1) * 128].rearrange("(p o) -> p o", o=1), writes=[n0f.r])
                    copy("dve", n0b.t[:], n0f.t[:], [n0f.r], [n0b.r])
                    dma("sq", n0row.t[:], n0_in[sh:sh + 1, :], writes=[n0row.r])
                    ps = psA.next()
                    for c in range(2):
                        mm(ps.t[0:16, 0:512], QT2.t[:, c, 0:16], C0b.t[:, c, :], c == 0, c == 1, [QT2.r, C0b.r], [ps.r])
                    it = inter.next()
                    op("act", lambda: nc.scalar.activation(out=it.t[:, :], in_=ps.t[0:16, 0:512], func=AF.Copy, scale=wint.t[0:16, :]),
                       [ps.r] + colres, [it.r])
                    ps2 = psA.next()
                    for c in range(2):
                        mm(ps2.t[0:16, 0:1], QT2.t[:, c, 0:16], n0b.t[:, c:c + 1], c == 0, c == 1, [QT2.r, n0b.r], [ps2.r])
                    qn0w = Buf(None); qn0w.t = bc_.t
                    op("dve", lambda: nc.vector.tensor_tensor(out=bc_.t[0:16, :], in0=ps2.t[0:16, 0:1], in1=wint.t[0:16, :], op=ALU.mult),
                       [ps2.r] + colres, colres)
                    qn0w.r = Mc.r
                    st_ = ml_A(16, QT2.t[:, :, 0:16], [QT2.r], [(0, 16, tril16)], nM.t[0:16, :], colres)
                    ml_B(st_, ebc.t[0:16, :], colres, lambda k0, w: V_ml_s[t0_:t0_ + 16, h * 512:(h + 1) * 512], h, (it, qn0w),
                         OG[OWN + t0_:OWN + t0_ + 16, h * 512:(h + 1) * 512], A_s[OWN + t0_:OWN + t0_ + 16, h * 512:(h + 1) * 512])
                    kb, vb, kg = kb16.next(), vb16.next(), kg16.next()
                    dma("sq", kb.t[:], K_tm_s[t0_:t0_ + 16, h * 256:(h + 1) * 256], writes=[kb.r])
                    dma("sq", vb.t[:], V_ml_s[t0_:t0_ + 16, h * 512:(h + 1) * 512], writes=[vb.r])
                    op("dve", lambda: nc.vector.tensor_tensor(out=cc_.t[0:16, :], in0=cc_.t[0:16, :], in1=mlb16.t[0:16, :], op=ALU.subtract),
                       colres, colres)
                    op("act", lambda: nc.scalar.activation(out=cc_.t[0:16, :], in_=cc_.t[0:16, :], func=AF.Exp), colres, colres)
                    op("dve", lambda: nc.vector.tensor_tensor(out=m0b.t[:, :], in0=m0b.t[:, :], in1=mlb16.t[:, :], op=ALU.subtract), colres, colres)
                    op("act", lambda: nc.scalar.activation(out=m0b.t[:, :], in_=m0b.t[:, :], func=AF.Exp), colres, colres)
                    copy("dve", wtb.t[:], cc_.t[0:16, :], colres, [wtb.r])
                    op("dve", lambda: nc.vector.tensor_scalar(out=kg.t[:], in0=kb.t[:], scalar1=cc_.t[0:16, :], scalar2=None, op0=ALU.mult),
                       [kb.r] + colres, [kg.r])
                    for dc in range(2):
                        ps = psA.next()
                        mm(ps.t[:, 0:512], kg.t[:, dc * 128:(dc + 1) * 128], vb.t[:], True, True, [kg.r, vb.r], [ps.r])
                        cs_ = cstr.next()
                        op("dve", lambda: nc.vector.scalar_tensor_tensor(out=cs_.t[:], in0=c0f.t[:, dc, :], scalar=m0b.t[:, :], in1=ps.t[:, 0:512],
                                                                         op0=ALU.mult, op1=ALU.add), [c0f.r, ps.r] + colres, [cs_.r])
                        dma("sq", s_C[sh, dc * 128:(dc + 1) * 128, :], cs_.t[:], reads=[cs_.r])
                    ps = psA.next()
                    mm(ps.t[0:1, 0:256], wtb.t[:, :], kb.t[:], True, True, [wtb.r, kb.r], [ps.r])
                    op("dve", lambda: nc.vector.scalar_tensor_tensor(out=nst.t[:], in0=n0row.t[:], scalar=m0b.t[0:1, :], in1=ps.t[0:1, 0:256],
                                                                     op0=ALU.mult, op1=ALU.add), [n0row.r, ps.r] + colres, [nst.r])
                    dma("sq", s_n[sh:sh + 1, :], nst.t[:], reads=[nst.r])
        if stop_after <= 6:
            K.finish()
            return nc

        tiles_own = [(ti * 512, 512, [(0, 512, 0)]) for ti in range(NTO)] + [(OWN, 64, sample_groups)]

        with K.scope():
            aT = K.sb([128, 32, 512], BF16)
            yT = K.sb([128, 32, 512], F32)
            rowsb = Ring([K.sb([128, 4096], BF16) for _ in range(1)])
            rows_ring = Ring([K.sb([128, 4096], F32) for _ in range(1)])
            sq_ring = Ring([K.sb([128, 512], BF16) for _ in range(3)])
            rbc = K.sb([128, 512], F32)
            WL = WLoader(3)
            for (c0, T, groups) in tiles_own:
                prep_plain(A_s[c0:c0 + T, :], T, aT, rowsb)
                for cb in range(16):
                    wb, wv = WL.load(f"wo{cb}")
                    for cc in range(2):
                        ch = cb * 2 + cc
                        ps = psA.next()
                        for k in range(32):
                            mm(ps.t[:, 0:T], wv[:, k, cc * 128:(cc + 1) * 128], aT.t[:, k, 0:T], k == 0, k == 31, [wb.r, aT.r], [ps.r])
                        copy(K.alt(), yT.t[:, ch, 0:T], ps.t[:, 0:T], [ps.r], [yT.r])
                src = x_own[c0:c0 + T, :] if c0 < OWN else x_smp
                post(yT, T, G1, groups, src, X1[c0:c0 + T, :], rows_ring, sq_ring, rbc)
        if stop_after <= 7:
            K.finish()
            return nc

        with K.scope():
            uT = K.sb([128, 32, 512], BF16)
            facc = K.sb([128, 32, 512], F32)
            rows_ring = Ring([K.sb([128, 4096], F32) for _ in range(1)])
            sq_ring = Ring([K.sb([128, 512], BF16) for _ in range(2)])
            rbc = K.sb([128, 512], F32)
            ss = K.sb([128, 1], F32); rstd = K.sb([128, 1], F32)
            r32 = Ring([K.sb([128, 512], F32) for _ in range(2)])
            hT = K.sb([128, 4, 512], BF16)
            WL = WLoader(4)
            jk = View(facc.t[:, 0:4, :].rearrange("p a b -> p (a b)").bitcast(BF16), facc.r)
            for (c0, T, groups) in tiles_own:
                prep(X1[c0:c0 + T, :], T, A2, B2, groups, uT, rows_ring, jk, ss, rstd)
                for hp in range(NHB // 2):
                    w1p = [WL.load(f"f1_{2 * hp + j}") for j in range(2)]
                    w2p = [WL.load(f"f2_{2 * hp + j}") for j in range(2)]
                    for j in range(2):
                        w1b, w1 = w1p[j]
                        for cc in range(2):
                            ps = psA.next()
                            for k in range(32):
                                mm(ps.t[:, 0:T], w1[:, k, cc * 128:(cc + 1) * 128], uT.t[:, k, 0:T], k == 0, k == 31, [w1b.r, uT.r], [ps.r])
                            r_ = r32.next()
                            op("act", lambda: nc.scalar.activation(out=r_.t[:, 0:T], in_=ps.t[:, 0:T], func=AF.Relu), [ps.r], [r_.r])
                            op("dve", lambda: nc.vector.tensor_tensor(out=hT.t[:, 2 * j + cc, 0:T], in0=r_.t[:, 0:T], in1=r_.t[:, 0:T], op=ALU.mult),
                               [r_.r], [hT.r])
                    for oc in range(32):
                        ps = psA.next()
                        for kk in range(4):
                            j, k = divmod(kk, 2)
                            w2b, w2 = w2p[j]
                            mm(ps.t[:, 0:T], w2[:, k, oc * 128:(oc + 1) * 128], hT.t[:, kk, 0:T], kk == 0, kk == 3, [w2b.r, hT.r], [ps.r])
                        f_ = facc.t[:, oc, 0:T]
                        if hp == 0:
                            copy(K.alt(), f_, ps.t[:, 0:T], [ps.r], [facc.r])
                        else:
                            op("dve", lambda: nc.vector.tensor_tensor(out=f_, in0=ps.t[:, 0:T], in1=f_, op=ALU.add), [ps.r, facc.r], [facc.r])
                dst = y_own[c0:c0 + T, :] if c0 < OWN else y_smp
                post(facc, T, G2, groups, X1[c0:c0 + T, :], dst, rows_ring, sq_ring, rbc)
        K.finish()
    return nc


def _fm(v, n):
    return np.ascontiguousarray(np.asarray(v, np.float32).reshape(n, 128).T)


def host_inputs(inp, SEQ, PAST):
    f = lambda a: np.ascontiguousarray(np.asarray(a, np.float32))
    NBLK, OWN = SEQ // 128, SEQ // 4
    NOB, NQ = OWN // 128, OWN // 4 * 0 + OWN + 64
    inv = (10000.0 ** (-np.arange(32, dtype=np.float32) / 32)).astype(np.float32)

    def cs_table(pos):
        ang = pos.astype(np.float32)[:, None] * inv[None, :]
        return np.concatenate([np.cos(ang), np.sin(ang)], axis=1).astype(np.float32)

    shared = dict(
        w_ada=f(inp["w_ada"][0]), b_ada_fm=_fm(inp["b_ada"][0], 192),
        g_pre1_fm=_fm(inp["g_pre1"][0], 32), g_post1_fm=_fm(inp["g_post1"][0], 32),
        g_pre2_fm=_fm(inp["g_pre2"][0], 32), g_post2_fm=_fm(inp["g_post2"][0], 32),
        w_in=f(inp["w_in"][0]), b_ig=f(inp["b_ig"][0]).reshape(4, 1), b_fg=f(inp["b_fg"][0]).reshape(4, 1),
        g_mlnorm=f(inp["g_mlnorm"][0]).reshape(1, 2048), g_cq_fm=_fm(inp["g_cq"][0], 8),
        w_uq=f(inp["w_uq"][0]).reshape(1024, 3072), g_ckv=f(inp["g_ckv"][0]).reshape(1, 512),
        w_uk=f(inp["w_uk"][0]).reshape(512, 2048), w_uv=f(inp["w_uv"][0]).reshape(512, 2048),
        w_out=f(inp["w_out"][0]), w_ff1=f(inp["w_ff1"][0]), w_ff2=f(inp["w_ff2"][0]),
        ident=np.eye(128, dtype=np.float32), cs_all=cs_table(np.arange(SEQ)),
        tril16=np.tril(np.ones((16, 16), np.float32)),
    )
    xp, xs = f(inp["x_prompt"]), f(inp["x_sample"])
    maps = []
    kk = np.arange(512)[None, :]
    for c in range(8):
        b, j = c // 4, c % 4
        qq = (j * 128 + np.arange(128))[:, None]
        c5 = np.concatenate([f(inp["c_prompt"])[b:b + 1], f(inp["c_sample"])[4 * c:4 * c + 4]], axis=0)
        own_pos = (np.arange(NOB)[:, None] * 4 + j) * 128 + np.arange(128)[None, :]
        sel = np.zeros((NBLK, NOB), np.float32)
        sel[np.arange(NOB) * 4 + j, np.arange(NOB)] = 1.0
        m = dict(shared)
        m.update(
            x_all=xp[b], x_own=np.ascontiguousarray(xp[b].reshape(NBLK, 128, 4096)[j::4].reshape(OWN, 4096)),
            x_smp=np.ascontiguousarray(xs[4 * c:4 * c + 4].reshape(64, 4096)),
            c_T=np.ascontiguousarray(c5.T.reshape(32, 128, 5).transpose(1, 0, 2)),
            ckv_past=f(inp["cache_mla_ckv"][0, 4 * c:4 * c + 4]), kr_past=f(inp["cache_mla_krope"][0, 4 * c:4 * c + 4]),
            C0=f(inp["state_mlstm_C"][0, 4 * c:4 * c + 4]).reshape(16, 256, 512),
            n0=f(inp["state_mlstm_n"][0, 4 * c:4 * c + 4]).reshape(16, 256),
            m0T=np.ascontiguousarray(f(inp["state_mlstm_m"][0, 4 * c:4 * c + 4]).reshape(4, 4).T),
            cs_own=np.concatenate([cs_table(own_pos.reshape(-1)), cs_table(PAST + np.tile(np.arange(16), 4))], axis=0),
            bias_mla=np.where(kk // 64 <= qq // 64, 0.0, -1e9).astype(np.float32),
            mask_ml=(kk <= qq).astype(np.float32),
            sel=sel,
        )
        maps.append(m)
    return maps


def assemble(results, SEQ, PAST, BATCH=2):
    NBLK, OWN = SEQ // 128, SEQ // 4
    NOB = OWN // 128
    y_p = np.zeros((BATCH, SEQ, 4096), np.float32)
    y_s = np.zeros((32, 16, 4096), np.float32)
    p_ckv = np.zeros((1, BATCH, SEQ, 512), np.float32); p_kr = np.zeros((1, BATCH, SEQ, 64), np.float32)
    p_C = np.zeros((1, BATCH, 4, 256, 512), np.float32); p_n = np.zeros((1, BATCH, 4, 256), np.float32)
    p_m = np.zeros((1, BATCH, 4), np.float32)
    s_ckv = np.zeros((1, 32, 16, 512), np.float32); s_kr = np.zeros((1, 32, 16, 64), np.float32)
    s_C = np.zeros((1, 32, 4, 256, 512), np.float32); s_n = np.zeros((1, 32, 4, 256), np.float32)
    s_m = np.zeros((1, 32, 4), np.float32)
    for c, r in enumerate(results):
        b, j = c // 4, c % 4
        y_p[b].reshape(NBLK, 128, 4096)[j::4] = np.asarray(r["y_own"], np.float32).reshape(NOB, 128, 4096)
        y_s[4 * c:4 * c + 4] = np.asarray(r["y_smp"], np.float32).reshape(4, 16, 4096)
        if j == 0:
            p_ckv[0, b] = r["ckv_o"]; p_kr[0, b] = r["kr_o"]; p_C[0, b] = r["C_o"]; p_n[0, b] = r["n_o"]
            p_m[0, b] = np.asarray(r["m_o"]).reshape(4)
        s_ckv[0, 4 * c:4 * c + 4] = np.asarray(r["s_ckv"]).reshape(4, 16, 512)
        s_kr[0, 4 * c:4 * c + 4] = np.asarray(r["s_kr"]).reshape(4, 16, 64)
        s_C[0, 4 * c:4 * c + 4] = np.asarray(r["s_C"]).reshape(4, 4, 256, 512)
        s_n[0, 4 * c:4 * c + 4] = np.asarray(r["s_n"]).reshape(4, 4, 256)
        s_m[0, 4 * c:4 * c + 4] = np.asarray(r["s_m"]).reshape(4, 4).T
    return (y_p, y_s, p_ckv, p_kr, p_C, p_n, p_m, s_ckv, s_kr, s_C, s_n, s_m)


def kernel(**inputs):
    inp = {k: np.asarray(v) for k, v in inputs.items()}
    SEQ = inp["x_prompt"].shape[1]
    PAST = inp["cache_mla_ckv"].shape[2]
    nc = build_program(SEQ=SEQ, PAST=PAST)
    maps = host_inputs(inp, SEQ, PAST)
    res = run_bass_kernel_spmd(nc, maps, core_ids=list(range(8)))
    return assemble(res.results, SEQ, PAST, BATCH=inp["x_prompt"].shape[0])
```

```python
import contextlib
import numpy as np
import concourse.bass as bass
import concourse.mybir as mybir
from concourse.bass_utils import run_bass_kernel_spmd

F32 = mybir.dt.float32
BF16 = mybir.dt.bfloat16
ALU = mybir.AluOpType
AF = mybir.ActivationFunctionType
AX = mybir.AxisListType
EPS = 1e-6
MLA_SCALE = 192.0 ** -0.5


class Res:
    __slots__ = ("name", "w", "r")

    def __init__(self, name=None):
        self.name = name
        self.w = None
        self.r = []


class Buf:
    __slots__ = ("t", "r")

    def __init__(self, t):
        self.t = t
        self.r = Res()


class View:
    __slots__ = ("t", "r")

    def __init__(self, ap, res):
        self.t = ap
        self.r = res


class Ring:
    def __init__(self, bufs):
        self.b = bufs
        self.i = 0

    def next(self):
        b = self.b[self.i % len(self.b)]
        self.i += 1
        return b


class _Eng:
    W = 24000

    def __init__(self, K, name, handle):
        self.K, self.name, self.h = K, name, handle
        self.count = 0
        self.sems = []
        self.seen = {}

    def sem_for(self, idx):
        w = (idx - 1) // self.W
        while len(self.sems) <= w:
            self.sems.append(self.K.new_sem(f"{self.name}{len(self.sems)}"))
        return self.sems[w], (idx - 1) % self.W + 1

    def key(self, idx):
        return (self.name, (idx - 1) // self.W)


class _Dma(_Eng):
    NS = 8

    def __init__(self, K, name, handle):
        super().__init__(K, name, handle)
        self.slots = [K.new_sem(f"{name}s{i}") for i in range(self.NS)]

    def sem_for(self, idx):
        i = idx - 1
        return self.slots[i % self.NS], 16 * (i // self.NS + 1)

    def key(self, idx):
        return (self.name, (idx - 1) % self.NS)


class Kern:
    def __init__(self, nc, stack):
        self.nc = nc
        self.gstack = stack
        self.stacks = [stack]
        self.e = {
            "pe": _Eng(self, "pe", nc.tensor),
            "act": _Eng(self, "act", nc.scalar),
            "dve": _Eng(self, "dve", nc.vector),
            "pool": _Eng(self, "pool", nc.gpsimd),
            "sq": _Dma(self, "sq", nc.sync),
            "gq": _Dma(self, "gq", nc.gpsimd),
        }
        self._alt = 0
        self.nsb = 0

    def new_sem(self, name):
        return self.gstack.enter_context(self.nc.semaphore(name))

    def sb(self, shape, dt, name=None):
        self.nsb += 1
        t = self.stacks[-1].enter_context(self.nc.sbuf_tensor(f"t{self.nsb}_{name or 'x'}", list(shape), dt))
        return Buf(t)

    def ps(self, shape, dt, name):
        return Buf(self.gstack.enter_context(self.nc.psum_tensor(name, list(shape), dt)))

    def alt(self):
        self._alt ^= 1
        return "act" if self._alt else "dve"

    @contextlib.contextmanager
    def scope(self):
        st = contextlib.ExitStack()
        self.stacks.append(st)
        try:
            yield
            self.barrier()
        finally:
            self.stacks.pop()
            st.close()

    def _wait(self, E, F, n):
        if E is F and E.name == "pe":
            return
        k = F.key(n)
        if E.seen.get(k, 0) >= n:
            return
        sem, val = F.sem_for(n)
        E.h.wait_ge(sem, val)
        E.seen[k] = n

    def op(self, en, fn, reads=(), writes=(), inc=True):
        E = self.e[en]
        idx = E.count + 1
        isd = isinstance(E, _Dma)
        for r in reads:
            if r.w is not None:
                F, n = r.w
                if not (F is E and n >= idx):
                    self._wait(E, F, n)
        for r in writes:
            if r.w is not None:
                F, n = r.w
                if not (F is E and n >= idx):
                    self._wait(E, F, n)
            for (F, n) in r.r:
                if not (F is E and n >= idx):
                    self._wait(E, F, n)
        if isd and idx > E.NS:
            self._wait(E, E, idx - E.NS)
        ins = fn()
        if inc or isd:
            E.count = idx
            sem, _ = E.sem_for(idx)
            ins.then_inc(sem, 16 if isd else 1)
        for r in writes:
            r.w = (E, idx)
            r.r = []
        for r in reads:
            r.r.append((E, idx))
        return ins

    def barrier(self, with_gq=False):
        for en in ("pe", "act", "dve", "sq"):
            E = self.e[en]
            for fn_ in ("pe", "act", "dve", "pool"):
                F = self.e[fn_]
                if F.count and F is not E:
                    self._wait(E, F, F.count)
            for qn in (("sq", "gq") if with_gq else ("sq",)):
                Q = self.e[qn]
                for i in range(max(0, Q.count - Q.NS), Q.count):
                    self._wait(E, Q, i + 1)

    def finish(self):
        self.barrier(with_gq=True)
        self.nc.kcounts = {k: e.count for k, e in self.e.items()}


C_Q, C_K, C_V, C_O, C_IG, C_FG, C_CQ, C_CKV, C_KR = 0, 1024, 2048, 4096, 6144, 6148, 6152, 7176, 7688


def build_program(SEQ=8192, PAST=1024, debug=False, stop_after=99, NHB=64):
    nc = bass.Bass("TRN2", target_bir_lowering=False)
    NBLK, NTA, OWN = SEQ // 128, SEQ // 512, SEQ // 4
    NOB, NTO, NQ = OWN // 128, OWN // 512, OWN + 64
    NKS = PAST + 16
    NPT = PAST // 512

    def din(name, shape, dt=F32):
        return nc.dram_tensor(name, list(shape), dt, kind="ExternalInput").ap()

    def dout(name, shape, dt=F32):
        return nc.dram_tensor(name, list(shape), dt, kind="ExternalOutput").ap()

    def dscr(name, shape, dt):
        return nc.dram_tensor(name, list(shape), dt, kind="ExternalOutput" if debug else "Internal").ap()

    x_all = din("x_all", [SEQ, 4096]); x_own = din("x_own", [OWN, 4096]); x_smp = din("x_smp", [64, 4096])
    c_T = din("c_T", [128, 32, 5])
    ckv_past = din("ckv_past", [4, PAST, 512]); kr_past = din("kr_past", [4, PAST, 64])
    C0_in = din("C0", [16, 256, 512]); n0_in = din("n0", [16, 256]); m0_in = din("m0T", [4, 4])
    w_ada = din("w_ada", [4096, 24576]); b_ada_fm = din("b_ada_fm", [128, 192])
    gpre1_d = din("g_pre1_fm", [128, 32]); gpost1_d = din("g_post1_fm", [128, 32])
    gpre2_d = din("g_pre2_fm", [128, 32]); gpost2_d = din("g_post2_fm", [128, 32])
    w_in = din("w_in", [4096, 7752]); b_ig_d = din("b_ig", [4, 1]); b_fg_d = din("b_fg", [4, 1])
    g_ml_d = din("g_mlnorm", [1, 2048]); g_cq_d = din("g_cq_fm", [128, 8]); w_uq = din("w_uq", [1024, 3072])
    g_ckv_d = din("g_ckv", [1, 512]); w_uk = din("w_uk", [512, 2048]); w_uv = din("w_uv", [512, 2048])
    w_out = din("w_out", [4096, 4096]); w_ff1 = din("w_ff1", [4096, 16384]); w_ff2 = din("w_ff2", [16384, 4096])
    ident_d = din("ident", [128, 128]); cs_all = din("cs_all", [SEQ, 64]); cs_own = din("cs_own", [NQ, 64])
    bias_mla_d = din("bias_mla", [128, 512]); mask_ml_d = din("mask_ml", [128, 512])
    sel_d = din("sel", [NBLK, NOB]); tril16_d = din("tril16", [16, 16])
    y_own = dout("y_own", [OWN, 4096]); y_smp = dout("y_smp", [64, 4096])
    ckv_o = dout("ckv_o", [SEQ, 512]); kr_o = dout("kr_o", [SEQ, 64])
    C_o = dout("C_o", [4, 256, 512]); n_o = dout("n_o", [4, 256]); m_o = dout("m_o", [4, 1])
    s_ckv = dout("s_ckv", [64, 512]); s_kr = dout("s_kr", [64, 64])
    s_C = dout("s_C", [16, 256, 512]); s_n = dout("s_n", [16, 256]); s_m = dout("s_m", [4, 4])
    NWB = 19 + 19 + 16 + 128
    WB_A = dscr("WB_A", [54, 128, 8192], BF16)
    WB_F1 = dscr("WB_F1", [64, 128, 8192], BF16)
    WB_F2 = dscr("WB_F2", [64, 128, 8192], BF16)

    class _WB:
        def __getitem__(self, i):
            if i < 54:
                return WB_A[i]
            j = i - 54
            return WB_F1[j // 2] if j % 2 == 0 else WB_F2[j // 2]
    WBLK = _WB()
    KT_ml = dscr("KT_ml", [1024, SEQ], BF16); K_tm = dscr("K_tm", [SEQ, 1024], BF16); V_ml = dscr("V_ml", [SEQ, 2048], BF16)
    GATES = dscr("GATES", [8, SEQ], F32); ROWS = dscr("ROWS", [3, 4, SEQ], F32)
    KT_mla = dscr("KT_mla", [2048, SEQ], BF16); KRT = dscr("KRT", [64, SEQ], BF16); V_mla = dscr("V_mla", [SEQ, 2048], BF16)
    QT_ml = dscr("QT_ml", [1024, NQ], BF16); OG = dscr("OG", [NQ, 2048], BF16)
    QN_T = dscr("QN_T", [2048, NQ], BF16); QR_T = dscr("QR_T", [1024, NQ], BF16)
    A_s = dscr("A_s", [NQ, 4096], BF16); X1 = dscr("X1", [NQ, 4096], F32)
    KT_ml_s = dscr("KT_ml_s", [1024, 64], BF16); K_tm_s = dscr("K_tm_s", [64, 1024], BF16); V_ml_s = dscr("V_ml_s", [64, 2048], BF16)
    GATES_s = dscr("GATES_s", [8, 64], F32); ROWS_s = dscr("ROWS_s", [3, 4, 64], F32)
    KT_mla_s = dscr("KT_mla_s", [2048, 64], BF16); KRT_s = dscr("KRT_s", [64, 64], BF16); V_mla_s = dscr("V_mla_s", [64, 2048], BF16)
    KT_mla_p = dscr("KT_mla_p", [4, 2048, PAST], BF16); KRT_p = dscr("KRT_p", [4, 64, PAST], BF16)
    V_mla_p = dscr("V_mla_p", [4, PAST, 2048], BF16)
    junk_ckv = dscr("junk_ckv", [PAST, 512], F32); junk_kr = dscr("junk_kr", [PAST, 64], F32)

    with contextlib.ExitStack() as gst:
        K = Kern(nc, gst)
        op = K.op

        def dma(q, out, in_, reads=(), writes=()):
            h = nc.sync if q == "sq" else nc.gpsimd
            return op(q, lambda: h.dma_start(out=out, in_=in_, allow_slow_non_contiguous=True), reads, writes)

        def copy(en, out, in_, reads, writes):
            if en == "act":
                return op("act", lambda: nc.scalar.copy(out=out, in_=in_), reads, writes)
            h = nc.vector if en == "dve" else nc.gpsimd
            return op(en, lambda: h.tensor_copy(out=out, in_=in_), reads, writes)

        def mm(out, lhsT, rhs, start, stop, reads, writes, inc=None):
            return op("pe", lambda: nc.tensor.matmul(out, lhsT=lhsT, rhs=rhs, start=start, stop=stop),
                      reads, writes, inc=(stop if inc is None else inc))

        def tr(out, in_, idn, reads, writes, inc=True):
            return op("pe", lambda: nc.tensor.transpose(out, in_, idn), reads, writes, inc=inc)

        psA = Ring([K.ps([128, 512], F32, f"psA{i}") for i in range(3)])
        psT = Ring([K.ps([128, 512], F32, f"psT{i}") for i in range(2)])
        psO = Ring([K.ps([128, 512], F32, f"psO{i}") for i in range(2)])
        psX = K.ps([128, 512], F32, "psX")

        def bfv(ps):
            return ps.t[:].bitcast(BF16)

        ident = K.sb([128, 128], F32, "ident"); identb = K.sb([128, 128], BF16, "identb")
        ones_bf = K.sb([128, 128], BF16, "ones_bf")
        mods = K.sb([128, 192, 5], F32, "mods")
        A1 = K.sb([128, 32, 5], F32, "A1"); G1 = K.sb([128, 32, 5], F32, "G1")
        A2 = K.sb([128, 32, 5], F32, "A2"); G2 = K.sb([128, 32, 5], F32, "G2")
        gfm = K.sb([128, 4, 32], F32, "gfm")
        gcq = K.sb([128, 8], F32, "gcq")
        bada = K.sb([128, 192], F32, "bada")
        bigfg = K.sb([4, 2], F32, "bigfg")
        Mown = K.sb([128, 4, NOB], F32, "Mown"); bown = K.sb([128, 4, NOB], F32, "bown")
        negMown = K.sb([128, 4, NOB], F32, "negMown"); ebown = K.sb([128, 4, NOB], F32, "ebown")
        dma("sq", ident.t[:], ident_d, writes=[ident.r])
        dma("gq", identb.t[:], ident_d, writes=[identb.r])
        op("pool", lambda: nc.gpsimd.memset(ones_bf.t[:], 1.0), writes=[ones_bf.r])
        eps_t = K.sb([128, 1], F32, "eps_t")
        op("pool", lambda: nc.gpsimd.memset(eps_t.t[:], EPS), writes=[eps_t.r])
        for i, g in enumerate((gpre1_d, gpost1_d, gpre2_d, gpost2_d)):
            dma("sq", gfm.t[:, i, :], g, writes=[gfm.r])
        dma("sq", gcq.t[:], g_cq_d, writes=[gcq.r])
        dma("sq", bada.t[:], b_ada_fm, writes=[bada.r])
        dma("sq", bigfg.t[:, 0:1], b_ig_d, writes=[bigfg.r])
        dma("sq", bigfg.t[:, 1:2], b_fg_d, writes=[bigfg.r])

        wres = {}
        wshape = {}
        wnext = [0]

        def wconv(name, src3, KC, ncols):
            i = wnext[0]
            wnext[0] += 1
            assert i < NWB and KC * ncols <= 8192
            r = Res(name)
            dst = WBLK[i][:, 0:KC * ncols].rearrange("p (k c) -> p k c", c=ncols)
            dma("gq", dst, src3, writes=[r])
            wres[name] = (i, r)
            wshape[name] = (KC, ncols)

        def w3(src, c0, ncols):
            return src.rearrange("(k p) c -> p k c", p=128)[:, :, c0:c0 + ncols]

        def conv_stage1():
            for i in range(4):
                wconv(f"k{i}", w3(w_in, C_K + 256 * i, 256), 32, 256)
            for i in range(8):
                wconv(f"v{i}", w3(w_in, C_V + 256 * i, 256), 32, 256)
            for i in range(2):
                wconv(f"ckv{i}", w3(w_in, C_CKV + 256 * i, 256), 32, 256)
            wconv("ig", w3(w_in, C_IG, 4), 32, 4)
            wconv("fg", w3(w_in, C_FG, 4), 32, 4)
            wconv("kr", w3(w_in, C_KR, 64), 32, 64)
            wconv("wuk", w3(w_uk, 0, 2048), 4, 2048)
            wconv("wuv", w3(w_uv, 0, 2048), 4, 2048)

        def conv_rest():
            for i in range(4):
                wconv(f"q{i}", w3(w_in, C_Q + 256 * i, 256), 32, 256)
            for i in range(8):
                wconv(f"o{i}", w3(w_in, C_O + 256 * i, 256), 32, 256)
            for i in range(4):
                wconv(f"cq{i}", w3(w_in, C_CQ + 256 * i, 256), 32, 256)
            for i in range(3):
                wconv(f"uq{i}", w3(w_uq, 1024 * i, 1024), 8, 1024)
            for i in range(16):
                wconv(f"wo{i}", w3(w_out, 256 * i, 256), 32, 256)
            for i in range(64):
                wconv(f"f1_{i}", w3(w_ff1, 256 * i, 256), 32, 256)
                wconv(f"f2_{i}", w_ff2[256 * i:256 * (i + 1), :].rearrange("(k p) c -> p k c", p=128), 2, 4096)

        class WLoader:
            def __init__(self, nbuf):
                self.ring = Ring([K.sb([128, 8192], BF16) for _ in range(nbuf)])

            def load(self, name, buf=None):
                i, r = wres[name]
                KC, ncols = wshape[name]
                b = buf or self.ring.next()
                dma("sq", b.t[:, 0:KC * ncols], WBLK[i][:, 0:KC * ncols], reads=[r], writes=[b.r])
                return b, b.t[:, 0:KC * ncols].rearrange("p (k c) -> p k c", c=ncols)

        def rstd_from_ss(out, ss, inv_n, reads, writes):
            P = out.shape[0]
            op("act", lambda: nc.scalar.activation(out=out, in_=ss, func=AF.Sqrt, bias=eps_t.t[0:P, :], scale=inv_n),
               list(reads) + [eps_t.r], writes)
            op("dve", lambda: nc.vector.reciprocal(out=out, in_=out), list(writes), writes)

        sample_groups = [(16 * s, 16, 1 + s) for s in range(4)]

        def prep(src, T, Amod, Bmod, groups, uT, rows_ring, junk, ss, rstd):
            nsub = (T + 127) // 128
            for s in range(nsub):
                rows = min(128, T - s * 128)
                xt = rows_ring.next()
                dma("sq", xt.t[:rows, :], src[s * 128:s * 128 + rows, :], writes=[xt.r])
                op("act", lambda: nc.scalar.activation(out=junk.t[:rows, :], in_=xt.t[:rows, :], func=AF.Square,
                                                       accum_out=ss.t[:rows, :]), [xt.r], [junk.r, ss.r])
                rstd_from_ss(rstd.t[:rows, :], ss.t[:rows, :], 1.0 / 4096, [ss.r], [rstd.r])
                op("act", lambda: nc.scalar.activation(out=xt.t[:rows, :], in_=xt.t[:rows, :], func=AF.Copy,
                                                       scale=rstd.t[:rows, :]), [rstd.r, xt.r], [xt.r])
                for g in range(8):
                    pt = psT.next()
                    for i in range(4):
                        k = g * 4 + i
                        tr(pt.t[:, i * 128:i * 128 + rows], xt.t[:rows, k * 128:(k + 1) * 128], ident.t[:rows, :rows],
                           [xt.r, ident.r], [pt.r], inc=(i == 3))
                    for i in range(4):
                        k = g * 4 + i
                        for (c0, n, r) in groups:
                            lo, hi = max(c0, s * 128), min(c0 + n, s * 128 + rows)
                            if lo >= hi:
                                continue
                            o_ = uT.t[:, k, lo:hi]
                            i_ = pt.t[:, i * 128 + lo - s * 128:i * 128 + hi - s * 128]
                            a_, b_ = Amod.t[:, k, r:r + 1], Bmod[:, k, r:r + 1]
                            if K.alt() == "act":
                                op("act", lambda: nc.scalar.activation(out=o_, in_=i_, func=AF.Identity, bias=b_, scale=a_),
                                   [pt.r, Amod.r, mods.r], [uT.r])
                            else:
                                op("dve", lambda: nc.vector.tensor_scalar(out=o_, in0=i_, scalar1=a_, scalar2=b_,
                                                                          op0=ALU.mult, op1=ALU.add),
                                   [pt.r, Amod.r, mods.r], [uT.r])

        def prep_plain(src, T, aT, rowsb_ring):
            nsub = (T + 127) // 128
            for s in range(nsub):
                rows = min(128, T - s * 128)
                xt = rowsb_ring.next()
                dma("sq", xt.t[:rows, :], src[s * 128:s * 128 + rows, :], writes=[xt.r])
                for g in range(4):
                    pt = psT.next()
                    pv = bfv(pt)
                    for i in range(8):
                        k = g * 8 + i
                        tr(pv[:, i * 128:i * 128 + rows], xt.t[:rows, k * 128:(k + 1) * 128], identb.t[:rows, :rows],
                           [xt.r, identb.r], [pt.r], inc=(i == 7))
                    o_ = aT.t[:, g * 8:(g + 1) * 8, s * 128:s * 128 + rows]
                    i_ = pv.rearrange("p (a b) -> p a b", b=128)[:, :, 0:rows]
                    copy(K.alt(), o_, i_, [pt.r], [aT.r])

        def post(yT, T, Gmod, groups, res_src, dst, rows_ring, sq_ring, rbc):
            for k in range(32):
                sqb = sq_ring.next()
                op("act", lambda: nc.scalar.activation(out=sqb.t[:, 0:T], in_=yT.t[:, k, 0:T], func=AF.Square),
                   [yT.r], [sqb.r])
                mm(psX.t[:, 0:T], ones_bf.t[:], sqb.t[:, 0:T], k == 0, k == 31, [ones_bf.r, sqb.r], [psX.r], inc=True)
            rstd_from_ss(rbc.t[:, 0:T], psX.t[:, 0:T], 1.0 / 4096, [psX.r], [rbc.r])
            for k in range(32):
                for (c0, n, r) in groups:
                    y_ = yT.t[:, k, c0:c0 + n]
                    op("dve", lambda: nc.vector.scalar_tensor_tensor(out=y_, in0=y_, scalar=Gmod.t[:, k, r:r + 1],
                                                          in1=rbc.t[:, c0:c0 + n], op0=ALU.mult, op1=ALU.mult),
                       [yT.r, Gmod.r, rbc.r], [yT.r])
            nsub = (T + 127) // 128
            for s in range(nsub):
                rows = min(128, T - s * 128)
                xr = rows_ring.next()
                dma("sq", xr.t[:rows, :], res_src[s * 128:s * 128 + rows, :], writes=[xr.r])
                for g in range(8):
                    pt = psT.next()
                    for i in range(4):
                        tr(pt.t[:rows, i * 128:(i + 1) * 128], yT.t[:, g * 4 + i, s * 128:s * 128 + rows], ident.t[:],
                           [yT.r, ident.r], [pt.r], inc=(i == 3))
                    x_ = xr.t[:rows, g * 512:(g + 1) * 512]
                    op("dve", lambda: nc.vector.tensor_tensor(out=x_, in0=pt.t[:rows, 0:512], in1=x_, op=ALU.add),
                       [pt.r, xr.r], [xr.r])
                dma("sq", dst[s * 128:s * 128 + rows, :], xr.t[:rows, :], reads=[xr.r])

        conv_stage1()
        with K.scope():
            cT = K.sb([128, 32, 5], F32); cs = K.sb([128, 32, 5], F32)
            dma("sq", cT.t[:], c_T, writes=[cT.r])
            op("act", lambda: nc.scalar.activation(out=cs.t[:], in_=cT.t[:], func=AF.Silu), [cT.r], [cs.r])
            wr = Ring([K.sb([128, 32, 512], F32) for _ in range(2)])
            t5r = Ring([K.sb([8, 512], F32) for _ in range(2)])
            wav = w_ada.rearrange("(k p) c -> p k c", p=128)
            conv_rest()
            for cb in range(48):
                wb = wr.next()
                dma("sq", wb.t[:], wav[:, :, cb * 512:(cb + 1) * 512], writes=[wb.r])
                ps = psA.next()
                for k in range(32):
                    mm(ps.t[0:5, 0:512], cs.t[:, k, :], wb.t[:, k, :], k == 0, k == 31, [cs.r, wb.r], [ps.r])
                t5 = t5r.next()
                copy("act", t5.t[0:5, :], ps.t[0:5, 0:512], [ps.r], [t5.r])
                pt = psT.next()
                for i in range(4):
                    tr(pt.t[:, i * 5:(i + 1) * 5], t5.t[0:5, i * 128:(i + 1) * 128], ident.t[0:5, 0:5],
                       [t5.r, ident.r], [pt.r], inc=(i == 3))
                op("dve", lambda: nc.vector.tensor_tensor(
                    out=mods.t[:, 4 * cb:4 * cb + 4, :], in0=pt.t[:, 0:20].rearrange("p (a b) -> p a b", b=5),
                    in1=bada.t[:, 4 * cb:4 * cb + 4, None].to_broadcast([128, 4, 5]), op=ALU.add),
                   [pt.r, bada.r], [mods.r])

            def bc5(i):
                return gfm.t[:, i, :, None].to_broadcast([128, 32, 5])
            op("dve", lambda: nc.vector.scalar_tensor_tensor(out=A1.t[:], in0=mods.t[:, 32:64, :], scalar=1.0, in1=bc5(0),
                                                             op0=ALU.add, op1=ALU.mult), [mods.r, gfm.r], [A1.r])
            op("dve", lambda: nc.vector.tensor_tensor(out=G1.t[:], in0=mods.t[:, 64:96, :], in1=bc5(1), op=ALU.mult),
               [mods.r, gfm.r], [G1.r])
            op("dve", lambda: nc.vector.scalar_tensor_tensor(out=A2.t[:], in0=mods.t[:, 128:160, :], scalar=1.0, in1=bc5(2),
                                                             op0=ALU.add, op1=ALU.mult), [mods.r, gfm.r], [A2.r])
            op("dve", lambda: nc.vector.tensor_tensor(out=G2.t[:], in0=mods.t[:, 160:192, :], in1=bc5(3), op=ALU.mult),
               [mods.r, gfm.r], [G2.r])
        B1 = mods.t[:, 0:32, :]
        B2 = mods.t[:, 96:128, :]
        if stop_after <= 0:
            K.finish()
            return nc

        def mla_up(ckv32, kro, T, wuk, wuv, dKT, dKRT, dV, c0, L):
            nsub = (T + 127) // 128
            rws = [min(128, T - s * 128) for s in range(nsub)]
            op("dve", lambda: nc.vector.tensor_copy(out=L["ckvb"].t[:, 0:nsub, :], in_=ckv32.t[:, 0:nsub, :]),
               [ckv32.r], [L["ckvb"].r])
            op("dve", lambda: nc.vector.tensor_copy(out=L["krb"].t[:, 0:nsub, :], in_=kro.t[:, 0:nsub, :]),
               [kro.r], [L["krb"].r])
            for c in range(4):
                pt = psT.next()
                pv = bfv(pt)
                for s in range(nsub):
                    tr(pv[:, s * 128:s * 128 + rws[s]], L["ckvb"].t[:rws[s], s, c * 128:(c + 1) * 128],
                       identb.t[:rws[s], :rws[s]], [L["ckvb"].r, identb.r], [pt.r], inc=(s == nsub - 1))
                copy(K.alt(), L["ckvT"].t[:, c, 0:T], pv[:, 0:T], [pt.r], [L["ckvT"].r])
            pt = psT.next()
            pv = bfv(pt)
            for s in range(nsub):
                tr(pv[0:64, s * 128:s * 128 + rws[s]], L["krb"].t[:rws[s], s, :], identb.t[:rws[s], :rws[s]],
                   [L["krb"].r, identb.r], [pt.r], inc=(s == nsub - 1))
            copy(K.alt(), L["krT"].t[:, 0:T], pv[0:64, 0:T], [pt.r], [L["krT"].r])
            dma("sq", dKRT[:, c0:c0 + T], L["krT"].t[:, 0:T], reads=[L["krT"].r])
            for h in range(16):
                ps = psA.next()
                for c in range(4):
                    mm(ps.t[:, 0:T], wuk[1][:, c, h * 128:(h + 1) * 128], L["ckvT"].t[:, c, 0:T], c == 0, c == 3,
                       [wuk[0].r, L["ckvT"].r], [ps.r])
                st = L["st512"].next()
                copy(K.alt(), st.t[:, 0:T], ps.t[:, 0:T], [ps.r], [st.r])
                dma("sq", dKT[h * 128:(h + 1) * 128, c0:c0 + T], st.t[:, 0:T], reads=[st.r])
            vst = L["vst"]
            for s in range(nsub):
                for cb in range(4):
                    ps = psA.next()
                    for c in range(4):
                        mm(ps.t[:rws[s], 0:512], L["ckvT"].t[:, c, s * 128:s * 128 + rws[s]], wuv[1][:, c, cb * 512:(cb + 1) * 512],
                           c == 0, c == 3, [wuv[0].r, L["ckvT"].r], [ps.r])
                    copy(K.alt(), vst.t[:rws[s], s, cb * 512:(cb + 1) * 512], ps.t[:rws[s], 0:512], [ps.r], [vst.r])
            if T % 128 == 0:
                dma("sq", dV[c0:c0 + T, :].rearrange("(s p) f -> p s f", p=128), vst.t[:, 0:nsub, :], reads=[vst.r])
            else:
                dma("sq", dV[c0:c0 + T, :], vst.t[:T, 0, :], reads=[vst.r])

        def tm_out(dst, c0, T, buf, width):
            if T % 128 == 0:
                dma("sq", dst[c0:c0 + T, :].rearrange("(s p) f -> p s f", p=128), buf.t[:, 0:T // 128, 0:width], reads=[buf.r])
            else:
                dma("sq", dst[c0:c0 + T, :], buf.t[:T, 0, 0:width], reads=[buf.r])

        def tm_in(buf, src, c0, T, width):
            if T % 128 == 0:
                dma("sq", buf.t[:, 0:T // 128, 0:width], src[c0:c0 + T, :].rearrange("(s p) f -> p s f", p=128), writes=[buf.r])
            else:
                dma("sq", buf.t[:T, 0, 0:width], src[c0:c0 + T, :], writes=[buf.r])

        def rope_tm(x, cst, nsub, rows, L, out):
            x1, x2 = x.t[:rows, 0:nsub, 0:32], x.t[:rows, 0:nsub, 32:64]
            co, si = cst.t[:rows, 0:nsub, 0:32], cst.t[:rows, 0:nsub, 32:64]
            t = [L["rt"].t[:rows, i, 0:nsub, :] for i in range(4)]
            rr = [x.r, cst.r]
            op("dve", lambda: nc.vector.tensor_tensor(out=t[0], in0=x1, in1=co, op=ALU.mult), rr, [L["rt"].r])
            op("dve", lambda: nc.vector.tensor_tensor(out=t[1], in0=x2, in1=si, op=ALU.mult), rr, [L["rt"].r])
            op("dve", lambda: nc.vector.tensor_tensor(out=t[2], in0=x2, in1=co, op=ALU.mult), rr, [L["rt"].r])
            op("dve", lambda: nc.vector.tensor_tensor(out=t[3], in0=x1, in1=si, op=ALU.mult), rr, [L["rt"].r])
            op("dve", lambda: nc.vector.tensor_tensor(out=out.t[:rows, 0:nsub, 0:32], in0=t[0], in1=t[1], op=ALU.subtract),
               [L["rt"].r], [out.r])
            op("dve", lambda: nc.vector.tensor_tensor(out=out.t[:rows, 0:nsub, 32:64], in0=t[2], in1=t[3], op=ALU.add),
               [L["rt"].r], [out.r])

        def kv_side(uT, T, c0, D, WL, wuk, wuv, L, cs_src):
            nsub = (T + 127) // 128
            rws = [min(128, T - s * 128) for s in range(nsub)]
            ktm = L["ktm"]
            for cb in range(4):
                wb, wv = WL.load(f"k{cb}")
                for cc in range(2):
                    ch = cb * 2 + cc
                    ps = psA.next()
                    for k in range(32):
                        mm(ps.t[:, 0:T], wv[:, k, cc * 128:(cc + 1) * 128], uT.t[:, k, 0:T], k == 0, k == 31, [wb.r, uT.r], [ps.r])
                    st = L["st512"].next()
                    if K.alt() == "act":
                        op("act", lambda: nc.scalar.activation(out=st.t[:, 0:T], in_=ps.t[:, 0:T], func=AF.Copy, scale=0.0625),
                           [ps.r], [st.r])
                    else:
                        op("dve", lambda: nc.vector.tensor_scalar(out=st.t[:, 0:T], in0=ps.t[:, 0:T], scalar1=0.0625,
                                                                  scalar2=None, op0=ALU.mult), [ps.r], [st.r])
                    dma("sq", D["KT_ml"][ch * 128:(ch + 1) * 128, c0:c0 + T], st.t[:, 0:T], reads=[st.r])
                    pt = psT.next()
                    pv = bfv(pt)
                    for s in range(nsub):
                        tr(pv[:rws[s], s * 128:(s + 1) * 128], st.t[:, s * 128:s * 128 + rws[s]], identb.t[:],
                           [st.r, identb.r], [pt.r], inc=(s == nsub - 1))
                    rmax = rws[0]
                    copy(K.alt(), ktm.t[:rmax, 0:nsub, ch * 128:(ch + 1) * 128],
                         pv.rearrange("p (a b) -> p a b", b=128)[:rmax, 0:nsub, :], [pt.r], [ktm.r])
            tm_out(D["K_tm"], c0, T, ktm, 1024)
            vst = L["vst"]
            for cb in range(8):
                wb, wv = WL.load(f"v{cb}")
                for s in range(nsub):
                    ps = psA.next()
                    for k in range(32):
                        mm(ps.t[:rws[s], 0:256], uT.t[:, k, s * 128:s * 128 + rws[s]], wv[:, k, :], k == 0, k == 31, [wb.r, uT.r], [ps.r])
                    copy(K.alt(), vst.t[:rws[s], s, cb * 256:(cb + 1) * 256], ps.t[:rws[s], 0:256], [ps.r], [vst.r])
            tm_out(D["V_ml"], c0, T, vst, 2048)
            for gi, nm in enumerate(("ig", "fg")):
                wb, wv = WL.load(nm)
                ps = psA.next()
                for k in range(32):
                    mm(ps.t[0:4, 0:T], wv[:, k, :], uT.t[:, k, 0:T], k == 0, k == 31, [wb.r, uT.r], [ps.r])
                gs = L["gst"].next()
                op("act", lambda: nc.scalar.activation(out=gs.t[0:4, 0:T], in_=ps.t[0:4, 0:T], func=AF.Identity,
                                                       bias=bigfg.t[:, gi:gi + 1], scale=1.0), [ps.r, bigfg.r], [gs.r])
                dma("sq", D["GATES"][4 * gi:4 * gi + 4, c0:c0 + T], gs.t[0:4, 0:T], reads=[gs.r])
            ckv32 = L["ckv32"]
            for cb in range(2):
                wb, wv = WL.load(f"ckv{cb}")
                for s in range(nsub):
                    ps = psA.next()
                    for k in range(32):
                        mm(ps.t[:rws[s], 0:256], uT.t[:, k, s * 128:s * 128 + rws[s]], wv[:, k, :], k == 0, k == 31, [wb.r, uT.r], [ps.r])
                    copy(K.alt(), ckv32.t[:rws[s], s, cb * 256:(cb + 1) * 256], ps.t[:rws[s], 0:256], [ps.r], [ckv32.r])
            wb, wv = WL.load("kr")
            kr32 = L["kr32"]
            for s in range(nsub):
                ps = psA.next()
                for k in range(32):
                    mm(ps.t[:rws[s], 0:64], uT.t[:, k, s * 128:s * 128 + rws[s]], wv[:, k, :], k == 0, k == 31, [wb.r, uT.r], [ps.r])
                copy(K.alt(), kr32.t[:rws[s], s, :], ps.t[:rws[s], 0:64], [ps.r], [kr32.r])
            ss4, rs4 = L["ss4"], L["rs4"]
            for s in range(nsub):
                op("act", lambda: nc.scalar.activation(out=L["junk512"].t[:rws[s], :], in_=ckv32.t[:rws[s], s, :], func=AF.Square,
                                                       accum_out=ss4.t[:rws[s], s:s + 1]), [ckv32.r], [L["junk512"].r, ss4.r])
            rstd_from_ss(rs4.t[:rws[0], 0:nsub], ss4.t[:rws[0], 0:nsub], 1.0 / 512, [ss4.r], [rs4.r])
            for s in range(nsub):
                c_ = ckv32.t[:rws[s], s, :]
                op("dve", lambda: nc.vector.scalar_tensor_tensor(out=c_, in0=c_, scalar=rs4.t[:rws[s], s:s + 1],
                                                                 in1=L["gckv"].t[:rws[s], :], op0=ALU.mult, op1=ALU.mult),
                   [ckv32.r, rs4.r, L["gckv"].r], [ckv32.r])
            tm_out(D["ckv_o"], c0, T, ckv32, 512)
            cst = L["cst"]
            tm_in(cst, cs_src, 0, T, 64)
            rope_tm(kr32, cst, nsub, rws[0], L, L["kro"])
            tm_out(D["kr_o"], c0, T, L["kro"], 64)
            mla_up(ckv32, L["kro"], T, wuk, wuv, D["KT_mla"], D["KRT"], D["V_mla"], c0, L)

        def kv_locals():
            L = {}
            L["ktm"] = K.sb([128, 4, 1024], BF16)
            L["vst"] = K.sb([128, 4, 2048], BF16)
            L["st512"] = Ring([K.sb([128, 512], BF16) for _ in range(3)])
            L["gst"] = Ring([K.sb([4, 512], F32) for _ in range(2)])
            L["ckv32"] = K.sb([128, 4, 512], F32)
            L["kr32"] = K.sb([128, 4, 64], F32)
            L["kro"] = K.sb([128, 4, 64], F32)
            L["cst"] = K.sb([128, 4, 64], F32)
            L["rt"] = K.sb([128, 4, 4, 32], F32)
            L["ss4"] = K.sb([128, 4], F32)
            L["rs4"] = K.sb([128, 4], F32)
            L["junk512"] = K.sb([128, 512], BF16)
            L["gckv"] = K.sb([128, 512], F32)
            L["ckvb"] = K.sb([128, 4, 512], BF16)
            L["krb"] = K.sb([128, 4, 64], BF16)
            L["ckvT"] = K.sb([128, 4, 512], BF16)
            L["krT"] = K.sb([64, 512], BF16)
            dma("sq", L["gckv"].t[:], g_ckv_d.partition_broadcast(128), writes=[L["gckv"].r])
            return L

        Dp = dict(KT_ml=KT_ml, K_tm=K_tm, V_ml=V_ml, GATES=GATES, ckv_o=ckv_o, kr_o=kr_o, KT_mla=KT_mla, KRT=KRT, V_mla=V_mla)
        Ds = dict(KT_ml=KT_ml_s, K_tm=K_tm_s, V_ml=V_ml_s, GATES=GATES_s, ckv_o=s_ckv, kr_o=s_kr, KT_mla=KT_mla_s, KRT=KRT_s,
                  V_mla=V_mla_s)

        with K.scope():
            uT = K.sb([128, 32, 512], BF16)
            rows_ring = Ring([K.sb([128, 4096], F32) for _ in range(2)])
            ss = K.sb([128, 1], F32); rstd = K.sb([128, 1], F32)
            WL = WLoader(2)
            L = kv_locals()
            junk = View(L["vst"].t[:, 0:2, :].rearrange("p a b -> p (a b)"), L["vst"].r)
            wukb = K.sb([128, 8192], BF16); wuvb = K.sb([128, 8192], BF16)
            wuk = WL.load("wuk", wukb); wuv = WL.load("wuv", wuvb)
            for ti in range(NTA):
                prep(x_all[ti * 512:(ti + 1) * 512, :], 512, A1, B1, [(0, 512, 0)], uT, rows_ring, junk, ss, rstd)
                kv_side(uT, 512, ti * 512, Dp, WL, wuk, wuv, L, cs_all[ti * 512:(ti + 1) * 512, :])
            prep(x_smp, 64, A1, B1, sample_groups, uT, rows_ring, junk, ss, rstd)
            kv_side(uT, 64, 0, Ds, WL, wuk, wuv, L, cs_own[OWN:OWN + 64, :])
            for sq_ in range(4):
                for ti in range(NPT):
                    tm_in(L["ckv32"], ckv_past[sq_], ti * 512, 512, 512)
                    tm_in(L["kro"], kr_past[sq_], ti * 512, 512, 64)
                    mla_up(L["ckv32"], L["kro"], 512, wuk, wuv, KT_mla_p[sq_], KRT_p[sq_], V_mla_p[sq_], ti * 512, L)
        if stop_after <= 1:
            K.finish()
            return nc

        def vmemset(buf, ap, val):
            op("dve", lambda: nc.vector.memset(ap, val), [], [buf.r])

        with K.scope():
            SG = min(2048, SEQ)
            ones4 = K.sb([4, SG], F32); vmemset(ones4, ones4.t[:], 1.0)
            one1 = K.sb([4, 1], F32); vmemset(one1, one1.t[:], 1.0)
            bprev = K.sb([4, 1], F32); Mprev = K.sb([4, 1], F32)
            igt = K.sb([4, SG], F32); fgt = K.sb([4, SG], F32); bt = K.sb([4, SG], F32)
            ct = K.sb([4, SG], F32); Mt = K.sb([4, SG], F32)
            m0T = K.sb([4, 4], F32); smt = K.sb([4, 4], F32); mf = K.sb([4, 1], F32)
            dma("sq", m0T.t[:], m0_in, writes=[m0T.r])

            def scan_seg(ig_src, fg_src, n, c_dst, b_dst, M_dst):
                dma("sq", igt.t[:, 0:n], ig_src, writes=[igt.r])
                dma("sq", fgt.t[:, 0:n], fg_src, writes=[fgt.r])
                f_ = fgt.t[:, 0:n]
                op("act", lambda: nc.scalar.activation(out=f_, in_=f_, func=AF.Exp, scale=-1.0), [fgt.r], [fgt.r])
                op("act", lambda: nc.scalar.activation(out=f_, in_=f_, func=AF.Ln, bias=one1.t[:, 0:1], scale=1.0),
                   [fgt.r, one1.r], [fgt.r])
                op("dve", lambda: nc.vector.tensor_tensor_scan(out=bt.t[:, 0:n], data0=ones4.t[:, 0:n], data1=f_,
                                                               initial=bprev.t[:, 0:1], op0=ALU.mult, op1=ALU.subtract),
                   [ones4.r, fgt.r, bprev.r], [bt.r])
                op("dve", lambda: nc.vector.tensor_tensor(out=ct.t[:, 0:n], in0=igt.t[:, 0:n], in1=bt.t[:, 0:n], op=ALU.subtract),
                   [igt.r, bt.r], [ct.r])
                op("dve", lambda: nc.vector.tensor_tensor_scan(out=Mt.t[:, 0:n], data0=ones4.t[:, 0:n], data1=ct.t[:, 0:n],
                                                               initial=Mprev.t[:, 0:1], op0=ALU.mult, op1=ALU.max),
                   [ones4.r, ct.r, Mprev.r], [Mt.r])
                op("dve", lambda: nc.vector.tensor_copy(out=bprev.t[:], in_=bt.t[:, n - 1:n]), [bt.r], [bprev.r])
                op("dve", lambda: nc.vector.tensor_copy(out=Mprev.t[:], in_=Mt.t[:, n - 1:n]), [Mt.r], [Mprev.r])
                dma("sq", c_dst, ct.t[:, 0:n], reads=[ct.r])
                dma("sq", b_dst, bt.t[:, 0:n], reads=[bt.r])
                dma("sq", M_dst, Mt.t[:, 0:n], reads=[Mt.r])

            vmemset(bprev, bprev.t[:], 0.0); vmemset(Mprev, Mprev.t[:], 0.0)
            for g0 in range(0, SEQ, SG):
                scan_seg(GATES[0:4, g0:g0 + SG], GATES[4:8, g0:g0 + SG], SG,
                         ROWS[0, :, g0:g0 + SG], ROWS[1, :, g0:g0 + SG], ROWS[2, :, g0:g0 + SG])
            op("dve", lambda: nc.vector.tensor_tensor(out=mf.t[:], in0=bprev.t[:], in1=Mprev.t[:], op=ALU.add),
               [bprev.r, Mprev.r], [mf.r])
            dma("sq", m_o, mf.t[:], reads=[mf.r])
            for s_ in range(4):
                vmemset(bprev, bprev.t[:], 0.0)
                op("dve", lambda: nc.vector.tensor_copy(out=Mprev.t[:], in_=m0T.t[:, s_:s_ + 1]), [m0T.r], [Mprev.r])
                scan_seg(GATES_s[0:4, 16 * s_:16 * s_ + 16], GATES_s[4:8, 16 * s_:16 * s_ + 16], 16,
                         ROWS_s[0, :, 16 * s_:16 * s_ + 16], ROWS_s[1, :, 16 * s_:16 * s_ + 16], ROWS_s[2, :, 16 * s_:16 * s_ + 16])
                op("dve", lambda: nc.vector.tensor_tensor(out=smt.t[:, s_:s_ + 1], in0=bprev.t[:], in1=Mprev.t[:], op=ALU.add),
                   [bprev.r, Mprev.r], [smt.r])
            dma("sq", s_m, smt.t[:], reads=[smt.r])
        if stop_after <= 2:
            K.finish()
            return nc

        with K.scope():
            selsb = K.sb([NBLK, NOB], F32)
            dma("sq", selsb.t[:], sel_d, writes=[selsb.r])
            rowr = Ring([K.sb([NBLK, 128], F32) for _ in range(2)])
            for h in range(4):
                for ridx, dst in ((2, Mown), (1, bown)):
                    rb = rowr.next()
                    dma("sq", rb.t[:], ROWS[ridx, h, :].rearrange("(b t) -> b t", t=128), writes=[rb.r])
                    mm(psX.t[:, 0:NOB], rb.t[:, :], selsb.t[:, :], True, True, [rb.r, selsb.r], [psX.r])
                    copy("dve", dst.t[:, h, :], psX.t[:, 0:NOB], [psX.r], [dst.r])
            op("dve", lambda: nc.vector.tensor_scalar(out=negMown.t[:], in0=Mown.t[:], scalar1=-1.0, scalar2=None, op0=ALU.mult),
               [Mown.r], [negMown.r])
            op("dve", lambda: nc.vector.tensor_tensor(out=ebown.t[:], in0=bown.t[:], in1=Mown.t[:], op=ALU.add),
               [bown.r, Mown.r], [ebown.r])
            op("act", lambda: nc.scalar.activation(out=ebown.t[:], in_=ebown.t[:], func=AF.Exp, scale=-1.0), [ebown.r], [ebown.r])
            wcol = K.sb([128, 4, NBLK], F32); wcolb = K.sb([128, 4, NBLK], BF16)
            mlb = K.sb([128, 4], F32)
            for h in range(4):
                rb = rowr.next()
                dma("sq", rb.t[:], ROWS[0, h, :].rearrange("(b t) -> b t", t=128), writes=[rb.r])
                dma("sq", mlb.t[:, h:h + 1], ROWS[2, h, SEQ - 1:SEQ].partition_broadcast(128), writes=[mlb.r])
                pt = psT.next()
                tr(pt.t[:, 0:NBLK], rb.t[:, :], ident.t[:NBLK, :NBLK], [rb.r, ident.r], [pt.r])
                op("dve", lambda: nc.vector.tensor_scalar(out=wcol.t[:, h, :], in0=pt.t[:, 0:NBLK], scalar1=mlb.t[:, h:h + 1],
                                                          scalar2=None, op0=ALU.subtract), [pt.r, mlb.r], [wcol.r])
            op("act", lambda: nc.scalar.activation(out=wcol.t[:], in_=wcol.t[:], func=AF.Exp), [wcol.r], [wcol.r])
            copy("dve", wcolb.t[:], wcol.t[:], [wcol.r], [wcolb.r])
            kbr = Ring([K.sb([128, 256], BF16) for _ in range(3)])
            vbr = Ring([K.sb([128, 512], BF16) for _ in range(3)])
            kgr = Ring([K.sb([128, 256], BF16) for _ in range(3)])
            cstr = Ring([K.sb([128, 512], F32) for _ in range(2)])
            nst = K.sb([1, 256], F32)
            for h in range(4):
                po = [psO.next(), psO.next()]
                for blk in range(NBLK):
                    kb, vb, kg = kbr.next(), vbr.next(), kgr.next()
                    dma("sq", kb.t[:], K_tm[blk * 128:(blk + 1) * 128, h * 256:(h + 1) * 256], writes=[kb.r])
                    dma("sq", vb.t[:], V_ml[blk * 128:(blk + 1) * 128, h * 512:(h + 1) * 512], writes=[vb.r])
                    op("dve", lambda: nc.vector.tensor_scalar(out=kg.t[:], in0=kb.t[:], scalar1=wcol.t[:, h, blk:blk + 1],
                                                              scalar2=None, op0=ALU.mult), [kb.r, wcol.r], [kg.r])
                    for dc in range(2):
                        op("pe", lambda: nc.tensor.matmul(po[dc].t[:, 0:512], lhsT=kg.t[:, dc * 128:(dc + 1) * 128], rhs=vb.t[:],
                                                          start=(blk == 0), stop=(blk == NBLK - 1)),
                           [kg.r, vb.r], [po[dc].r], inc=True)
                    op("pe", lambda: nc.tensor.matmul(psX.t[0:1, 0:256], lhsT=wcolb.t[:, h, blk:blk + 1], rhs=kb.t[:],
                                                      start=(blk == 0), stop=(blk == NBLK - 1)),
                       [wcolb.r, kb.r], [psX.r], inc=True)
                for dc in range(2):
                    cs_ = cstr.next()
                    copy(K.alt(), cs_.t[:], po[dc].t[:, 0:512], [po[dc].r], [cs_.r])
                    dma("sq", C_o[h, dc * 128:(dc + 1) * 128, :], cs_.t[:], reads=[cs_.r])
                copy("dve", nst.t[:], psX.t[0:1, 0:256], [psX.r], [nst.r])
                dma("sq", n_o[h:h + 1, :], nst.t[:], reads=[nst.r])
        if stop_after <= 3:
            K.finish()
            return nc

        def q_side(uT, T, c0, WL, L):
            nsub = (T + 127) // 128
            rws = [min(128, T - s * 128) for s in range(nsub)]
            for cb in range(4):
                wb, wv = WL.load(f"q{cb}")
                for cc in range(2):
                    ch = cb * 2 + cc
                    ps = psA.next()
                    for k in range(32):
                        mm(ps.t[:, 0:T], wv[:, k, cc * 128:(cc + 1) * 128], uT.t[:, k, 0:T], k == 0, k == 31, [wb.r, uT.r], [ps.r])
                    st = L["st512"].next()
                    copy(K.alt(), st.t[:, 0:T], ps.t[:, 0:T], [ps.r], [st.r])
                    dma("sq", QT_ml[ch * 128:(ch + 1) * 128, c0:c0 + T], st.t[:, 0:T], reads=[st.r])
            og = L["og"]
            for cb in range(8):
                wb, wv = WL.load(f"o{cb}")
                for s in range(nsub):
                    ps = psA.next()
                    for k in range(32):
                        mm(ps.t[:rws[s], 0:256], uT.t[:, k, s * 128:s * 128 + rws[s]], wv[:, k, :], k == 0, k == 31, [wb.r, uT.r], [ps.r])
                    op("act", lambda: nc.scalar.activation(out=og.t[:rws[s], s, cb * 256:(cb + 1) * 256], in_=ps.t[:rws[s], 0:256],
                                                           func=AF.Sigmoid), [ps.r], [og.r])
            tm_out(OG, c0, T, og, 2048)
            cq32, cqnT, rbc = L["cq32"], L["cqnT"], L["rbc"]
            for cb in range(4):
                wb, wv = WL.load(f"cq{cb}")
                for cc in range(2):
                    ch = cb * 2 + cc
                    ps = psA.next()
                    for k in range(32):
                        mm(ps.t[:, 0:T], wv[:, k, cc * 128:(cc + 1) * 128], uT.t[:, k, 0:T], k == 0, k == 31, [wb.r, uT.r], [ps.r])
                    copy("dve", cq32.t[:, ch, 0:T], ps.t[:, 0:T], [ps.r], [cq32.r])
                    sqb = L["sqb"].next()
                    op("act", lambda: nc.scalar.activation(out=sqb.t[:, 0:T], in_=cq32.t[:, ch, 0:T], func=AF.Square), [cq32.r], [sqb.r])
                    mm(psX.t[:, 0:T], ones_bf.t[:], sqb.t[:, 0:T], ch == 0, ch == 7, [ones_bf.r, sqb.r], [psX.r], inc=True)
            rstd_from_ss(rbc.t[:, 0:T], psX.t[:, 0:T], 1.0 / 1024, [psX.r], [rbc.r])
            for ch in range(8):
                op("dve", lambda: nc.vector.scalar_tensor_tensor(out=cqnT.t[:, ch, 0:T], in0=cq32.t[:, ch, 0:T], scalar=gcq.t[:, ch:ch + 1],
                                                                 in1=rbc.t[:, 0:T], op0=ALU.mult, op1=ALU.mult),
                   [cq32.r, gcq.r, rbc.r], [cqnT.r])
            qm32, qmb, rt2, cso, qnst, qrst = L["qm32"], L["qmb"], L["rt2"], L["cso"], L["qnst"], L["qrst"]
            for s in range(nsub):
                rows = rws[s]
                for cb in range(3):
                    wb, wv = WL.load(f"uq{cb}")
                    for half in range(2):
                        ps = psA.next()
                        for k in range(8):
                            mm(ps.t[:rows, 0:512], cqnT.t[:, k, s * 128:s * 128 + rows], wv[:, k, half * 512:(half + 1) * 512],
                               k == 0, k == 7, [wb.r, cqnT.r], [ps.r])
                        copy(K.alt(), qm32.t[:rows, cb * 1024 + half * 512:cb * 1024 + (half + 1) * 512], ps.t[:rows, 0:512], [ps.r], [qm32.r])
                dma("sq", cso.t[:rows, :], cs_own[c0 + s * 128:c0 + s * 128 + rows, :], writes=[cso.r])
                qv = qm32.t[:rows, :].rearrange("p (h e) -> p h e", e=192)
                qb = qmb.t[:rows, :].rearrange("p (h e) -> p h e", e=192)
                x1, x2 = qv[:, :, 128:160], qv[:, :, 160:192]
                co = cso.t[:rows, None, 0:32].to_broadcast([rows, 16, 32])
                si = cso.t[:rows, None, 32:64].to_broadcast([rows, 16, 32])
                t = [rt2.t[:rows, i, :, :] for i in range(4)]
                rr = [qm32.r, cso.r]
                op("dve", lambda: nc.vector.tensor_tensor(out=t[0], in0=x1, in1=co, op=ALU.mult), rr, [rt2.r])
                op("dve", lambda: nc.vector.tensor_tensor(out=t[1], in0=x2, in1=si, op=ALU.mult), rr, [rt2.r])
                op("dve", lambda: nc.vector.tensor_tensor(out=t[2], in0=x2, in1=co, op=ALU.mult), rr, [rt2.r])
                op("dve", lambda: nc.vector.tensor_tensor(out=t[3], in0=x1, in1=si, op=ALU.mult), rr, [rt2.r])
                op("act", lambda: nc.scalar.copy(out=qb[:, :, 0:128], in_=qv[:, :, 0:128]), [qm32.r], [qmb.r])
                op("dve", lambda: nc.vector.tensor_tensor(out=qb[:, :, 128:160], in0=t[0], in1=t[1], op=ALU.subtract), [rt2.r], [qmb.r])
                op("dve", lambda: nc.vector.tensor_tensor(out=qb[:, :, 160:192], in0=t[2], in1=t[3], op=ALU.add), [rt2.r], [qmb.r])
                for g in range(2):
                    pt = psT.next()
                    pv = bfv(pt)
                    for i in range(8):
                        tr(pv[:, i * 128:i * 128 + rows], qb[:, g * 8 + i, 0:128], identb.t[:rows, :rows], [qmb.r, identb.r], [pt.r], inc=(i == 7))
                    copy(K.alt(), qnst.t[:, g * 8:(g + 1) * 8, 0:rows], pv.rearrange("p (a b) -> p a b", b=128)[:, :, 0:rows], [pt.r], [qnst.r])
                for g in range(2):
                    pt = psT.next()
                    pv = bfv(pt)
                    for i in range(8):
                        tr(pv[0:64, i * 128:i * 128 + rows], qb[:, g * 8 + i, 128:192], identb.t[:rows, :rows], [qmb.r, identb.r], [pt.r], inc=(i == 7))
                    copy(K.alt(), qrst.t[:, g * 8:(g + 1) * 8, 0:rows], pv.rearrange("p (a b) -> p a b", b=128)[0:64, :, 0:rows], [pt.r], [qrst.r])
                n0_ = c0 + s * 128
                dma("sq", QN_T.rearrange("(h d) n -> d h n", d=128)[:, :, n0_:n0_ + rows], qnst.t[:, :, 0:rows], reads=[qnst.r])
                dma("sq", QR_T.rearrange("(h d) n -> d h n", d=64)[:, :, n0_:n0_ + rows], qrst.t[:, :, 0:rows], reads=[qrst.r])

        with K.scope():
            uT = K.sb([128, 32, 512], BF16)
            rows_ring = Ring([K.sb([128, 4096], F32) for _ in range(2)])
            ss = K.sb([128, 1], F32); rstd = K.sb([128, 1], F32)
            WL = WLoader(2)
            L = dict(st512=Ring([K.sb([128, 512], BF16) for _ in range(3)]), og=K.sb([128, 4, 2048], BF16),
                     cq32=K.sb([128, 8, 512], F32), cqnT=K.sb([128, 8, 512], BF16), rbc=K.sb([128, 512], F32),
                     sqb=Ring([K.sb([128, 512], BF16) for _ in range(3)]), qm32=K.sb([128, 3072], F32), qmb=K.sb([128, 3072], BF16),
                     rt2=K.sb([128, 4, 16, 32], F32), cso=K.sb([128, 64], F32), qnst=K.sb([128, 16, 128], BF16),
                     qrst=K.sb([64, 16, 128], BF16))
            junk = View(L["og"].t[:, 0:2, :].rearrange("p a b -> p (a b)"), L["og"].r)
            for ti in range(NTO):
                prep(x_own[ti * 512:(ti + 1) * 512, :], 512, A1, B1, [(0, 512, 0)], uT, rows_ring, junk, ss, rstd)
                q_side(uT, 512, ti * 512, WL, L)
            prep(x_smp, 64, A1, B1, sample_groups, uT, rows_ring, junk, ss, rstd)
            q_side(uT, 64, OWN, WL, L)
        if stop_after <= 4:
            K.finish()
            return nc

        NPG = (max(NBLK, (NKS + 127) // 128) + 7) // 8

        def pv_blocks(nk):
            return [(k0, min(128, nk - k0)) for k0 in range(0, nk, 128)]

        def transposes_and_pv(nq, Pb, nk, PT, PTres, po, ncols, vget, ceng=None):
            blks = pv_blocks(nk)
            nb = len(blks)
            for g0 in range(0, nb, 8):
                grp = blks[g0:g0 + 8]
                pt = psT.next()
                pv = bfv(pt)
                pr = PTres[g0 // 8]
                for i, (k0, w) in enumerate(grp):
                    tr(pv[:w, i * 128:i * 128 + nq], Pb.t[:nq, k0:k0 + w], identb.t[:nq, :nq], [Pb.r, identb.r], [pt.r],
                       inc=(i == len(grp) - 1))
                full = [i for i, (k0, w) in enumerate(grp) if w == 128]
                if full:
                    nf = len(full)
                    copy(ceng or K.alt(), PT.t[:, g0:g0 + nf, 0:nq], pv.rearrange("p (a b) -> p a b", b=128)[:, 0:nf, 0:nq], [pt.r], [pr])
                for i, (k0, w) in enumerate(grp):
                    if w < 128:
                        copy(ceng or K.alt(), PT.t[:w, g0 + i, 0:nq], pv[:w, i * 128:i * 128 + nq], [pt.r], [pr])
                for i, (k0, w) in enumerate(grp):
                    bi = g0 + i
                    v_ap, v_res = vget(bi, w)
                    op("pe", lambda: nc.tensor.matmul(po.t[:nq, 0:ncols], lhsT=PT.t[:w, bi, 0:nq], rhs=v_ap,
                                                      start=(bi == 0), stop=(bi == nb - 1)),
                       [pr] + v_res, [po.r], inc=(bi == nb - 1))

        def attn_A1(nq, qn, qr, qres, KTs, KRTs, tiles, Sb):
            for (k0, w, bias) in tiles:
                ps = psA.next()
                mm(ps.t[:nq, 0:w], qn, KTs.t[:, k0:k0 + w], True, False, qres + [KTs.r], [ps.r])
                mm(ps.t[:nq, 0:w], qr, KRTs.t[0:64, k0:k0 + w], False, True, qres + [KRTs.r], [ps.r])
                if bias is None:
                    copy("act", Sb.t[:nq, k0:k0 + w], ps.t[:nq, 0:w], [ps.r], [Sb.r])
                else:
                    op("dve", lambda: nc.vector.tensor_tensor(out=Sb.t[:nq, k0:k0 + w], in0=ps.t[:nq, 0:w], in1=bias.t[:nq, 0:w],
                                                              op=ALU.add), [ps.r, bias.r], [Sb.r])

        def attn_A2(nq, tiles, Sb, Pb, sm):
            nk = sum(w for (_, w, _) in tiles)
            mx, negm, rsum, rinv = sm
            op("dve", lambda: nc.vector.reduce_max(out=mx.t[:nq, :], in_=Sb.t[:nq, 0:nk], axis=AX.X), [Sb.r], [mx.r])
            op("dve", lambda: nc.vector.tensor_scalar(out=negm.t[:nq, :], in0=mx.t[:nq, :], scalar1=-MLA_SCALE, scalar2=None,
                                                      op0=ALU.mult), [mx.r], [negm.r])
            op("act", lambda: nc.scalar.activation(out=Pb.t[:nq, 0:nk], in_=Sb.t[:nq, 0:nk], func=AF.Exp, bias=negm.t[:nq, :],
                                                   scale=MLA_SCALE, accum_out=rsum.t[:nq, :]), [Sb.r, negm.r], [Pb.r, rsum.r])
            op("dve", lambda: nc.vector.reciprocal(out=rinv.t[:nq, :], in_=rsum.t[:nq, :]), [rsum.r], [rinv.r])
            return (nq, nk, Pb, rinv)

        def attn_A(nq, qn, qr, qres, KTs, KRTs, tiles, Sb, Pb, sm):
            attn_A1(nq, qn, qr, qres, KTs, KRTs, tiles, Sb)
            return attn_A2(nq, tiles, Sb, Pb, sm)

        def attn_B(st, Vs, PT, PTres, out_ap, out_res):
            nq, nk, Pb, rinv = st
            po = psO.next()
            transposes_and_pv(nq, Pb, nk, PT, PTres, po, 128, lambda bi, w: (Vs.t[:w, bi, :], [Vs.r]), ceng="dve")
            op("dve", lambda: nc.vector.tensor_scalar(out=out_ap, in0=po.t[:nq, 0:128], scalar1=rinv.t[:nq, :], scalar2=None, op0=ALU.mult),
               [po.r, rinv.r], [out_res])

        def attn_tile(nq, qn, qr, qres, KTs, KRTs, Vs, tiles, Sb, Pb, PT, PTres, sm, out_ap, out_res):
            st = attn_A(nq, qn, qr, qres, KTs, KRTs, tiles, Sb, Pb, sm)
            attn_B(st, Vs, PT, PTres, out_ap, out_res)

        KMAX = max(SEQ, NKS)
        with K.scope():
            KRTs = K.sb([64, KMAX], BF16); KTs = K.sb([128, KMAX], BF16)
            Vs = K.sb([128, (KMAX + 127) // 128, 128], BF16)
            QN = K.sb([128, OWN], BF16); QR = K.sb([64, OWN], BF16)
            biasm = K.sb([128, 512], F32)
            Sbr = Ring([K.sb([128, KMAX], F32) for _ in range(2)])
            Pbr = Ring([K.sb([128, KMAX], BF16) for _ in range(2)])
            PT = K.sb([128, NPG * 8, 128], BF16)
            PTres = [Res() for _ in range(NPG)]
            Aacc = K.sb([128, NOB, 128], BF16)
            Asmp = K.sb([16, 16, 128], BF16)
            smr = Ring([tuple(K.sb([128, 1], F32) for _ in range(4)) for _ in range(2)])
            dma("sq", biasm.t[:], bias_mla_d, writes=[biasm.r])
            dma("sq", KRTs.t[:, 0:SEQ], KRT, writes=[KRTs.r])
            for h in range(16):
                dma("sq", KTs.t[:, 0:SEQ], KT_mla[h * 128:(h + 1) * 128, :], writes=[KTs.r])
                dma("sq", Vs.t[:, 0:NBLK, :], V_mla.rearrange("(b p) f -> p b f", p=128)[:, :, h * 128:(h + 1) * 128], writes=[Vs.r])
                dma("sq", QN.t[:], QN_T[h * 128:(h + 1) * 128, 0:OWN], writes=[QN.r])
                dma("sq", QR.t[:], QR_T[h * 64:(h + 1) * 64, 0:OWN], writes=[QR.r])
                prev = None
                for m in range(NOB):
                    tiles = [(kt * 512, 512, None) for kt in range(m)] + [(m * 512, 512, biasm)]
                    Sb_, Pb_, sm_ = Sbr.next(), Pbr.next(), smr.next()
                    attn_A1(128, QN.t[:, m * 128:(m + 1) * 128], QR.t[:, m * 128:(m + 1) * 128], [QN.r, QR.r], KTs, KRTs, tiles, Sb_)
                    if prev is not None:
                        attn_B(prev[0], Vs, PT, PTres, Aacc.t[:, prev[1], :], Aacc.r)
                    cur = attn_A2(128, tiles, Sb_, Pb_, sm_)
                    prev = (cur, m)
                attn_B(prev[0], Vs, PT, PTres, Aacc.t[:, prev[1], :], Aacc.r)
                dma("sq", A_s[0:OWN, :].rearrange("(m p) f -> p m f", p=128)[:, :, 2048 + h * 128:2048 + (h + 1) * 128], Aacc.t[:],
                    reads=[Aacc.r])
            for sq_ in range(4):
                dma("sq", KRTs.t[:, 0:PAST], KRT_p[sq_], writes=[KRTs.r])
                dma("sq", KRTs.t[:, PAST:NKS], KRT_s[:, 16 * sq_:16 * sq_ + 16], writes=[KRTs.r])
                for h in range(16):
                    dma("sq", KTs.t[:, 0:PAST], KT_mla_p[sq_][h * 128:(h + 1) * 128, :], writes=[KTs.r])
                    dma("sq", KTs.t[:, PAST:NKS], KT_mla_s[h * 128:(h + 1) * 128, 16 * sq_:16 * sq_ + 16], writes=[KTs.r])
                    dma("sq", Vs.t[:, 0:PAST // 128, :], V_mla_p[sq_].rearrange("(b p) f -> p b f", p=128)[:, :, h * 128:(h + 1) * 128],
                        writes=[Vs.r])
                    dma("sq", Vs.t[0:16, PAST // 128, :], V_mla_s[16 * sq_:16 * sq_ + 16, h * 128:(h + 1) * 128], writes=[Vs.r])
                    dma("sq", QN.t[:, 0:16], QN_T[h * 128:(h + 1) * 128, OWN + 16 * sq_:OWN + 16 * sq_ + 16], writes=[QN.r])
                    dma("sq", QR.t[:, 0:16], QR_T[h * 64:(h + 1) * 64, OWN + 16 * sq_:OWN + 16 * sq_ + 16], writes=[QR.r])
                    tiles = [(i * 512, 512, None) for i in range(NPT)] + [(PAST, 16, None)]
                    attn_tile(16, QN.t[:, 0:16], QR.t[:, 0:16], [QN.r, QR.r], KTs, KRTs, Vs, tiles,
                              Sbr.next(), Pbr.next(), PT, PTres, smr.next(), Asmp.t[:, h, :], Asmp.r)
                dma("sq", A_s[OWN + 16 * sq_:OWN + 16 * sq_ + 16, 2048:4096], Asmp.t[:].rearrange("p h d -> p (h d)"), reads=[Asmp.r])
        if stop_after <= 5:
            K.finish()
            return nc

        with K.scope():
            maskm = K.sb([128, 512], F32); tril16 = K.sb([16, 16], F32)
            gml = K.sb([128, 2048], F32)
            KT2 = K.sb([128, 2, SEQ], BF16); QT2 = K.sb([128, 2, OWN], BF16)
            cbc = K.sb([128, SEQ], F32); Eb = K.sb([128, SEQ], BF16); tmpE = K.sb([128, 512], F32)
            Pbr = Ring([K.sb([128, SEQ], BF16) for _ in range(2)])
            PT = K.sb([128, NPG * 8, 128], BF16)
            PTres = [Res() for _ in range(NPG)]
            vring = Ring([K.sb([128, 4, 512], BF16) for _ in range(2)])
            hbr = Ring([K.sb([128, 512], F32) for _ in range(2)])
            ogr = Ring([K.sb([128, 512], BF16) for _ in range(2)])
            astr = Ring([K.sb([128, 512], BF16) for _ in range(2)])
            junk5 = K.sb([128, 512], BF16)
            smr = Ring([tuple(K.sb([128, 1], F32) for _ in range(6)) for _ in range(2)])
            dma("sq", maskm.t[:], mask_ml_d, writes=[maskm.r])
            dma("sq", tril16.t[:], tril16_d, writes=[tril16.r])
            dma("sq", gml.t[:], g_ml_d.partition_broadcast(128), writes=[gml.r])

            def ml_A(nq, qT, qres, tiles, negM, colres):
                nk = sum(w for (_, w, _) in tiles)
                Pb = Pbr.next()
                sm = smr.next()
                nqv = sm[0]
                lastk0 = tiles[-1][0]
                lw = nk - lastk0
                if lastk0 > 0:
                    op("act", lambda: nc.scalar.activation(out=Eb.t[:nq, 0:lastk0], in_=cbc.t[:nq, 0:lastk0], func=AF.Exp, bias=negM,
                                                           scale=1.0), [cbc.r] + colres, [Eb.r])
                op("dve", lambda: nc.vector.tensor_scalar(out=tmpE.t[:nq, 0:lw], in0=cbc.t[:nq, lastk0:nk], scalar1=negM, scalar2=0.0,
                                                          op0=ALU.add, op1=ALU.min), [cbc.r] + colres, [tmpE.r])
                op("act", lambda: nc.scalar.activation(out=Eb.t[:nq, lastk0:nk], in_=tmpE.t[:nq, 0:lw], func=AF.Exp), [tmpE.r], [Eb.r])
                for (k0, w, mask) in tiles:
                    ps = psA.next()
                    for c in range(2):
                        mm(ps.t[:nq, 0:w], qT[:, c, :], KT2.t[:, c, k0:k0 + w], c == 0, c == 1, qres + [KT2.r], [ps.r])
                    op("dve", lambda: nc.vector.tensor_tensor(out=Pb.t[:nq, k0:k0 + w], in0=ps.t[:nq, 0:w], in1=Eb.t[:nq, k0:k0 + w],
                                                              op=ALU.mult), [ps.r, Eb.r], [Pb.r])
                    if mask is not None:
                        op("dve", lambda: nc.vector.tensor_tensor(out=Pb.t[:nq, k0:k0 + w], in0=Pb.t[:nq, k0:k0 + w],
                                                                  in1=mask.t[:nq, 0:w], op=ALU.mult), [Pb.r, mask.r], [Pb.r])
                op("dve", lambda: nc.vector.reduce_sum(out=nqv.t[:nq, :], in_=Pb.t[:nq, 0:nk], axis=AX.X), [Pb.r], [nqv.r])
                return (nq, nk, Pb, sm, tiles)

            def ml_B(st, eb, colres, vsrc, h, init, og_src, a_dst):
                nq, nk, Pb, sm, tiles = st
                nqv, an, den, rden, ssq, rs = sm
                po = psO.next()
                nb = (nk + 127) // 128
                for ti_, (k0, w, _) in enumerate(tiles):
                    vb = vring.next()
                    if w % 128 == 0:
                        dma("sq", vb.t[:, 0:w // 128, :], vsrc(k0, w).rearrange("(b p) f -> p b f", p=128), writes=[vb.r])
                    else:
                        dma("sq", vb.t[:w, 0, :], vsrc(k0, w), writes=[vb.r])
                    tb = pv_blocks(w)
                    pt = psT.next()
                    pv = bfv(pt)
                    pr = PTres[(k0 // 512) % NPG]
                    for i, (b0, bw) in enumerate(tb):
                        tr(pv[:bw, i * 128:i * 128 + nq], Pb.t[:nq, k0 + b0:k0 + b0 + bw], identb.t[:nq, :nq], [Pb.r, identb.r], [pt.r],
                           inc=(i == len(tb) - 1))
                    g0 = (k0 // 128)
                    if all(bw == 128 for (_, bw) in tb):
                        copy("act", PT.t[:, g0:g0 + len(tb), 0:nq], pv.rearrange("p (a b) -> p a b", b=128)[:, 0:len(tb), 0:nq], [pt.r], [pr])
                    else:
                        for i, (b0, bw) in enumerate(tb):
                            copy("act", PT.t[:bw, g0 + i, 0:nq], pv[:bw, i * 128:i * 128 + nq], [pt.r], [pr])
                    for i, (b0, bw) in enumerate(tb):
                        bi = g0 + i
                        op("pe", lambda: nc.tensor.matmul(po.t[:nq, 0:512], lhsT=PT.t[:bw, bi, 0:nq], rhs=vb.t[:bw, i, :],
                                                          start=(bi == 0), stop=(bi == nb - 1)),
                           [pr, vb.r], [po.r], inc=(i == len(tb) - 1))
                if init is not None:
                    op("dve", lambda: nc.vector.tensor_tensor(out=nqv.t[:nq, :], in0=nqv.t[:nq, :], in1=init[1].t[:nq, :], op=ALU.add),
                       [nqv.r, init[1].r], [nqv.r])
                op("dve", lambda: nc.vector.tensor_scalar(out=an.t[:nq, :], in0=nqv.t[:nq, :], scalar1=-1.0, scalar2=None, op0=ALU.mult),
                   [nqv.r], [an.r])
                op("dve", lambda: nc.vector.tensor_tensor(out=an.t[:nq, :], in0=an.t[:nq, :], in1=nqv.t[:nq, :], op=ALU.max),
                   [nqv.r, an.r], [an.r])
                op("dve", lambda: nc.vector.tensor_tensor(out=den.t[:nq, :], in0=an.t[:nq, :], in1=eb, op=ALU.max), [an.r] + colres, [den.r])
                op("dve", lambda: nc.vector.reciprocal(out=rden.t[:nq, :], in_=den.t[:nq, :]), [den.r], [rden.r])
                hb = hbr.next()
                if init is None:
                    op("act", lambda: nc.scalar.activation(out=hb.t[:nq, :], in_=po.t[:nq, 0:512], func=AF.Copy, scale=rden.t[:nq, :]),
                       [po.r, rden.r], [hb.r])
                else:
                    op("dve", lambda: nc.vector.tensor_tensor(out=hb.t[:nq, :], in0=po.t[:nq, 0:512], in1=init[0].t[:nq, :], op=ALU.add),
                       [po.r, init[0].r], [hb.r])
                    op("act", lambda: nc.scalar.activation(out=hb.t[:nq, :], in_=hb.t[:nq, :], func=AF.Copy, scale=rden.t[:nq, :]),
                       [hb.r, rden.r], [hb.r])
                op("act", lambda: nc.scalar.activation(out=junk5.t[:nq, :], in_=hb.t[:nq, :], func=AF.Square, accum_out=ssq.t[:nq, :]),
                   [hb.r], [junk5.r, ssq.r])
                rstd_from_ss(rs.t[:nq, :], ssq.t[:nq, :], 1.0 / 512, [ssq.r], [rs.r])
                og = ogr.next()
                dma("sq", og.t[:nq, :], og_src, writes=[og.r])
                op("dve", lambda: nc.vector.scalar_tensor_tensor(out=hb.t[:nq, :], in0=hb.t[:nq, :], scalar=rs.t[:nq, :],
                                                                 in1=gml.t[:nq, h * 512:(h + 1) * 512], op0=ALU.mult, op1=ALU.mult),
                   [hb.r, rs.r, gml.r], [hb.r])
                ast = astr.next()
                op("dve", lambda: nc.vector.tensor_tensor(out=ast.t[:nq, :], in0=hb.t[:nq, :], in1=og.t[:nq, :], op=ALU.mult),
                   [hb.r, og.r], [ast.r])
                dma("sq", a_dst, ast.t[:nq, :], reads=[ast.r])

            for h in range(4):
                dma("sq", KT2.t[:], KT_ml[h * 256:(h + 1) * 256, :].rearrange("(c p) n -> p c n", p=128), writes=[KT2.r])
                dma("sq", QT2.t[:], QT_ml[h * 256:(h + 1) * 256, 0:OWN].rearrange("(c p) n -> p c n", p=128), writes=[QT2.r])
                dma("sq", cbc.t[:], ROWS[0, h, :].partition_broadcast(128), writes=[cbc.r])
                prev = None
                cres_ = [negMown.r, ebown.r]
                for m in range(NOB):
                    tiles = [(kt * 512, 512, None) for kt in range(m)] + [(m * 512, 512, maskm)]
                    cur = ml_A(128, QT2.t[:, :, m * 128:(m + 1) * 128], [QT2.r], tiles, negMown.t[:, h, m:m + 1], cres_)
                    if prev is not None:
                        pm = prev[1]
                        ml_B(prev[0], ebown.t[:, h, pm:pm + 1], cres_, lambda k0, w: V_ml[k0:k0 + w, h * 512:(h + 1) * 512], h, None,
                             OG[pm * 128:(pm + 1) * 128, h * 512:(h + 1) * 512], A_s[pm * 128:(pm + 1) * 128, h * 512:(h + 1) * 512])
                    prev = (cur, m)
                pm = prev[1]
                ml_B(prev[0], ebown.t[:, h, pm:pm + 1], cres_, lambda k0, w: V_ml[k0:k0 + w, h * 512:(h + 1) * 512], h, None,
                     OG[pm * 128:(pm + 1) * 128, h * 512:(h + 1) * 512], A_s[pm * 128:(pm + 1) * 128, h * 512:(h + 1) * 512])
            colr = Ring([tuple(K.sb([128, 1], F32) for _ in range(8)) for _ in range(2)])
            C0f = Ring([K.sb([128, 2, 512], F32) for _ in range(1)]); C0b = K.sb([128, 2, 512], BF16)
            n0f = K.sb([128, 2], F32); n0b = K.sb([128, 2], BF16); n0row = K.sb([1, 256], F32)
            inter = Ring([K.sb([16, 512], F32) for _ in range(2)])
            kb16 = Ring([K.sb([16, 256], BF16) for _ in range(2)]); vb16 = Ring([K.sb([16, 512], BF16) for _ in range(2)])
            kg16 = Ring([K.sb([16, 256], BF16) for _ in range(2)]); wtb = K.sb([16, 1], BF16)
            cstr = Ring([K.sb([128, 512], F32) for _ in range(2)]); nst = K.sb([1, 256], F32)
            for sq_ in range(4):
                t0_ = 16 * sq_
                for h in range(4):
                    sh = sq_ * 4 + h
                    Mc, bc_, nM, ebc, m0b, wint, cc_, mlb16 = colr.next()
                    colres = [Mc.r]
                    dma("sq", KT2.t[:, :, 0:16], KT_ml_s[h * 256:(h + 1) * 256, t0_:t0_ + 16].rearrange("(c p) n -> p c n", p=128), writes=[KT2.r])
                    dma("sq", QT2.t[:, :, 0:16], QT_ml[h * 256:(h + 1) * 256, OWN + t0_:OWN + t0_ + 16].rearrange("(c p) n -> p c n", p=128),
                        writes=[QT2.r])
                    dma("sq", cbc.t[0:16, 0:16], ROWS_s[0, h, t0_:t0_ + 16].partition_broadcast(16), writes=[cbc.r])
                    dma("sq", Mc.t[0:16, :], ROWS_s[2, h, t0_:t0_ + 16].rearrange("(t o) -> t o", o=1), writes=[Mc.r])
                    dma("sq", bc_.t[0:16, :], ROWS_s[1, h, t0_:t0_ + 16].rearrange("(t o) -> t o", o=1), writes=[Mc.r])
                    dma("sq", cc_.t[0:16, :], ROWS_s[0, h, t0_:t0_ + 16].rearrange("(t o) -> t o", o=1), writes=[Mc.r])
                    dma("sq", m0b.t[:, :], m0_in[h, sq_:sq_ + 1].partition_broadcast(128), writes=[Mc.r])
                    dma("sq", mlb16.t[:, :], ROWS_s[2, h, t0_ + 15:t0_ + 16].partition_broadcast(128), writes=[Mc.r])
                    op("dve", lambda: nc.vector.tensor_scalar(out=nM.t[0:16, :], in0=Mc.t[0:16, :], scalar1=-1.0, scalar2=None, op0=ALU.mult),
                       colres, colres)
                    op("dve", lambda: nc.vector.tensor_tensor(out=ebc.t[0:16, :], in0=bc_.t[0:16, :], in1=Mc.t[0:16, :], op=ALU.add), colres, colres)
                    op("act", lambda: nc.scalar.activation(out=ebc.t[0:16, :], in_=ebc.t[0:16, :], func=AF.Exp, scale=-1.0), colres, colres)
                    op("dve", lambda: nc.vector.tensor_tensor(out=wint.t[0:16, :], in0=m0b.t[0:16, :], in1=Mc.t[0:16, :], op=ALU.subtract),
                       colres, colres)
                    op("act", lambda: nc.scalar.activation(out=wint.t[0:16, :], in_=wint.t[0:16, :], func=AF.Exp), colres, colres)
                    c0f = C0f.next()
                    dma("sq", c0f.t[:], C0_in[sh].rearrange("(c p) f -> p c f", p=128), writes=[c0f.r])
                    copy("dve", C0b.t[:], c0f.t[:], [c0f.r], [C0b.r])
                    for c_ in range(2):
                        dma("sq", n0f.t[:, c_:c_ + 1], n0_in[sh, c_ * 128:(c_ + 1) * 128].rearrange("(p o) -> p o", o=1), writes=[n0f.r])
                    copy("dve", n0b.t[:], n0f.t[:], [n0f.r], [n0b.r])
                    dma("sq", n0row.t[:], n0_in[sh:sh + 1, :], writes=[n0row.r])
                    ps = psA.next()
                    for c in range(2):
                        mm(ps.t[0:16, 0:512], QT2.t[:, c, 0:16], C0b.t[:, c, :], c == 0, c == 1, [QT2.r, C0b.r], [ps.r])
                    it = inter.next()
                    op("act", lambda: nc.scalar.activation(out=it.t[:, :], in_=ps.t[0:16, 0:512], func=AF.Copy, scale=wint.t[0:16, :]),
                       [ps.r] + colres, [it.r])
                    ps2 = psA.next()
                    for c in range(2):
                        mm(ps2.t[0:16, 0:1], QT2.t[:, c, 0:16], n0b.t[:, c:c + 1], c == 0, c == 1, [QT2.r, n0b.r], [ps2.r])
                    qn0w = Buf(None); qn0w.t = bc_.t
                    op("dve", lambda: nc.vector.tensor_tensor(out=bc_.t[0:16, :], in0=ps2.t[0:16, 0:1], in1=wint.t[0:16, :], op=ALU.mult),
                       [ps2.r] + colres, colres)
                    qn0w.r = Mc.r
                    st_ = ml_A(16, QT2.t[:, :, 0:16], [QT2.r], [(0, 16, tril16)], nM.t[0:16, :], colres)
                    ml_B(st_, ebc.t[0:16, :], colres, lambda k0, w: V_ml_s[t0_:t0_ + 16, h * 512:(h + 1) * 512], h, (it, qn0w),
                         OG[OWN + t0_:OWN + t0_ + 16, h * 512:(h + 1) * 512], A_s[OWN + t0_:OWN + t0_ + 16, h * 512:(h + 1) * 512])
                    kb, vb, kg = kb16.next(), vb16.next(), kg16.next()
                    dma("sq", kb.t[:], K_tm_s[t0_:t0_ + 16, h * 256:(h + 1) * 256], writes=[kb.r])
                    dma("sq", vb.t[:], V_ml_s[t0_:t0_ + 16, h * 512:(h + 1) * 512], writes=[vb.r])
                    op("dve", lambda: nc.vector.tensor_tensor(out=cc_.t[0:16, :], in0=cc_.t[0:16, :], in1=mlb16.t[0:16, :], op=ALU.subtract),
                       colres, colres)
                    op("act", lambda: nc.scalar.activation(out=cc_.t[0:16, :], in_=cc_.t[0:16, :], func=AF.Exp), colres, colres)
                    op("dve", lambda: nc.vector.tensor_tensor(out=m0b.t[:, :], in0=m0b.t[:, :], in1=mlb16.t[:, :], op=ALU.subtract), colres, colres)
                    op("act", lambda: nc.scalar.activation(out=m0b.t[:, :], in_=m0b.t[:, :], func=AF.Exp), colres, colres)
                    copy("dve", wtb.t[:], cc_.t[0:16, :], colres, [wtb.r])
                    op("dve", lambda: nc.vector.tensor_scalar(out=kg.t[:], in0=kb.t[:], scalar1=cc_.t[0:16, :], scalar2=None, op0=ALU.mult),
                       [kb.r] + colres, [kg.r])
                    for dc in range(2):
                        ps = psA.next()
                        mm(ps.t[:, 0:512], kg.t[:, dc * 128:(dc + 1) * 128], vb.t[:], True, True, [kg.r, vb.r], [ps.r])
                        cs_ = cstr.next()
                        op("dve", lambda: nc.vector.scalar_tensor_tensor(out=cs_.t[:], in0=c0f.t[:, dc, :], scalar=m0b.t[:, :], in1=ps.t[:, 0:512],
                                                                         op0=ALU.mult, op1=ALU.add), [c0f.r, ps.r] + colres, [cs_.r])
                        dma("sq", s_C[sh, dc * 128:(dc + 1) * 128, :], cs_.t[:], reads=[cs_.r])
                    ps = psA.next()
                    mm(ps.t[0:1, 0:256], wtb.t[:, :], kb.t[:], True, True, [wtb.r, kb.r], [ps.r])
                    op("dve", lambda: nc.vector.scalar_tensor_tensor(out=nst.t[:], in0=n0row.t[:], scalar=m0b.t[0:1, :], in1=ps.t[0:1, 0:256],
                                                                     op0=ALU.mult, op1=ALU.add), [n0row.r, ps.r] + colres, [nst.r])
                    dma("sq", s_n[sh:sh + 1, :], nst.t[:], reads=[nst.r])
        if stop_after <= 6:
            K.finish()
            return nc

        tiles_own = [(ti * 512, 512, [(0, 512, 0)]) for ti in range(NTO)] + [(OWN, 64, sample_groups)]

        with K.scope():
            aT = K.sb([128, 32, 512], BF16)
            yT = K.sb([128, 32, 512], F32)
            rowsb = Ring([K.sb([128, 4096], BF16) for _ in range(2)])
            rows_ring = Ring([K.sb([128, 4096], F32) for _ in range(2)])
            sq_ring = Ring([K.sb([128, 512], BF16) for _ in range(3)])
            rbc = K.sb([128, 512], F32)
            WL = WLoader(3)
            for (c0, T, groups) in tiles_own:
                prep_plain(A_s[c0:c0 + T, :], T, aT, rowsb)
                for cb in range(16):
                    wb, wv = WL.load(f"wo{cb}")
                    for cc in range(2):
                        ch = cb * 2 + cc
                        ps = psA.next()
                        for k in range(32):
                            mm(ps.t[:, 0:T], wv[:, k, cc * 128:(cc + 1) * 128], aT.t[:, k, 0:T], k == 0, k == 31, [wb.r, aT.r], [ps.r])
                        copy(K.alt(), yT.t[:, ch, 0:T], ps.t[:, 0:T], [ps.r], [yT.r])
                src = x_own[c0:c0 + T, :] if c0 < OWN else x_smp
                post(yT, T, G1, groups, src, X1[c0:c0 + T, :], rows_ring, sq_ring, rbc)
        if stop_after <= 7:
            K.finish()
            return nc

        with K.scope():
            uT = K.sb([128, 32, 512], BF16)
            facc = K.sb([128, 32, 512], F32)
            rows_ring = Ring([K.sb([128, 4096], F32) for _ in range(1)])
            sq_ring = Ring([K.sb([128, 512], BF16) for _ in range(2)])
            rbc = K.sb([128, 512], F32)
            ss = K.sb([128, 1], F32); rstd = K.sb([128, 1], F32)
            r32 = Ring([K.sb([128, 512], F32) for _ in range(2)])
            hT = K.sb([128, 4, 512], BF16)
            WL = WLoader(4)
            jk = View(facc.t[:, 0:4, :].rearrange("p a b -> p (a b)").bitcast(BF16), facc.r)
            for (c0, T, groups) in tiles_own:
                prep(X1[c0:c0 + T, :], T, A2, B2, groups, uT, rows_ring, jk, ss, rstd)
                for hp in range(NHB // 2):
                    w1p = [WL.load(f"f1_{2 * hp + j}") for j in range(2)]
                    w2p = [WL.load(f"f2_{2 * hp + j}") for j in range(2)]
                    for j in range(2):
                        w1b, w1 = w1p[j]
                        for cc in range(2):
                            ps = psA.next()
                            for k in range(32):
                                mm(ps.t[:, 0:T], w1[:, k, cc * 128:(cc + 1) * 128], uT.t[:, k, 0:T], k == 0, k == 31, [w1b.r, uT.r], [ps.r])
                            r_ = r32.next()
                            op("act", lambda: nc.scalar.activation(out=r_.t[:, 0:T], in_=ps.t[:, 0:T], func=AF.Relu), [ps.r], [r_.r])
                            op("dve", lambda: nc.vector.tensor_tensor(out=hT.t[:, 2 * j + cc, 0:T], in0=r_.t[:, 0:T], in1=r_.t[:, 0:T], op=ALU.mult),
                               [r_.r], [hT.r])
                    for oc in range(32):
                        ps = psA.next()
                        for kk in range(4):
                            j, k = divmod(kk, 2)
                            w2b, w2 = w2p[j]
                            mm(ps.t[:, 0:T], w2[:, k, oc * 128:(oc + 1) * 128], hT.t[:, kk, 0:T], kk == 0, kk == 3, [w2b.r, hT.r], [ps.r])
                        f_ = facc.t[:, oc, 0:T]
                        if hp == 0:
                            copy(K.alt(), f_, ps.t[:, 0:T], [ps.r], [facc.r])
                        else:
                            op("dve", lambda: nc.vector.tensor_tensor(out=f_, in0=ps.t[:, 0:T], in1=f_, op=ALU.add), [ps.r, facc.r], [facc.r])
                dst = y_own[c0:c0 + T, :] if c0 < OWN else y_smp
                post(facc, T, G2, groups, X1[c0:c0 + T, :], dst, rows_ring, sq_ring, rbc)
        K.finish()
    return nc


def _fm(v, n):
    return np.ascontiguousarray(np.asarray(v, np.float32).reshape(n, 128).T)


def host_inputs(inp, SEQ, PAST):
    f = lambda a: np.ascontiguousarray(np.asarray(a, np.float32))
    NBLK, OWN = SEQ // 128, SEQ // 4
    NOB, NQ = OWN // 128, OWN // 4 * 0 + OWN + 64
    inv = (10000.0 ** (-np.arange(32, dtype=np.float32) / 32)).astype(np.float32)

    def cs_table(pos):
        ang = pos.astype(np.float32)[:, None] * inv[None, :]
        return np.concatenate([np.cos(ang), np.sin(ang)], axis=1).astype(np.float32)

    shared = dict(
        w_ada=f(inp["w_ada"][0]), b_ada_fm=_fm(inp["b_ada"][0], 192),
        g_pre1_fm=_fm(inp["g_pre1"][0], 32), g_post1_fm=_fm(inp["g_post1"][0], 32),
        g_pre2_fm=_fm(inp["g_pre2"][0], 32), g_post2_fm=_fm(inp["g_post2"][0], 32),
        w_in=f(inp["w_in"][0]), b_ig=f(inp["b_ig"][0]).reshape(4, 1), b_fg=f(inp["b_fg"][0]).reshape(4, 1),
        g_mlnorm=f(inp["g_mlnorm"][0]).reshape(1, 2048), g_cq_fm=_fm(inp["g_cq"][0], 8),
        w_uq=f(inp["w_uq"][0]).reshape(1024, 3072), g_ckv=f(inp["g_ckv"][0]).reshape(1, 512),
        w_uk=f(inp["w_uk"][0]).reshape(512, 2048), w_uv=f(inp["w_uv"][0]).reshape(512, 2048),
        w_out=f(inp["w_out"][0]), w_ff1=f(inp["w_ff1"][0]), w_ff2=f(inp["w_ff2"][0]),
        ident=np.eye(128, dtype=np.float32), cs_all=cs_table(np.arange(SEQ)),
        tril16=np.tril(np.ones((16, 16), np.float32)),
    )
    xp, xs = f(inp["x_prompt"]), f(inp["x_sample"])
    maps = []
    kk = np.arange(512)[None, :]
    for c in range(8):
        b, j = c // 4, c % 4
        qq = (j * 128 + np.arange(128))[:, None]
        c5 = np.concatenate([f(inp["c_prompt"])[b:b + 1], f(inp["c_sample"])[4 * c:4 * c + 4]], axis=0)
        own_pos = (np.arange(NOB)[:, None] * 4 + j) * 128 + np.arange(128)[None, :]
        sel = np.zeros((NBLK, NOB), np.float32)
        sel[np.arange(NOB) * 4 + j, np.arange(NOB)] = 1.0
        m = dict(shared)
        m.update(
            x_all=xp[b], x_own=np.ascontiguousarray(xp[b].reshape(NBLK, 128, 4096)[j::4].reshape(OWN, 4096)),
            x_smp=np.ascontiguousarray(xs[4 * c:4 * c + 4].reshape(64, 4096)),
            c_T=np.ascontiguousarray(c5.T.reshape(32, 128, 5).transpose(1, 0, 2)),
            ckv_past=f(inp["cache_mla_ckv"][0, 4 * c:4 * c + 4]), kr_past=f(inp["cache_mla_krope"][0, 4 * c:4 * c + 4]),
            C0=f(inp["state_mlstm_C"][0, 4 * c:4 * c + 4]).reshape(16, 256, 512),
            n0=f(inp["state_mlstm_n"][0, 4 * c:4 * c + 4]).reshape(16, 256),
            m0T=np.ascontiguousarray(f(inp["state_mlstm_m"][0, 4 * c:4 * c + 4]).reshape(4, 4).T),
            cs_own=np.concatenate([cs_table(own_pos.reshape(-1)), cs_table(PAST + np.tile(np.arange(16), 4))], axis=0),
            bias_mla=np.where(kk // 64 <= qq // 64, 0.0, -1e9).astype(np.float32),
            mask_ml=(kk <= qq).astype(np.float32),
            sel=sel,
        )
        maps.append(m)
    return maps


def assemble(results, SEQ, PAST, BATCH=2):
    NBLK, OWN = SEQ // 128, SEQ // 4
    NOB = OWN // 128
    y_p = np.zeros((BATCH, SEQ, 4096), np.float32)
    y_s = np.zeros((32, 16, 4096), np.float32)
    p_ckv = np.zeros((1, BATCH, SEQ, 512), np.float32); p_kr = np.zeros((1, BATCH, SEQ, 64), np.float32)
    p_C = np.zeros((1, BATCH, 4, 256, 512), np.float32); p_n = np.zeros((1, BATCH, 4, 256), np.float32)
    p_m = np.zeros((1, BATCH, 4), np.float32)
    s_ckv = np.zeros((1, 32, 16, 512), np.float32); s_kr = np.zeros((1, 32, 16, 64), np.float32)
    s_C = np.zeros((1, 32, 4, 256, 512), np.float32); s_n = np.zeros((1, 32, 4, 256), np.float32)
    s_m = np.zeros((1, 32, 4), np.float32)
    for c, r in enumerate(results):
        b, j = c // 4, c % 4
        y_p[b].reshape(NBLK, 128, 4096)[j::4] = np.asarray(r["y_own"], np.float32).reshape(NOB, 128, 4096)
        y_s[4 * c:4 * c + 4] = np.asarray(r["y_smp"], np.float32).reshape(4, 16, 4096)
        if j == 0:
            p_ckv[0, b] = r["ckv_o"]; p_kr[0, b] = r["kr_o"]; p_C[0, b] = r["C_o"]; p_n[0, b] = r["n_o"]
            p_m[0, b] = np.asarray(r["m_o"]).reshape(4)
        s_ckv[0, 4 * c:4 * c + 4] = np.asarray(r["s_ckv"]).reshape(4, 16, 512)
        s_kr[0, 4 * c:4 * c + 4] = np.asarray(r["s_kr"]).reshape(4, 16, 64)
        s_C[0, 4 * c:4 * c + 4] = np.asarray(r["s_C"]).reshape(4, 4, 256, 512)
        s_n[0, 4 * c:4 * c + 4] = np.asarray(r["s_n"]).reshape(4, 4, 256)
        s_m[0, 4 * c:4 * c + 4] = np.asarray(r["s_m"]).reshape(4, 4).T
    return (y_p, y_s, p_ckv, p_kr, p_C, p_n, p_m, s_ckv, s_kr, s_C, s_n, s_m)


def kernel(**inputs):
    inp = {k: np.asarray(v) for k, v in inputs.items()}
    SEQ = inp["x_prompt"].shape[1]
    PAST = inp["cache_mla_ckv"].shape[2]
    nc = build_program(SEQ=SEQ, PAST=PAST)
    maps = host_inputs(inp, SEQ, PAST)
    res = run_bass_kernel_spmd(nc, maps, core_ids=list(range(8)))
    return assemble(res.results, SEQ, PAST, BATCH=inp["x_prompt"].shape[0])
```
